# Optimizing a Trainium2 kernel written in Bass

```python
import jax, jax.numpy as jnp
from jax import lax
import numpy as np

D_MODEL = 1024
BATCH = 16
SEQ = 256
DEPTH = 4
DEC_BATCH = 8
DEC_SEQ = 1024
PAST_LEN = 256

GRID_W = 64
N_EVEN = (DEPTH + 1) // 2
N_ODD = DEPTH // 2
EPS = 1e-6
ROPE_BASE = 10000.0
NEG_INF = -1e30
Q_BLOCK = 128
A_HEADS = 16
A_KV_HEADS = 4
A_HEAD_DIM = 64
A_WIDTH = A_HEADS * A_HEAD_DIM
A_KV_WIDTH = A_KV_HEADS * A_HEAD_DIM
WINDOW = 128
BLOCK = 128
B_HEADS = 4
B_HEAD_DIM = 256
B_WIDTH = B_HEADS * B_HEAD_DIM
B_CHUNK = 64
B_CONV = 3
C_HEADS = 16
C_NOPE = 64
C_ROPE = 32
C_VDIM = 64
C_Q_RANK = 384
C_KV_RANK = 256
C_WIDTH = C_HEADS * C_VDIM
AB_SIZES = [A_WIDTH, A_KV_WIDTH, A_KV_WIDTH, A_WIDTH, B_WIDTH, B_WIDTH, B_WIDTH, B_WIDTH, B_WIDTH, 4 * B_HEADS]
AB_IN = sum(AB_SIZES)
AB_SPLITS = [sum(AB_SIZES[:i + 1]) for i in range(len(AB_SIZES) - 1)]
C_SIZES = [C_Q_RANK, C_KV_RANK, C_ROPE, C_WIDTH]
C_IN = sum(C_SIZES)
C_SPLITS = [sum(C_SIZES[:i + 1]) for i in range(len(C_SIZES) - 1)]

kernel_name = "hybrid_dit_swa_mlstm_mla_step"

F32 = jnp.float32


def _rmsnorm(x, g):
    xf = x.astype(F32)
    y = xf * lax.rsqrt(jnp.mean(xf * xf, axis=-1, keepdims=True) + EPS)
    return (y * g.astype(F32)).astype(x.dtype)


def _adaln(cond, w, b):
    m = jax.nn.silu(cond) @ w + b
    shift, scale, gate = jnp.split(m[:, None, :], 3, axis=-1)
    return shift, scale, gate


def _grid_positions(n_tokens):
    n_rows = n_tokens // GRID_W
    rows = jnp.repeat(jnp.arange(n_rows, dtype=jnp.int32), GRID_W)
    cols = jnp.tile(jnp.arange(GRID_W, dtype=jnp.int32), n_rows)
    return rows, cols


def _rotate(seg, pos):
    half = seg.shape[-1] // 2
    freqs = jnp.power(ROPE_BASE, -jnp.arange(half, dtype=F32) / half)
    ang = pos.astype(F32)[:, None] * freqs[None, :]
    cos = jnp.cos(ang)[None, :, None, :]
    sin = jnp.sin(ang)[None, :, None, :]
    s1, s2 = seg[..., :half], seg[..., half:]
    return jnp.concatenate([s1 * cos - s2 * sin, s2 * cos + s1 * sin], axis=-1)


def _axial_rope(x, rows, cols):
    d_axis = x.shape[-1] // 2
    xf = x.astype(F32)
    out = jnp.concatenate([_rotate(xf[..., :d_axis], rows), _rotate(xf[..., d_axis:], cols)], axis=-1)
    return out.astype(x.dtype)


def _blocked_attn(q, k, v, sink, scale):
    B, Tq, KV, G, dq = q.shape
    nb = Tq // Q_BLOCK
    qb = jnp.moveaxis(q.reshape(B, nb, Q_BLOCK, KV, G, dq), 1, 0)

    def one(qi):
        s = jnp.einsum('bqhgd,bkhd->bhgqk', qi, k, preferred_element_type=F32) * scale
        if sink is not None:
            sk = jnp.broadcast_to(sink.astype(F32)[None, :, :, None, None], s.shape[:-1] + (1,))
            s = jnp.concatenate([s, sk], axis=-1)
        p = jax.nn.softmax(s, axis=-1)
        if sink is not None:
            p = p[..., :-1]
        o = jnp.einsum('bhgqk,bkhd->bqhgd', p.astype(v.dtype), v, preferred_element_type=F32)
        return o.astype(v.dtype)

    out = lax.map(one, qb)
    return jnp.moveaxis(out, 0, 1).reshape(B, Tq, -1)


def _window_attn_latent(q, k, v, ck, cv, sink):
    B, T, H, d = q.shape
    KV = k.shape[2]
    G = H // KV
    L = ck.shape[1]
    nb = T // BLOCK
    scale = d ** -0.5
    pad = ((0, 0), (BLOCK, BLOCK), (0, 0), (0, 0))
    kp = jnp.pad(k, pad)
    vp = jnp.pad(v, pad)
    qb = jnp.moveaxis(q.reshape(B, nb, BLOCK, KV, G, d), 1, 0)
    sink_b = jnp.broadcast_to(sink.astype(F32).reshape(1, KV, G, 1, 1), (B, KV, G, BLOCK, 1))

    def one_block(args):
        qi, i = args
        start = i * BLOCK
        ki = lax.dynamic_slice_in_dim(kp, start, 3 * BLOCK, axis=1)
        vi = lax.dynamic_slice_in_dim(vp, start, 3 * BLOCK, axis=1)
        qpos = start + jnp.arange(BLOCK)
        kpos = start - BLOCK + jnp.arange(3 * BLOCK)
        valid = (jnp.abs(qpos[:, None] - kpos[None, :]) <= WINDOW) & (kpos >= 0)[None, :] & (kpos < T)[None, :]
        s_loc = jnp.einsum('bqhgd,bkhd->bhgqk', qi, ki, preferred_element_type=F32) * scale
        s_loc = jnp.where(valid, s_loc, NEG_INF)
        s_ctx = jnp.einsum('bqhgd,bkhd->bhgqk', qi, ck, preferred_element_type=F32) * scale
        p = jax.nn.softmax(jnp.concatenate([s_loc, s_ctx, sink_b], axis=-1), axis=-1)
        p_loc = p[..., :3 * BLOCK].astype(v.dtype)
        p_ctx = p[..., 3 * BLOCK:3 * BLOCK + L].astype(v.dtype)
        o = (jnp.einsum('bhgqk,bkhd->bqhgd', p_loc, vi, preferred_element_type=F32)
             + jnp.einsum('bhgqk,bkhd->bqhgd', p_ctx, cv, preferred_element_type=F32))
        return o.astype(v.dtype).reshape(B, BLOCK, H * d)

    out = lax.map(one_block, (qb, jnp.arange(nb)))
    return jnp.moveaxis(out, 0, 1).reshape(B, T, H * d)


def _dwconv_centred(x, w):
    K = w.shape[0]
    return lax.conv_general_dilated(x, w[:, None, :].astype(x.dtype), window_strides=(1,),
                                    padding=[(K // 2, K // 2)], dimension_numbers=('NWC', 'WIO', 'NWC'),
                                    feature_group_count=x.shape[-1])


def _mlstm_chunkwise(q, k, v, li, lf, C0, n0, m0):
    B, T, NH, DK = q.shape
    DV = v.shape[-1]
    nc = T // B_CHUNK

    def to_chunks(a):
        return a.astype(F32).reshape(B, nc, B_CHUNK, NH, -1).transpose(1, 0, 3, 2, 4)

    def gate_chunks(a):
        return a.astype(F32).reshape(B, nc, B_CHUNK, NH).transpose(1, 0, 3, 2)

    causal = jnp.tril(jnp.ones((B_CHUNK, B_CHUNK), dtype=bool))

    def step(carry, xs):
        C, n, m = carry
        qc, kc, vc, lic, lfc = xs
        b = jnp.cumsum(lfc, axis=-1)
        log_d = jnp.where(causal, b[..., :, None] - b[..., None, :] + lic[..., None, :], NEG_INF)
        log_init = b + m[..., None]
        m_t = jnp.maximum(log_init, jnp.max(log_d, axis=-1))
        d_mat = jnp.exp(log_d - m_t[..., None])
        w_init = jnp.exp(log_init - m_t)
        s = jnp.einsum('bhtd,bhsd->bhts', qc, kc) * d_mat
        num = w_init[..., None] * jnp.einsum('bhtd,bhde->bhte', qc, C) + jnp.einsum('bhts,bhse->bhte', s, vc)
        den = w_init * jnp.einsum('bhtd,bhd->bht', qc, n) + jnp.sum(s, axis=-1)
        h = num / jnp.maximum(jnp.abs(den), jnp.exp(-m_t))[..., None]
        b_last = b[..., -1]
        log_w = b_last[..., None] - b + lic
        m_new = jnp.maximum(b_last + m, jnp.max(log_w, axis=-1))
        w_s = jnp.exp(log_w - m_new[..., None])
        w_0 = jnp.exp(b_last + m - m_new)
        kw = kc * w_s[..., None]
        C_new = w_0[..., None, None] * C + jnp.einsum('bhsd,bhse->bhde', kw, vc)
        n_new = w_0[..., None] * n + jnp.sum(kw, axis=-2)
        return (C_new, n_new, m_new), h

    init = (C0.astype(F32), n0.astype(F32), m0.astype(F32))
    xs = (to_chunks(q), to_chunks(k), to_chunks(v), gate_chunks(li), gate_chunks(lf))
    (C, n, m), h = lax.scan(step, init, xs)
    h = h.transpose(1, 0, 3, 2, 4).reshape(B, T, NH, DV)
    return h, (C, n, m)


def _mlstm_bidir(q, k, v, o, g, conv_w, norm_w, init):
    B, T, _ = q.shape
    qk = jax.nn.silu(_dwconv_centred(jnp.concatenate([q, k], axis=-1), conv_w))
    q, k = jnp.split(qk, 2, axis=-1)
    q = q.reshape(B, T, B_HEADS, B_HEAD_DIM)
    k = k.reshape(B, T, B_HEADS, B_HEAD_DIM) * (B_HEAD_DIM ** -0.5)
    v = v.reshape(B, T, B_HEADS, B_HEAD_DIM)
    i_f, f_f, i_b, f_b = jnp.split(g.astype(F32), 4, axis=-1)
    if init is None:
        C0 = jnp.zeros((B, 2, B_HEADS, B_HEAD_DIM, B_HEAD_DIM), F32)
        n0 = jnp.zeros((B, 2, B_HEADS, B_HEAD_DIM), F32)
        m0 = jnp.zeros((B, 2, B_HEADS), F32)
    else:
        C0, n0, m0 = init
    h_f, (Cf, nf, mf) = _mlstm_chunkwise(q, k, v, i_f, jax.nn.log_sigmoid(f_f), C0[:, 0], n0[:, 0], m0[:, 0])
    rev = lambda a: jnp.flip(a, axis=1)
    h_b, (Cb, nbk, mb) = _mlstm_chunkwise(rev(q), rev(k), rev(v), rev(i_b), rev(jax.nn.log_sigmoid(f_b)),
                                          C0[:, 1], n0[:, 1], m0[:, 1])
    og = jax.nn.sigmoid(o.astype(F32)).reshape(B, T, B_HEADS, B_HEAD_DIM)
    h = og * (h_f + rev(h_b))
    h = h * lax.rsqrt(jnp.mean(h * h, axis=-1, keepdims=True) + EPS)
    h = (h.reshape(B, T, B_WIDTH) * norm_w.astype(F32)).astype(v.dtype)
    states = (jnp.stack([Cf, Cb], axis=1), jnp.stack([nf, nbk], axis=1), jnp.stack([mf, mb], axis=1))
    return h, states


def _mixer_ab(h, w_in, sink, conv_w, gate_bias, norm_w, w_out, ctx, pos):
    B, T, _ = h.shape
    G = A_HEADS // A_KV_HEADS
    qa, ka, va, za, qb, kb, vb, ob, zb, gb = jnp.split(h @ w_in, AB_SPLITS, axis=-1)
    qa = qa.reshape(B, T, A_HEADS, A_HEAD_DIM)
    ka = ka.reshape(B, T, A_KV_HEADS, A_HEAD_DIM)
    va = va.reshape(B, T, A_KV_HEADS, A_HEAD_DIM)
    if ctx is None:
        attn = _blocked_attn(qa.reshape(B, T, A_KV_HEADS, G, A_HEAD_DIM), ka, va,
                             sink.reshape(A_KV_HEADS, G), A_HEAD_DIM ** -0.5)
        init = None
    else:
        ck, cv, C0, n0, m0 = ctx
        rows, cols = pos
        attn = _window_attn_latent(_axial_rope(qa, rows, cols), _axial_rope(ka, rows, cols), va, ck, cv, sink)
        init = (C0, n0, m0)
    a_out = attn * jax.nn.silu(za)
    hb, states = _mlstm_bidir(qb, kb, vb, ob, gb + gate_bias, conv_w, norm_w, init)
    b_out = hb * jax.nn.silu(zb)
    out = jnp.concatenate([a_out, b_out], axis=-1) @ w_out
    return out, (ka, va) + states


def _mixer_c(h, w_in, q_norm, w_qb, kv_norm, w_kvb, w_out, ctx, pos):
    B, T, _ = h.shape
    qa, kva, kr, z = jnp.split(h @ w_in, C_SPLITS, axis=-1)
    q = (_rmsnorm(qa, q_norm) @ w_qb).reshape(B, T, C_HEADS, C_NOPE + C_ROPE)
    q_nope, q_rope = q[..., :C_NOPE], q[..., C_NOPE:]
    c_kv = _rmsnorm(kva, kv_norm)
    kr = kr[:, :, None, :]
    if ctx is None:
        keys_ckv, keys_kr = c_kv, kr
    else:
        rows, cols = pos
        cc, ckr = ctx
        q_rope = _axial_rope(q_rope, rows, cols)
        keys_ckv = jnp.concatenate([c_kv, cc], axis=1)
        keys_kr = jnp.concatenate([_axial_rope(kr, rows, cols), ckr[:, :, None, :]], axis=1)
    Tk = keys_ckv.shape[1]
    kv = (keys_ckv @ w_kvb).reshape(B, Tk, C_HEADS, C_NOPE + C_VDIM)
    k_nope, v = kv[..., :C_NOPE], kv[..., C_NOPE:]
    k = jnp.concatenate([k_nope, jnp.broadcast_to(keys_kr, (B, Tk, C_HEADS, C_ROPE))], axis=-1)
    qf = jnp.concatenate([q_nope, q_rope], axis=-1)[:, :, :, None, :]
    attn = _blocked_attn(qf, k, v, None, (C_NOPE + C_ROPE) ** -0.5)
    out = (attn * jax.nn.silu(z)) @ w_out
    return out, (c_kv, kr[:, :, 0, :])


def setup_inputs(seed: int = 0) -> dict:
    key = jax.random.key(seed)
    ks = iter(jax.random.split(key, 40))

    def nrm(shape, scale):
        return jax.random.normal(next(ks), shape, F32) * scale

    d_inv = D_MODEL ** -0.5
    gate_offset = jnp.repeat(jnp.array([0.0, 3.0, 0.0, 3.0], F32), B_HEADS)
    return {
        'x_prompt': nrm((BATCH, SEQ, D_MODEL), 1.0),
        'x_sample': nrm((DEC_BATCH, DEC_SEQ, D_MODEL), 1.0),
        'cache_a_k': nrm((DEC_BATCH, N_EVEN, PAST_LEN, A_KV_HEADS, A_HEAD_DIM), 1.0),
        'cache_a_v': nrm((DEC_BATCH, N_EVEN, PAST_LEN, A_KV_HEADS, A_HEAD_DIM), 1.0),
        'state_b_mem': nrm((DEC_BATCH, N_EVEN, 2, B_HEADS, B_HEAD_DIM, B_HEAD_DIM), 0.05),
        'state_b_norm': nrm((DEC_BATCH, N_EVEN, 2, B_HEADS, B_HEAD_DIM), 0.05),
        'state_b_max': nrm((DEC_BATCH, N_EVEN, 2, B_HEADS), 1.0),
        'cache_c_kv': nrm((DEC_BATCH, N_ODD, PAST_LEN, C_KV_RANK), 1.0),
        'cache_c_krope': nrm((DEC_BATCH, N_ODD, PAST_LEN, C_ROPE), 1.0),
        'c': nrm((DEC_BATCH, D_MODEL), 1.0),
        'c_ctx': nrm((D_MODEL,), 1.0),
        'norm_g': 1.0 + nrm((DEPTH, D_MODEL), 0.02),
        'w_mod': nrm((DEPTH, D_MODEL, 3 * D_MODEL), 0.5 * d_inv),
        'b_mod': nrm((DEPTH, 3 * D_MODEL), 0.02),
        'w_in_ab': nrm((N_EVEN, D_MODEL, AB_IN), d_inv),
        'sink_a': nrm((N_EVEN, A_HEADS), 0.5),
        'conv_b': nrm((N_EVEN, B_CONV, 2 * B_WIDTH), B_CONV ** -0.5),
        'gate_bias_b': gate_offset[None, :] + nrm((N_EVEN, 4 * B_HEADS), 0.1),
        'norm_b': 1.0 + nrm((N_EVEN, B_WIDTH), 0.02),
        'w_out_ab': nrm((N_EVEN, A_WIDTH + B_WIDTH, D_MODEL), (A_WIDTH + B_WIDTH) ** -0.5),
        'w_in_c': nrm((N_ODD, D_MODEL, C_IN), d_inv),
        'q_norm_c': 1.0 + nrm((N_ODD, C_Q_RANK), 0.02),
        'w_qb_c': nrm((N_ODD, C_Q_RANK, C_HEADS * (C_NOPE + C_ROPE)), C_Q_RANK ** -0.5),
        'kv_norm_c': 1.0 + nrm((N_ODD, C_KV_RANK), 0.02),
        'w_kvb_c': nrm((N_ODD, C_KV_RANK, C_HEADS * (C_NOPE + C_VDIM)), C_KV_RANK ** -0.5),
        'w_out_c': nrm((N_ODD, C_WIDTH, D_MODEL), C_WIDTH ** -0.5),
        'final_norm': 1.0 + nrm((D_MODEL,), 0.02),
    }


def reference(x_prompt, x_sample, cache_a_k, cache_a_v, state_b_mem, state_b_norm, state_b_max,
              cache_c_kv, cache_c_krope, c, c_ctx, norm_g, w_mod, b_mod, w_in_ab, sink_a, conv_b,
              gate_bias_b, norm_b, w_out_ab, w_in_c, q_norm_c, w_qb_c, kv_norm_c, w_kvb_c, w_out_c,
              final_norm):
    pos = _grid_positions(x_sample.shape[1])
    yp, ys = x_prompt, x_sample
    a_k, a_v, b_mem, b_nrm, b_max, c_kvs, c_krs = [], [], [], [], [], [], []
    for l in range(DEPTH):
        sh_p, sc_p, gt_p = _adaln(c_ctx[None, :], w_mod[l], b_mod[l])
        sh_s, sc_s, gt_s = _adaln(c, w_mod[l], b_mod[l])
        hp = _rmsnorm(yp, norm_g[l]) * (1.0 + sc_p) + sh_p
        hs = _rmsnorm(ys, norm_g[l]) * (1.0 + sc_s) + sh_s
        j = l // 2
        if l % 2 == 0:
            wts = (w_in_ab[j], sink_a[j], conv_b[j], gate_bias_b[j], norm_b[j], w_out_ab[j])
            out_p, (k_ctx, v_ctx, mem, nrm_, mx) = _mixer_ab(hp, *wts, None, None)
            ctx = (cache_a_k[:, j], cache_a_v[:, j], state_b_mem[:, j], state_b_norm[:, j], state_b_max[:, j])
            out_s = _mixer_ab(hs, *wts, ctx, pos)[0]
            a_k.append(k_ctx)
            a_v.append(v_ctx)
            b_mem.append(mem)
            b_nrm.append(nrm_)
            b_max.append(mx)
        else:
            wts = (w_in_c[j], q_norm_c[j], w_qb_c[j], kv_norm_c[j], w_kvb_c[j], w_out_c[j])
            out_p, (ckv, ckr) = _mixer_c(hp, *wts, None, None)
            out_s = _mixer_c(hs, *wts, (cache_c_kv[:, j], cache_c_krope[:, j]), pos)[0]
            c_kvs.append(ckv)
            c_krs.append(ckr)
        yp = yp + gt_p * out_p
        ys = ys + gt_s * out_s
    y_prompt = _rmsnorm(yp, final_norm)
    y_sample = _rmsnorm(ys, final_norm)
    return (y_prompt, y_sample, jnp.stack(a_k, axis=1), jnp.stack(a_v, axis=1), jnp.stack(b_mem, axis=1),
            jnp.stack(b_nrm, axis=1), jnp.stack(b_max, axis=1), jnp.stack(c_kvs, axis=1), jnp.stack(c_krs, axis=1))
```

```python
import contextlib
import numpy as np
import concourse.bass as bass
import concourse.mybir as mybir
from concourse.bass_utils import run_bass_kernel_spmd

F32 = mybir.dt.float32
F32R = mybir.dt.float32r
BF16 = mybir.dt.bfloat16
ALU = mybir.AluOpType
AF = mybir.ActivationFunctionType
AX = mybir.AxisListType

ENGS = ["sync", "scalar", "vector", "gpsimd", "tensor"]


class Op:
    __slots__ = ("eng", "fn", "deps", "dma", "semkey", "mark", "dmacount", "seq", "phase")

    def __init__(self, eng, fn, dma, semkey):
        self.eng = eng
        self.fn = fn
        self.deps = []
        self.dma = dma
        self.semkey = semkey
        self.mark = None
        self.dmacount = None
        self.seq = None


class Prog:
    def __init__(self, nc):
        self.nc = nc
        self.streams = {e: [] for e in ENGS}
        self.last_writer = {}
        self.readers = {}
        self.all_ops = []
        self.dma_counts = {}
        self.pending_barrier = {e: [] for e in ENGS}
        self.out_dmas = []
        self.phase = "init"

    def add(self, eng, fn, reads=(), writes=(), dma=False, semkey=None, is_out=False):
        if dma and semkey is None:
            semkey = writes[0] if writes else reads[0]
        o = Op(eng, fn, dma, semkey)
        o.phase = self.phase
        deps = {}

        def add_dep(p, raw):
            if p is None or p is o:
                return
            if (not raw) and (not p.dma) and (not dma) and p.eng == eng and eng == "tensor":
                return
            deps[id(p)] = p

        for k in reads:
            add_dep(self.last_writer.get(k), True)
            if isinstance(k, tuple) and k[0] == "ps":
                for r in self.readers.get(k, ()):
                    if r.eng != eng:
                        add_dep(r, True)
        for k in writes:
            add_dep(self.last_writer.get(k), False)
            for r in self.readers.get(k, ()):
                add_dep(r, False)
        for p in self.pending_barrier[eng]:
            add_dep(p, True)
        self.pending_barrier[eng] = []
        for k in writes:
            self.last_writer[k] = o
            self.readers[k] = []
        for k in reads:
            self.readers.setdefault(k, []).append(o)
        o.deps = list(deps.values())
        if dma:
            c = self.dma_counts.get(semkey, 0) + 1
            self.dma_counts[semkey] = c
            o.dmacount = c
            if is_out:
                self.out_dmas.append(o)
        o.seq = len(self.all_ops)
        self.all_ops.append(o)
        self.streams[eng].append(o)
        return o

    def barrier(self):
        lasts = []
        for e in ENGS:
            s = self.streams[e]
            if s:
                lasts.append(s[-1])
        dm = {}
        for o in self.all_ops:
            if o.dma:
                dm[o.semkey] = o
        lasts += list(dm.values())
        for e in ENGS:
            self.pending_barrier[e] = list(lasts)

    def emit(self):
        nc = self.nc
        self.barrier()
        self.add("sync", lambda e: e.nop(), reads=(), writes=())
        for o in self.all_ops:
            for p in o.deps:
                if not p.dma:
                    p.mark = True
        counts = {e: 0 for e in ENGS}
        for e in ENGS:
            for o in self.streams[e]:
                if o.mark:
                    counts[e] += 1
                    o.mark = counts[e]
        with contextlib.ExitStack() as st:
            esem = {e: st.enter_context(nc.semaphore("sem_" + e)) for e in ENGS}
            dsem = {}
            for i, k in enumerate(self.dma_counts):
                dsem[k] = st.enter_context(nc.semaphore("dsem%d" % i))
            block = st.enter_context(nc.Block())
            streams = self.streams

            def run(eng_name, e):
                known = {}
                for o in streams[eng_name]:
                    need = {}
                    for p in o.deps:
                        if p.dma:
                            key = ("d", p.semkey)
                            val = 16 * p.dmacount
                            sem = dsem[p.semkey]
                        else:
                            key = ("e", p.eng)
                            val = p.mark
                            sem = esem[p.eng]
                        if known.get(key, 0) >= val:
                            continue
                        if key not in need or need[key][1] < val:
                            need[key] = (sem, val)
                    for key, (sem, val) in need.items():
                        e.wait_ge(sem, val)
                        known[key] = val
                    ins = o.fn(e)
                    if o.dma:
                        ins.then_inc(dsem[o.semkey], 16)
                    elif o.mark:
                        ins.then_inc(esem[eng_name], 1)

            @block.sync
            def _(e):
                run("sync", e)

            @block.scalar
            def _(e):
                run("scalar", e)

            @block.vector
            def _(e):
                run("vector", e)

            @block.gpsimd
            def _(e):
                run("gpsimd", e)

            @block.tensor
            def _(e):
                run("tensor", e)


DM = 1024
DEPTH = 4
NEG = -30000.0
EPS = 1e-6
T_S = 1024
T_P = 512
NTOK = T_S + T_P
AB_OFF = {"qa": 0, "ka": 1024, "va": 1280, "za": 1536, "qb": 2560, "kb": 3584, "vb": 4608, "ob": 5632, "zb": 6656,
          "gb": 7680}
PASSES = [(0, 1024, [(0, 1024)], True), (1024, 512, [(0, 256), (256, 512)], False)]
ARENA_WORDS = 12800
ARENAR_WORDS = 7680


def _rope_perm(n_rot, d_axis):
    half = d_axis // 2
    idx = np.arange(n_rot)
    within = idx % d_axis
    return np.where(within < half, idx + half, idx - half)


def _rope_tables(n_rot, d_axis):
    half = d_axis // 2
    rows = np.repeat(np.arange(1024 // 64, dtype=np.int32), 64).astype(np.float32)
    cols = np.tile(np.arange(64, dtype=np.int32), 1024 // 64).astype(np.float32)
    freqs = np.power(np.float32(10000.0), -np.arange(half, dtype=np.float32) / np.float32(half)).astype(np.float32)
    cos = np.zeros((n_rot, 1024), np.float32)
    sin = np.zeros((n_rot, 1024), np.float32)
    for d in range(n_rot):
        axis = d // d_axis
        within = d % d_axis
        pos = rows if axis == 0 else cols
        ang = (pos * freqs[within % half]).astype(np.float32)
        cos[d] = np.cos(ang)
        s = np.sin(ang)
        sin[d] = -s if within < half else s
    return cos, sin


def _fm(v, nchunk):
    return np.ascontiguousarray(np.asarray(v, np.float32).reshape(nchunk, 128).T)


def prep_inputs(inp):
    f32 = np.float32
    sh = {}
    cA, sA = _rope_tables(64, 32)
    sh["cosA"] = np.concatenate([cA, cA], 0)
    sh["sinA"] = np.concatenate([sA, sA], 0)
    cC, sC = _rope_tables(32, 16)
    cosC = np.ones((128, 1024), f32)
    sinC = np.zeros((128, 1024), f32)
    cosC[64:96] = cC
    sinC[64:96] = sC
    sh["cosC"] = cosC
    sh["sinC"] = sinC
    kk = np.arange(128)[:, None]
    qq = np.arange(128)[None, :]
    mA = np.zeros((128, 384), f32)
    mA[:, 0:128] = np.where(kk <= qq, 0.0, NEG)
    mA[:, 256:384] = np.where(qq <= kk, 0.0, NEG)
    sh["maskA"] = mA
    sh["maskF"] = np.where(kk <= qq, 0.0, NEG).astype(f32)
    sh["maskB"] = np.where(kk >= qq, 0.0, NEG).astype(f32)
    sh["ident"] = np.eye(128, dtype=f32)
    pA = np.zeros((128, 128), f32)
    pidx = np.arange(128)
    pperm = np.where((pidx % 32) < 16, pidx + 16, pidx - 16)
    pA[pperm, pidx] = 1.0
    sh["permA"] = pA
    sel = np.zeros((128, 4, 128), f32)
    for base in (0, 32, 64):
        for h in range(4):
            sel[base + h, h, :] = 1.0
    sh["selrows"] = sel.reshape(128, 512)
    sh["fnormT"] = _fm(inp["final_norm"], 8)
    permA = _rope_perm(64, 32)
    permC = _rope_perm(32, 16)
    for l in range(DEPTH):
        sh["wmod%d" % l] = np.ascontiguousarray(inp["w_mod"][l])
        sh["bmodT%d" % l] = _fm(inp["b_mod"][l], 24)
        sh["normgT%d" % l] = _fm(inp["norm_g"][l], 8)
    for j in range(2):
        W = np.asarray(inp["w_in_ab"][j])
        WA = np.zeros((1024, 4, 704), f32)
        for g in range(4):
            qcols = AB_OFF["qa"] + g * 256 + np.arange(256)
            kcols = AB_OFF["ka"] + g * 64 + np.arange(64)
            WA[:, g, 0:256] = W[:, qcols]
            WA[:, g, 256:320] = W[:, kcols]
            WA[:, g, 320:384] = W[:, kcols]
            WA[:, g, 384:448] = W[:, AB_OFF["va"] + g * 64 + np.arange(64)]
            WA[:, g, 448:704] = W[:, AB_OFF["za"] + g * 256 + np.arange(256)]
        sh["WA%d" % j] = WA
        WB = np.zeros((1024, 4, 1280), f32)
        for h in range(4):
            for i, nm in enumerate(["qb", "kb", "ob", "zb", "vb"]):
                WB[:, h, i * 256:(i + 1) * 256] = W[:, AB_OFF[nm] + h * 256 + np.arange(256)]
        sh["WB%d" % j] = WB
        sh["WG%d" % j] = np.ascontiguousarray(W[:, AB_OFF["gb"]:AB_OFF["gb"] + 16])
        sh["woutab%d" % j] = np.ascontiguousarray(inp["w_out_ab"][j])
        sh["sinkb%d" % j] = np.ascontiguousarray(np.broadcast_to(np.asarray(inp["sink_a"][j], f32)[None, :], (128, 16)))
        cb = np.asarray(inp["conv_b"][j], f32)
        sh["convT%d" % j] = np.ascontiguousarray(cb.reshape(3, 16, 128).transpose(2, 1, 0))
        gbv = np.asarray(inp["gate_bias_b"][j], f32).reshape(4, 4).T
        gb = np.zeros((128, 4), f32)
        for base in (0, 32, 64):
            gb[base:base + 4] = gbv
        sh["gbias%d" % j] = gb
        sh["normbT%d" % j] = _fm(inp["norm_b"][j], 8)
    for j in range(2):
        W = np.asarray(inp["w_in_c"][j])
        sh["winc%d" % j] = np.ascontiguousarray(W)
        wkr = np.zeros((1024, 192), f32)
        wkr[:, 64:96] = W[:, 640:672]
        wkr[:, 160:192] = W[:, 640 + permC]
        sh["wkr%d" % j] = wkr
        Wq = np.asarray(inp["w_qb_c"][j]).reshape(384, 16, 96)
        wqb = np.zeros((384, 16, 192), f32)
        wqb[:, :, 0:96] = Wq
        wqb[:, :, 96:160] = Wq[:, :, 0:64]
        wqb[:, :, 160:192] = Wq[:, :, 64 + permC]
        sh["wqb%d" % j] = wqb
        sh["wkvb%d" % j] = np.ascontiguousarray(inp["w_kvb_c"][j])
        sh["woutc%d" % j] = np.ascontiguousarray(inp["w_out_c"][j])
        sh["qnormT%d" % j] = _fm(inp["q_norm_c"][j], 3)
        sh["kvnormT%d" % j] = _fm(inp["kv_norm_c"][j], 2)
    per = []
    for i in range(8):
        d = {}
        xs = np.asarray(inp["x_sample"][i])
        xp = np.asarray(inp["x_prompt"][2 * i:2 * i + 2]).reshape(512, 1024)
        d["xT"] = np.ascontiguousarray(np.concatenate([xs, xp], 0).T)
        cond = np.stack([np.asarray(inp["c"][i]), np.asarray(inp["c_ctx"])], -1)
        d["condT"] = np.ascontiguousarray(cond.reshape(8, 128, 2).transpose(1, 0, 2))
        for j in range(2):
            ck = np.asarray(inp["cache_a_k"][i, j])
            ckT = ck.transpose(1, 2, 0)
            d["KctxT%d" % j] = np.ascontiguousarray(np.concatenate([ckT, ckT], 1))
            d["Vctx%d" % j] = np.ascontiguousarray(np.asarray(inp["cache_a_v"][i, j]).reshape(256, 256))
            d["C0%d" % j] = np.ascontiguousarray(inp["state_b_mem"][i, j])
            d["n0%d" % j] = np.ascontiguousarray(inp["state_b_norm"][i, j])
            m0v = np.asarray(inp["state_b_max"][i, j]).T
            m0 = np.zeros((128, 2), f32)
            for base in (0, 32, 64):
                m0[base:base + 4] = m0v
            d["m0%d" % j] = m0
            d["cckvT%d" % j] = np.ascontiguousarray(np.asarray(inp["cache_c_kv"][i, j]).T)
            d["ckrT%d" % j] = np.ascontiguousarray(np.asarray(inp["cache_c_krope"][i, j]).T)
        per.append(d)
    return sh, per


IN_SPECS = None


def input_specs():
    s = {}
    s["xT"] = ([1024, NTOK], False)
    s["condT"] = ([128, 8, 2], False)
    for n in ["cosA", "sinA", "cosC", "sinC"]:
        s[n] = ([128, 1024], False)
    s["maskA"] = ([128, 384], False)
    s["maskF"] = ([128, 128], False)
    s["maskB"] = ([128, 128], False)
    s["ident"] = ([128, 128], False)
    s["permA"] = ([128, 128], False)
    s["selrows"] = ([128, 512], False)
    s["fnormT"] = ([128, 8], False)
    for l in range(DEPTH):
        s["wmod%d" % l] = ([1024, 3072], True)
        s["bmodT%d" % l] = ([128, 24], False)
        s["normgT%d" % l] = ([128, 8], False)
    for j in range(2):
        s["WA%d" % j] = ([1024, 4, 704], True)
        s["WB%d" % j] = ([1024, 4, 1280], True)
        s["WG%d" % j] = ([1024, 16], True)
        s["woutab%d" % j] = ([2048, 1024], True)
        s["sinkb%d" % j] = ([128, 16], False)
        s["convT%d" % j] = ([128, 16, 3], False)
        s["gbias%d" % j] = ([128, 4], False)
        s["normbT%d" % j] = ([128, 8], False)
        s["KctxT%d" % j] = ([4, 128, 256], False)
        s["Vctx%d" % j] = ([256, 256], False)
        s["C0%d" % j] = ([2, 4, 256, 256], False)
        s["n0%d" % j] = ([2, 4, 256], False)
        s["m0%d" % j] = ([128, 2], False)
        s["winc%d" % j] = ([1024, 1696], True)
        s["wkr%d" % j] = ([1024, 192], True)
        s["wqb%d" % j] = ([384, 16, 192], True)
        s["wkvb%d" % j] = ([256, 2048], True)
        s["woutc%d" % j] = ([1024, 1024], True)
        s["qnormT%d" % j] = ([128, 3], False)
        s["kvnormT%d" % j] = ([128, 2], False)
        s["cckvT%d" % j] = ([256, 256], True)
        s["ckrT%d" % j] = ([32, 256], False)
    return s


def output_specs():
    return {
        "yT_o": [1024, NTOK],
        "kaT_o": [2, 256, 512],
        "va_o": [2, 512, 256],
        "C_o": [2, 2, 2, 4, 256, 256],
        "n_o": [2, 2, 2, 4, 256],
        "m_o": [2, 2, 2, 4],
        "ckvT_o": [2, 256, 512],
        "krT_o": [2, 32, 512],
    }


def build(nlayers=DEPTH, do_final_norm=True):
    nc = bass.Bass("TRN2", target_bir_lowering=False)
    D = {}
    for name, (shape, r) in input_specs().items():
        D[name] = nc.dram_tensor(name, list(shape), F32R if r else F32, kind="ExternalInput").ap()
    for name, shape in output_specs().items():
        D[name] = nc.dram_tensor(name, list(shape), F32, kind="ExternalOutput").ap()
    for name, shape in DBG_SPECS.items():
        D[name] = nc.dram_tensor(name, list(shape), F32, kind="ExternalOutput").ap()
    P = Prog(nc)
    st = contextlib.ExitStack()
    with st:
        def sb(name, shape, dt=F32):
            return st.enter_context(nc.sbuf_tensor(name, list(shape), dt))

        psb = [st.enter_context(nc.psum_tensor("psb%d" % i, [128, 512], F32)) for i in range(8)]
        yT = sb("yT", [128, 8, NTOK])
        hT = sb("hT", [128, 8, 1024], F32R)
        wr = [sb("wr%d" % i, [128, 4096], F32R) for i in range(2)]
        tabC = sb("tabC", [128, 1024])
        tabS = sb("tabS", [128, 1024])
        arena = sb("arena", [128, ARENA_WORDS])
        arenaR = sb("arenaR", [128, ARENAR_WORDS], F32R)
        identF = sb("identF", [128, 128])
        identR = sb("identR", [128, 128], F32R)
        identB = sb("identB", [128, 128], BF16)
        permB = sb("permB", [128, 128], BF16)
        onesR = sb("onesR", [128, 128], F32R)
        ones4 = sb("ones4", [128, 128])
        selF = sb("selF", [128, 512])
        maskAf = arena[:, 0:384]
        maskAb = sb("maskAb", [128, 384], BF16)
        maskFf = arena[:, 384:640]
        maskFB = sb("maskFB", [128, 256], BF16)
        condt = sb("condt", [128, 8, 2])
        cond_t = sb("cond_t", [128, 8, 2])
        scond = sb("scond", [128, 8, 2], F32R)
        modv_b = [sb("modv%d" % i, [128, 24, 2]) for i in range(2)]
        avec_b = [sb("avec%d" % i, [128, 8, 2]) for i in range(2)]
        bmodt_b = [sb("bmodt%d" % i, [128, 24]) for i in range(2)]
        normgt_b = [sb("normgt%d" % i, [128, 8]) for i in range(2)]
        cur = {"l": 0}
        smallc = sb("smallc", [128, 96])
        sinkt = sb("sinkt", [128, 16])
        convt = sb("convt", [128, 16, 3])
        m0t = sb("m0t", [128, 2])
        epsb = sb("epsb", [128, 1])

        state = {"ps": 0, "uid": 0}

        def uid(prefix):
            state["uid"] += 1
            return (prefix, state["uid"])

        def ps_next(pool=(4, 5, 6, 7)):
            i = pool[state["ps"] % len(pool)]
            state["ps"] += 1
            return psb[i], ("ps", i)

        ALLPS = (0, 1, 2, 3, 4, 5, 6, 7)

        def mm(out, lhsT, rhs, start, stop, reads, writes):
            op_ = P.add("tensor", lambda e, o=out, l=lhsT, r=rhs, s=start, t=stop: e.matmul(o, l, r, start=s, stop=t),
                        reads, writes)
            op_.seq = 2 if lhsT.dtype == F32 else 1

        def act(out, in_, func, reads, writes, scale=1.0, bias=0.0):
            P.add("scalar", lambda e, o=out, i=in_, f=func, s=scale, b=bias: e.activation(out=o, in_=i, func=f, bias=b, scale=s),
                  reads, writes)

        def tt(eng, out, in0, in1, op, reads, writes):
            if eng == "gpsimd" and FLAGS.get("nopool"):
                eng = "vector"
            P.add(eng, lambda e, o=out, a=in0, b=in1, p=op: e.tensor_tensor(out=o, in0=a, in1=b, op=p), reads, writes)

        def ts(eng, out, in0, s1, op0, reads, writes, s2=None, op1=None):
            if eng == "gpsimd" and FLAGS.get("nopool"):
                eng = "vector"
            if op1 is None:
                P.add(eng, lambda e, o=out, a=in0, x=s1, p=op0: e.tensor_scalar(out=o, in0=a, scalar1=x, scalar2=None, op0=p),
                      reads, writes)
            else:
                P.add(eng, lambda e, o=out, a=in0, x=s1, p=op0, y=s2, q=op1: e.tensor_scalar(out=o, in0=a, scalar1=x, scalar2=y, op0=p, op1=q),
                      reads, writes)

        def stt(out, in0, scalar, in1, op0, op1, reads, writes):
            P.add("vector", lambda e, o=out, a=in0, s=scalar, b=in1, p=op0, q=op1: e.scalar_tensor_tensor(out=o, in0=a, scalar=s, in1=b, op0=p, op1=q),
                  reads, writes)

        def cp(eng, out, in_, reads, writes):
            if eng == "gpsimd" and FLAGS.get("nopool"):
                eng = "vector"
            if eng == "scalar":
                act(out, in_, AF.Copy, reads, writes)
            else:
                P.add(eng, lambda e, o=out, i=in_: e.tensor_copy(o, i), reads, writes)

        def memset(eng, ap, val, writes):
            if eng == "gpsimd" and FLAGS.get("nopool"):
                eng = "vector"
            P.add(eng, lambda e, a=ap, v=val: e.memset(a, v), (), writes)

        def dma(eng, out, in_, reads, writes, semkey=None, is_out=False):
            P.add(eng, lambda e, o=out, i=in_: e.dma_start(out=o, in_=i), reads, writes, dma=True, semkey=semkey, is_out=is_out)

        ar = {"off": 0, "offR": 0}

        def carveR(shape):
            n = 1
            for s_ in shape:
                n *= s_
            off = ar["offR"]
            ar["offR"] = off + n
            assert ar["offR"] <= ARENAR_WORDS, ("arenaR overflow", ar["offR"])
            ap = arenaR[:, off:off + n]
            if len(shape) == 2:
                ap = ap.rearrange("p (a b) -> p a b", a=shape[0])
            elif len(shape) == 3:
                ap = ap.rearrange("p (a b c) -> p a b c", a=shape[0], b=shape[1])
            return ap

        def carve(shape, dt=F32):
            n = 1
            for s_ in shape:
                n *= s_
            words = n if dt in (F32, F32R) else (n + 1) // 2
            off = ar["off"]
            ar["off"] = off + words
            assert ar["off"] <= ARENA_WORDS, ("arena overflow", ar["off"])
            ap = arena[:, off:off + words]
            if dt != F32:
                ap = ap.bitcast(dt)
            if len(shape) == 2:
                ap = ap.rearrange("p (a b) -> p a b", a=shape[0])
            elif len(shape) == 3:
                ap = ap.rearrange("p (a b c) -> p a b c", a=shape[0], b=shape[1])
            return ap

        wq_list = []
        wst = {"issued": 0, "next": 0}

        def w_issue_upto(i):
            while wst["issued"] <= min(i, len(wq_list) - 1):
                t = wst["issued"]
                name, dap, k, n = wq_list[t]
                slot = t % 2
                view = wr[slot][:, 0:k * n].rearrange("p (k n) -> p k n", k=k)
                dma("gpsimd", view, dap, (), [("wr", slot)])
                wst["issued"] += 1

        def w_pop(name):
            i = wst["next"]
            assert wq_list[i][0] == name, (wq_list[i][0], name)
            w_issue_upto(i + 1)
            wst["next"] += 1
            _, _, k, n = wq_list[i]
            slot = i % 2
            return wr[slot][:, 0:k * n].rearrange("p (k n) -> p k n", k=k), ("wr", slot)

        def wtile(name, dram2d, k, n):
            wq_list.append((name, dram2d.rearrange("(k p) n -> p k n", p=128), k, n))

        def nxt_mod(l, ip, b):
            if ip == 1 and l + 1 < nlayers:
                wtile("mod%d_%d" % (l + 1, b), D["wmod%d" % (l + 1)][:, b * 512:(b + 1) * 512], 8, 512)

        def layer_tiles(l, ip):
            j = l // 2
            if ip == 0 and l == 0:
                for b in range(6):
                    wtile("mod%d_%d" % (l, b), D["wmod%d" % l][:, b * 512:(b + 1) * 512], 8, 512)
            if l % 2 == 0:
                for g in range(4):
                    wtile("Aq", D["WA%d" % j][:, g, 0:256], 8, 256)
                    wtile("Ak", D["WA%d" % j][:, g, 256:384], 8, 128)
                    wtile("Avz", D["WA%d" % j][:, g, 384:704], 8, 320)
                    wtile("Ao", D["woutab%d" % j][g * 256:(g + 1) * 256, :], 2, 1024)
                    nxt_mod(l, ip, g)
                wtile("G", D["WG%d" % j][:, :], 8, 16)
                for h in range(4):
                    wtile("Bqk", D["WB%d" % j][:, h, 0:512], 8, 512)
                    wtile("Bv", D["WB%d" % j][:, h, 1024:1280], 8, 256)
                    wtile("Boz", D["WB%d" % j][:, h, 512:1024], 8, 512)
                    wtile("Bo", D["woutab%d" % j][1024 + h * 256:1024 + (h + 1) * 256, :], 2, 1024)
                    if h < 2:
                        nxt_mod(l, ip, 4 + h)
            else:
                wtile("Cqa", D["winc%d" % j][:, 0:384], 8, 384)
                wtile("Ckva", D["winc%d" % j][:, 384:640], 8, 256)
                wtile("Ckr", D["wkr%d" % j][:, :], 8, 192)
                nxt_mod(l, ip, 0)
                nxt_mod(l, ip, 1)
                for g in range(4):
                    wtile("Cqb", D["wqb%d" % j][:, g * 4:(g + 1) * 4, :].rearrange("a h c -> a (h c)"), 3, 768)
                    wtile("Ckvb", D["wkvb%d" % j][:, g * 512:(g + 1) * 512], 2, 512)
                    wtile("Cz", D["winc%d" % j][:, 672 + g * 256:672 + (g + 1) * 256], 8, 256)
                    wtile("Co", D["woutc%d" % j][g * 256:(g + 1) * 256, :], 2, 1024)
                    nxt_mod(l, ip, 2 + g)

        for l in range(nlayers):
            for ip in range(2):
                layer_tiles(l, ip)

        dma("sync", identF[:], D["ident"], (), ["identF"])
        cp("vector", identB[:], identF[:], ["identF"], ["identB"])
        dma("sync", arena[:, 640:768], D["permA"], (), ["permAf"])
        cp("vector", permB[:], arena[:, 640:768], ["permAf"], ["permB"])
        cp("vector", identR[:], identF[:], ["identF"], ["identR"])
        memset("vector", ones4[:], 1.0, ["ones4"])
        cp("vector", onesR[:], ones4[:, :], ["ones4"], ["onesR"])
        memset("vector", epsb[:], EPS, ["epsb"])
        dma("sync", selF[:], D["selrows"], (), ["selF"])
        dma("sync", maskAf[:], D["maskA"], (), ["maskAf"])
        cp("vector", maskAb[:], maskAf[:], ["maskAf"], ["maskAb"])
        dma("sync", maskFf[:, 0:128], D["maskF"], (), ["maskFf"])
        dma("sync", maskFf[:, 128:256], D["maskB"], (), ["maskFf"])
        cp("vector", maskFB[:], maskFf[:], ["maskFf"], ["maskFB"])
        for kc in range(8):
            dma("sync", yT[:, kc, :], D["xT"][kc * 128:(kc + 1) * 128, :], (), [("y", kc, b) for b in range(3)],
                semkey=("yload", kc))
        dma("sync", condt[:], D["condT"], (), ["condt"])
        act(cond_t[:], condt[:], AF.Tanh, ["condt"], ["cond_t"], scale=0.5)
        ts("vector", cond_t[:], cond_t[:], 0.5, ALU.mult, ["cond_t"], ["cond_t"], s2=0.5, op1=ALU.add)
        tt("vector", scond[:], cond_t[:], condt[:], ALU.mult, ["cond_t", "condt"], ["scond"])

        def ykey(kc, t0, blk):
            return ("y", kc, (t0 + blk * 512) // 512)

        def rstd_from_sumsq(ps_ap, pskey, n, out_ap, outkey, ncols, tmp_ap, tmpkey):
            act(tmp_ap, ps_ap, AF.Ln, [pskey, "epsb"], [tmpkey], scale=1.0 / n, bias=epsb[:, 0:1])
            act(out_ap, tmp_ap, AF.Exp, [tmpkey], [outkey], scale=-0.5)

        def y_update(ps_ap, pskey, oc, t0, blk, cond):
            k = ykey(oc, t0, blk)
            ysl = yT[:, oc, t0 + blk * 512:t0 + (blk + 1) * 512]
            mv = modv_b[cur["l"] % 2]
            stt(ysl, ps_ap, mv[:, 16 + oc, cond:cond + 1], ysl, ALU.mult, ALU.add, [pskey, k, ("modv", cur["l"] % 2)], [k])

        def out_proj(wname, aoT, aokey_fn, t0, TP, cond):
            wo, wkey = w_pop(wname)
            out_proj2(wo, wkey, aoT, aokey_fn, t0, TP, cond)

        def out_proj2(wo, wkey, aoT, aokey_fn, t0, TP, cond):
            for blk in range(TP // 512):
                for oc in range(8):
                    ps, pk = ps_next(ALLPS)
                    for fc in range(2):
                        mm(ps[:, :], wo[:, fc, oc * 128:(oc + 1) * 128], aoT[:, fc, blk * 512:(blk + 1) * 512],
                           fc == 0, fc == 1, [wkey, aokey_fn(fc, blk)], [pk])
                    y_update(ps[:, :], pk, oc, t0, blk, cond)

        def transpose_to_aoT(zs, zskey_fn, aoT, aokey_fn, nqt):
            for c in range(2):
                for b4 in range(nqt // 4):
                    ps, pk = ps_next(ALLPS)
                    for i in range(4):
                        qt = b4 * 4 + i
                        mm(ps[:, i * 128:(i + 1) * 128], zs[:, qt, c * 128:(c + 1) * 128], identB[:, :], True, True,
                           [zskey_fn(qt), "identB"], [pk])
                    cp("scalar", aoT[:, c, b4 * 512:(b4 + 1) * 512], ps[:, :], [pk], [aokey_fn(c, b4)])

        def attn_head(QT, qkeys_fn, qbase, qrows, KT, kkeys_fn, jobs, vfn, scale, sink_ap, zs, zskey_fn, hl, nqt, obank0):
            first = {}
            last = {}
            for ji, (kc0, lo, hi, mc, vid) in enumerate(jobs):
                for qt in range(lo, hi):
                    first.setdefault(qt, ji)
                    last[qt] = ji
            nb = (nqt + 3) // 4
            lastpv = {}
            for ji, (kc0, lo, hi, mc, vid) in enumerate(jobs):
                for qt in range(lo, hi):
                    lastpv[qt // 4] = (ji, qt)
            obanks = [(psb[obank0 + b], ("ps", obank0 + b)) for b in range(nb)]
            firstpv = {}
            for ji, (kc0, lo, hi, mc, vid) in enumerate(jobs):
                for qt in range(lo, hi):
                    firstpv.setdefault(qt // 4, (ji, qt))
            pieces = []
            for ji, (kc0, lo, hi, mc, vid) in enumerate(jobs):
                q0 = lo
                while q0 < hi:
                    q1 = min(hi, q0 + 4)
                    pieces.append((ji, kc0, lo, mc, vid, q0, q1))
                    q0 = q1

            def emit_score(pc):
                ji, kc0, lo, mc, vid, q0, q1 = pc
                ncol = (q1 - q0) * 128
                ps, pk = ps_next()
                mm(ps[:, 0:ncol], KT[qbase:qbase + qrows, kc0:kc0 + 128], QT[qbase:qbase + qrows, q0 * 128:q1 * 128],
                   True, mc is None, kkeys_fn(kc0) + qkeys_fn(q0, q1), [pk])
                if mc is not None:
                    m0_ = mc + (q0 - lo) * 128
                    mm(ps[:, 0:ncol], identB[:, :], maskAb[:, m0_:m0_ + ncol], False, True, ["identB", "maskAb"], [pk])
                eb, ek = e_next()
                act(eb[:, 0:ncol], ps[:, 0:ncol], AF.Exp, [pk], [ek], scale=scale)
                return eb, ek

            def emit_pv(pc, eb, ek):
                ji, kc0, lo, mc, vid, q0, q1 = pc
                vap, vkey = vfn(vid)
                for qt in range(q0, q1):
                    ob, ok = obanks[qt // 4]
                    c0 = (qt % 4) * 128
                    P.add("tensor", lambda e, o=ob[:, c0:c0 + 65], l=eb[:, (qt - q0) * 128:(qt - q0 + 1) * 128], r=vap,
                          s_=(firstpv[qt // 4] == (ji, qt)), t_=(lastpv[qt // 4] == (ji, qt)):
                          e.matmul(o, l, r, start=s_, stop=t_, skip_group_check=True), [ek, vkey], [ok])

            LA = 2
            q_ = [emit_score(pieces[i]) for i in range(min(LA, len(pieces)))]
            for i in range(len(pieces)):
                if i + LA < len(pieces):
                    q_.append(emit_score(pieces[i + LA]))
                eb_, ek_ = q_.pop(0)
                emit_pv(pieces[i], eb_, ek_)
            for b in range(nb):
                ob, ok = obanks[b]
                n4 = min(4, nqt - b * 4)
                ov = ob[:, 0:n4 * 128].rearrange("p (a c) -> p a c", a=n4)
                dsum, dk = small_next()
                if sink_ap is not None:
                    ts("vector", dsum[:, 0:n4], ov[:, :, 64], sink_ap, ALU.add, [ok, "sinkt"], [dk])
                else:
                    cp("vector", dsum[:, 0:n4], ov[:, :, 64], [ok], [dk])
                P.add("vector", lambda e, o=dsum[:, 4:4 + n4], i=dsum[:, 0:n4]: e.reciprocal(o, i), [dk], [dk])
                at, atk = atmp_next()
                tt("vector", at[:, 0:n4, :], ov[:, :, 0:64], dsum[:, 4:4 + n4].to_broadcast([128, n4, 64]), ALU.mult,
                   [ok, dk], [atk])
                zsl = zs[:, b * 4:b * 4 + n4, hl * 64:(hl + 1) * 64]
                zk = [zskey_fn(qt) for qt in range(b * 4, b * 4 + n4)]
                tt("gpsimd", zsl, zsl, at[:, 0:n4, :], ALU.mult, zk + [atk], zk)

        rot = {}

        def e_next():
            r = rot["E"]
            i = r["i"] % len(r["bufs"])
            r["i"] += 1
            return r["bufs"][i], ("E", i)

        def small_next():
            r = rot["S"]
            i = r["i"] % len(r["bufs"])
            r["i"] += 1
            return r["bufs"][i], ("dsum", i)

        def atmp_next():
            r = rot["A"]
            i = r["i"] % len(r["bufs"])
            r["i"] += 1
            return r["bufs"][i], ("atmp", i)

        def alloc_attn_rot():
            rot["E"] = {"i": 0, "bufs": [carve([512], BF16) for _ in range(4)]}
            rot["S"] = {"i": 0, "bufs": [carve([8]) for _ in range(2)]}
            rot["A"] = {"i": 0, "bufs": [carve([4, 64], BF16) for _ in range(2)]}

        def silu_tok(ps_ap, pskey, out_ap, outkey, tmp_ap, tmpkey, ncols):
            act(tmp_ap, ps_ap, AF.Tanh, [pskey], [tmpkey], scale=0.5)
            ts("gpsimd", tmp_ap, tmp_ap, 0.5, ALU.mult, [tmpkey], [tmpkey], s2=0.5, op1=ALU.add)
            tt("vector", out_ap, tmp_ap, ps_ap, ALU.mult, [tmpkey, pskey], [outkey])

        def rope_evac(psx, pkx, psp, pkp, r0, r1, cols, out_ap, outkey, tmpa, tmpb, tka, tkb):
            tt("vector", tmpa[r0:r1, :], psp[r0:r1, :], tabS[r0:r1, cols[0]:cols[1]], ALU.mult, [pkp, "tabS"], [tka])
            tt("vector", tmpb[r0:r1, :], psx[r0:r1, :], tabC[r0:r1, cols[0]:cols[1]], ALU.mult, [pkx, "tabC"], [tkb])
            tt("gpsimd", out_ap, tmpa[r0:r1, :], tmpb[r0:r1, :], ALU.add, [tka, tkb], [outkey])

        def dbg(name, ap, reads):
            if name in DBG_SPECS:
                P.add("sync", lambda e, o=D[name], i=ap: e.dma_start(out=o, in_=i), reads, [], dma=True, semkey=("dbg", name), is_out=True)

        def arena_reset(keep_base=False, keepR=False, barrier=True):
            if barrier:
                P.barrier()
            ar["off"] = ar.get("base", 0) if keep_base else 0
            if not keep_base:
                ar["base"] = 0
                ar["baseR"] = 0
            if not keepR:
                ar["offR"] = ar.get("baseR", 0) if keep_base else 0

        def mod_load_small(l):
            i = l % 2
            dma("sync", bmodt_b[i][:], D["bmodT%d" % l], (), [("bmodt", i)])
            dma("sync", normgt_b[i][:], D["normgT%d" % l], (), [("normgt", i)])

        def mod_tile(l, b):
            i = l % 2
            modrow = arena[:, ARENA_WORDS - 512:ARENA_WORDS]
            w, wk = w_pop("mod%d_%d" % (l, b))
            psr, pkr = ps_next(ALLPS)
            for kc in range(8):
                mm(psr[0:2, :], scond[:, kc, :], w[:, kc, :], kc == 0, kc == 7, [wk, "scond"], [pkr])
            cp("scalar", modrow[0:2, :], psr[0:2, :], [pkr], ["modrow"])
            ps, pk = ps_next(ALLPS)
            for oc4 in range(4):
                mm(ps[:, 2 * oc4:2 * oc4 + 2], modrow[0:2, oc4 * 128:(oc4 + 1) * 128], identF[0:2, 0:2], True, True,
                   ["modrow", "identF"], [pk])
            psv = ps[:, 0:8].rearrange("p (c j) -> p c j", j=2)
            for jj in range(2):
                tt("vector", modv_b[i][:, b * 4:(b + 1) * 4, jj], psv[:, :, jj], bmodt_b[i][:, b * 4:(b + 1) * 4], ALU.add,
                   [pk, ("bmodt", i)], [("modv", i)])

        def mod_finish(l):
            i = l % 2
            for jj in range(2):
                stt(avec_b[i][:, :, jj], modv_b[i][:, 8:16, jj], 1.0, normgt_b[i][:, :], ALU.add, ALU.mult,
                    [("modv", i), ("normgt", i)], [("avec", i)])

        def layer_mod(l):
            P.phase = "mod"
            mod_load_small(l)
            for b in range(6):
                mod_tile(l, b)
            mod_finish(l)

        def norm_blocks(t0, nblk, scale_fn, bias_fn, out_fn, extra_reads):
            sqb = [carveR([512]) for _ in range(2)]
            lnb = carve([512])
            rstd = carve([512])
            tmb = [carve([512]) for _ in range(2)]
            for blk in range(nblk):
                c0 = t0 + blk * 512
                ps, pk = ps_next(ALLPS)
                for kc in range(8):
                    act(sqb[kc % 2][:, :], yT[:, kc, c0:c0 + 512], AF.Square, [ykey(kc, t0, blk)], [("sqb", kc % 2)])
                    mm(ps[:, :], onesR[:, :], sqb[kc % 2][:, :], kc == 0, kc == 7, ["onesR", ("sqb", kc % 2)], [pk])
                rstd_from_sumsq(ps[:, :], pk, 1024.0, rstd[:, :], "rstd", 512, lnb[:, :], "lnb")
                for kc in range(8):
                    tt("vector", tmb[kc % 2][:, :], yT[:, kc, c0:c0 + 512], rstd[:, :], ALU.mult,
                       [ykey(kc, t0, blk), "rstd"], [("tmb", kc % 2)])
                    oap, okey = out_fn(kc, blk)
                    P.add("scalar", lambda e, o=oap, i=tmb[kc % 2][:, :], s=scale_fn(kc), b=bias_fn(kc):
                          e.activation(out=o, in_=i, func=AF.Identity, bias=b, scale=s),
                          [("tmb", kc % 2)] + extra_reads, [okey])

        def layer_h(t0, TP, cond):
            P.phase = "h"
            arena_reset()
            i_ = cur["l"] % 2
            norm_blocks(t0, TP // 512,
                        lambda kc: avec_b[i_][:, kc, cond:cond + 1], lambda kc: modv_b[i_][:, kc, cond:cond + 1],
                        lambda kc, blk: (hT[:, kc, blk * 512:(blk + 1) * 512], ("h", kc, blk)), [("avec", i_), ("modv", i_)])

        def final_out():
            P.phase = "final"
            arena_reset()
            fn = carve([8])
            dma("sync", fn[:, :], D["fnormT"], (), ["fnorm"])
            stg = [carve([512]) for _ in range(3)]
            cnt = {"i": 0}

            def out_fn(kc, blk):
                i = cnt["i"] % 3
                cnt["i"] += 1
                cnt["last"] = (i, kc, blk)
                return stg[i][:, :], ("stg", i)
            if do_final_norm:
                sqb = [carveR([512]) for _ in range(2)]
                lnb = carve([512])
                rstd = carve([512])
                tmb = [carve([512]) for _ in range(2)]
                for blk in range(3):
                    c0 = blk * 512
                    ps, pk = ps_next(ALLPS)
                    for kc in range(8):
                        act(sqb[kc % 2][:, :], yT[:, kc, c0:c0 + 512], AF.Square, [("y", kc, blk)], [("sqb", kc % 2)])
                        mm(ps[:, :], onesR[:, :], sqb[kc % 2][:, :], kc == 0, kc == 7, ["onesR", ("sqb", kc % 2)], [pk])
                    rstd_from_sumsq(ps[:, :], pk, 1024.0, rstd[:, :], "rstd", 512, lnb[:, :], "lnb")
                    for kc in range(8):
                        tt("vector", tmb[kc % 2][:, :], yT[:, kc, c0:c0 + 512], rstd[:, :], ALU.mult,
                           [("y", kc, blk), "rstd"], [("tmb", kc % 2)])
                        oap, okey = out_fn(kc, blk)
                        act(oap, tmb[kc % 2][:, :], AF.Copy, [("tmb", kc % 2), "fnorm"], [okey], scale=fn[:, kc:kc + 1])
                        dma("sync", D["yT_o"][kc * 128:(kc + 1) * 128, c0:c0 + 512], oap, [okey], [], semkey=okey, is_out=True)
            else:
                for kc in range(8):
                    dma("sync", D["yT_o"][kc * 128:(kc + 1) * 128, :], yT[:, kc, :], [("y", kc, b) for b in range(3)], [],
                        semkey=("yout", kc), is_out=True)

        def a_unit(j, g, t0, TP, segs, is_sample, cond):
            arena_reset(barrier=(g == 0))
            P.phase = "A_proj%d" % int(is_sample)
            nqt = TP // 128
            nblk = TP // 512
            NK = TP + (256 if is_sample else 0)
            nvt = nqt + (2 if is_sample else 0)
            qT = carve([2, TP], BF16)
            KT2 = [carve([NK], BF16) for _ in range(2)]
            Vaug = carve([nvt, 65], BF16)
            zs = carve([nqt, 256], BF16)
            aoT = carveR([2, TP])
            ta = [carve([512]) for _ in range(2)]
            tb = [carve([512]) for _ in range(2)]
            tz = [carve([256]) for _ in range(2)]
            stg = carve([512])
            stv = [carve([64]) for _ in range(2)]
            xbr = [carve([512], BF16) for _ in range(2)]
            alloc_attn_rot()
            vkeys = [("V", i) for i in range(nvt)]
            memset("gpsimd", Vaug[:, :, 64:65], 1.0, vkeys)
            allkt = [("KT", b) for b in range(nblk)] + ([("KT", "ctx")] if is_sample else [])
            memset("gpsimd", KT2[0][64:128, :], 0.0, allkt + ["KTz"])
            memset("gpsimd", KT2[1][0:64, :], 0.0, allkt + ["KTz"])
            hk = lambda kc, blk: ("h", kc, blk)
            wq, wk = w_pop("Aq")
            tiles = [("q", c, blk) for c in range(2) for blk in range(nblk)] + [("k", 0, blk) for blk in range(nblk)]
            st_ = {}
            wkh = {}

            def proj_x(i):
                kind, c, blk = tiles[i]
                cols = (blk * 512, (blk + 1) * 512)
                if kind == "k" and "w" not in wkh:
                    wkh["w"] = w_pop("Ak")
                psx, pkx = ps_next(ALLPS)
                for kc in range(8):
                    if kind == "q":
                        lhs, wkey_ = wq[:, kc, c * 128:(c + 1) * 128], wk
                    else:
                        lhs, wkey_ = wkh["w"][0][:, kc, 0:128], wkh["w"][1]
                    mm(psx[:, :], lhs, hT[:, kc, cols[0]:cols[1]], kc == 0, kc == 7, [wkey_, hk(kc, blk)], [pkx])
                if is_sample:
                    xb_, xk_ = xbr[i % 2], ("xb", i % 2)
                    cp("scalar", xb_[:, :], psx[:, :], [pkx], [xk_])
                st_[i] = (psx, pkx)

            def proj_y(i):
                kind, c, blk = tiles[i]
                cols = (blk * 512, (blk + 1) * 512)
                psx, pkx = st_.pop(i)
                if is_sample:
                    xb_, xk_ = xbr[i % 2], ("xb", i % 2)
                    psp, pkp = ps_next(ALLPS)
                    mm(psp[:, :], permB[:, :], xb_[:, :], True, True, ["permB", xk_], [pkp])
                    tA, tB, kA, kB = ta[i % 2], tb[i % 2], ("ta", i % 2), ("tb", i % 2)
                    tt("vector", tA[:, :], psp[:, :], tabS[:, cols[0]:cols[1]], ALU.mult, [pkp, "tabS"], [kA])
                    tt("vector", tB[:, :], psx[:, :], tabC[:, cols[0]:cols[1]], ALU.mult, [pkx, "tabC"], [kB])
                    if kind == "q":
                        tt("gpsimd", qT[:, c, cols[0]:cols[1]], tA[:, :], tB[:, :], ALU.add, [kA, kB], [("qT", c, blk)])
                    else:
                        tt("gpsimd", KT2[0][0:64, cols[0]:cols[1]], tA[0:64, :], tB[0:64, :], ALU.add, [kA, kB], [("KT", blk)])
                        tt("gpsimd", KT2[1][64:128, cols[0]:cols[1]], tA[64:128, :], tB[64:128, :], ALU.add, [kA, kB], [("KT", blk)])
                else:
                    if kind == "q":
                        cp("scalar", qT[:, c, cols[0]:cols[1]], psx[:, :], [pkx], [("qT", c, blk)])
                    else:
                        cp("scalar", KT2[0][0:64, cols[0]:cols[1]], psx[0:64, :], [pkx], [("KT", blk)])
                        cp("scalar", KT2[1][64:128, cols[0]:cols[1]], psx[64:128, :], [pkx], [("KT", blk)])
                        if not FLAGS.get("noka"):
                            cp("vector", stg[0:64, :], psx[0:64, :], [pkx], ["stg"])
                            dma("sync", D["kaT_o"][j, g * 64:(g + 1) * 64, :], stg[0:64, :], ["stg"], [], semkey=("kaout", g), is_out=True)

            proj_x(0)
            for i in range(len(tiles)):
                if i + 1 < len(tiles):
                    proj_x(i + 1)
                proj_y(i)
            if is_sample and not FLAGS.get("noctx"):
                dma("sync", stg[:, 0:256], D["KctxT%d" % j][g], (), ["stg"])
                cp("vector", KT2[0][0:64, TP:TP + 256], stg[0:64, 0:256], ["stg"], [("KT", "ctx")])
                cp("vector", KT2[1][64:128, TP:TP + 256], stg[64:128, 0:256], ["stg"], [("KT", "ctx")])
            if FLAGS.get("a_stop", 9) <= 2:
                for nm in ("Avz", "Ao"):
                    w_pop(nm)
                return
            wv, wvk = w_pop("Avz")
            for qt in range(nqt):
                ps, pk = ps_next(ALLPS)
                for kc in range(8):
                    mm(ps[:, 0:320], hT[:, kc, qt * 128:(qt + 1) * 128], wv[:, kc, 0:320], kc == 0, kc == 7,
                       [hk(kc, qt // 4), wvk], [pk])
                cp("scalar", Vaug[:, qt, 0:64], ps[:, 0:64], [pk], [("V", qt)])
                silu_tok(ps[:, 64:320], pk, zs[:, qt, :], ("zs", qt), tz[qt % 2][:, :], ("tz", qt % 2), 256)
                if not is_sample:
                    cp("vector", stv[qt % 2][:, :], ps[:, 0:64], [pk], [("stv", qt % 2)])
                    dma("sync", D["va_o"][j, qt * 128:(qt + 1) * 128, g * 64:(g + 1) * 64], stv[qt % 2][:, :],
                        [("stv", qt % 2)], [], semkey=("stv", qt % 2), is_out=True)
            if is_sample:
                for c in range(2):
                    dma("sync", stv[c][:, :], D["Vctx%d" % j][c * 128:(c + 1) * 128, g * 64:(g + 1) * 64], (), [("stv", c)])
                    cp("vector", Vaug[:, nqt + c, 0:64], stv[c][:, :], [("stv", c)], [("V", nqt + c)])
            if FLAGS.get("a_stop", 9) <= 3:
                w_pop("Ao")
                return
            P.phase = "A_attn%d" % int(is_sample)
            for hl in range(4):
                p = hl % 2
                c = hl // 2
                h = g * 4 + hl
                if is_sample:
                    jobs = [(TP, 0, 8, None, nqt), (TP + 128, 0, 8, None, nqt + 1)]
                    for jt in range(8):
                        lo = max(0, jt - 1)
                        hi = min(8, jt + 2)
                        jobs.append((jt * 128, lo, hi, (lo - (jt - 1)) * 128, jt))
                else:
                    jobs = []
                    for s_ in range(2):
                        for jt in (2 * s_, 2 * s_ + 1):
                            jobs.append((jt * 128, 2 * s_, 2 * s_ + 2, None, jt))
                attn_head(qT[:, c, :], lambda q0, q1, c=c: [("qT", c, b) for b in range(q0 // 4, (q1 - 1) // 4 + 1)],
                          0, 128, KT2[p], lambda kc0: ["KTz"] + ([("KT", "ctx")] if kc0 >= TP else [("KT", kc0 // 512)]),
                          jobs, lambda vid: (Vaug[:, vid, :], ("V", vid)), 0.125, sinkt[:, h:h + 1],
                          zs, lambda qt: ("zs", qt), hl, nqt, (hl % 2) * 2)
            if FLAGS.get("a_stop", 9) <= 4:
                w_pop("Ao")
                return
            P.phase = "A_out%d" % int(is_sample)
            transpose_to_aoT(zs, lambda qt: ("zs", qt), aoT, lambda c, b: ("aoT", c, b), nqt)
            if FLAGS.get("a_stop", 9) <= 5:
                w_pop("Ao")
                return
            out_proj("Ao", aoT, lambda fc, blk: ("aoT", fc, blk), t0, TP, cond)

        def run_layers(which):
            for l in range(nlayers):
                j = l // 2
                even = (l % 2 == 0)
                cur["l"] = l
                if l == 0:
                    layer_mod(l)
                else:
                    mod_finish(l)
                if even:
                    dma("sync", tabC[:], D["cosA"], (), ["tabC"])
                    dma("sync", tabS[:], D["sinA"], (), ["tabS"])
                    dma("sync", sinkt[:], D["sinkb%d" % j], (), ["sinkt_raw"])
                    act(sinkt[:], sinkt[:], AF.Exp, ["sinkt_raw"], ["sinkt"])
                    dma("sync", convt[:], D["convT%d" % j], (), ["convt"])
                    dma("sync", smallc[:, 0:4], D["gbias%d" % j], (), ["gbias"])
                    dma("sync", smallc[:, 8:16], D["normbT%d" % j], (), ["normbt"])
                    dma("sync", m0t[:], D["m0%d" % j], (), ["m0t"])
                else:
                    dma("sync", tabC[:], D["cosC"], (), ["tabC"])
                    dma("sync", tabS[:], D["sinC"], (), ["tabS"])
                    dma("sync", smallc[:, 16:19], D["qnormT%d" % j], (), ["qnormt"])
                    dma("sync", smallc[:, 24:26], D["kvnormT%d" % j], (), ["kvnormt"])
                for ip, (t0, TP, segs, is_sample) in enumerate(PASSES):
                    cond = 0 if is_sample else 1
                    pre = (ip == 1 and l + 1 < nlayers)
                    if pre:
                        mod_load_small(l + 1)

                    def premod(b):
                        if pre:
                            ph = P.phase
                            P.phase = "mod"
                            mod_tile(l + 1, b)
                            P.phase = ph
                    layer_h(t0, TP, cond)
                    if l == 0:
                        dbg("hT%d" % ip, hT[:, :, 0:TP].bitcast(F32), [("h", kc, b) for kc in range(8) for b in range(TP // 512)])
                    if even:
                        for g in range(4):
                            if "A" in which:
                                a_unit(j, g, t0, TP, segs, is_sample, cond)
                            else:
                                for nm in ("Aq", "Ak", "Avz", "Ao"):
                                    w_pop(nm)
                            premod(g)
                        if "B" in which:
                            gates_stage(j, t0, TP, segs, is_sample)
                            for hb in range(4):
                                b_unit(j, hb, t0, TP, segs, is_sample, cond)
                                if hb < 2:
                                    premod(4 + hb)
                        else:
                            w_pop("G")
                            for hb in range(4):
                                for nm in ("Bqk", "Bv", "Boz", "Bo"):
                                    w_pop(nm)
                                if hb < 2:
                                    premod(4 + hb)
                    else:
                        if "C" in which:
                            c_pre(j, t0, TP, segs, is_sample)
                            premod(0)
                            premod(1)
                            for g in range(4):
                                c_unit(j, g, t0, TP, segs, is_sample, cond)
                                premod(2 + g)
                        else:
                            for nm in ("Cqa", "Ckva", "Ckr"):
                                w_pop(nm)
                            premod(0)
                            premod(1)
                            for g in range(4):
                                for nm in ("Cqb", "Ckvb", "Cz", "Co"):
                                    w_pop(nm)
                                premod(2 + g)
            final_out()


        G = {}

        def gates_stage(j, t0, TP, segs, is_sample):
            P.phase = "gates%d" % int(is_sample)
            arena_reset()
            nchtot = TP // 128
            ranges = [carve([TP]) for _ in range(1)]
            wib = carve([TP], BF16)

            def garr(idx, d_):
                return ranges[idx][d_ * 32:d_ * 32 + 4, :], d_ * 32, ("ga", idx, d_)
            cols = carve([nchtot, 32])
            w0r = carve([nchtot])
            G["cols"] = cols
            G["w0r"] = w0r
            G["U"] = [garr(0, 0), garr(0, 1)]
            G["WIb"] = [(wib[d_ * 32:d_ * 32 + 4, :], d_ * 32, ("wib", d_)) for d_ in range(2)]
            ar["base"] = ar["off"]
            ranges += [carve([TP]) for _ in range(5)]
            G["G"] = [garr(2, 0), garr(2, 1)]
            ts("vector", smallc[:, 4:8], smallc[:, 0:4], -1.0, ALU.mult, ["gbias"], ["ngbias"])
            wg, wgk = w_pop("G")
            hk = lambda kc, blk: ("h", kc, blk)
            def gdir(d):
                T_li, b_li, k_li = garr(1, d)
                T_lf, b_lf, k_lf = garr(3, d)
                T_B, b_B, k_B = garr(4, d)
                T_m, b_m, k_m = garr(5, d)
                U, b_U, k_U = G["U"][d]
                Gg, b_G, k_G = G["G"][d]
                qi_i = 2 * d
                qi_f = 2 * d + 1
                for blk in range(TP // 512):
                    c0, c1 = blk * 512, (blk + 1) * 512
                    ps, pk = ps_next(ALLPS)
                    for kc in range(8):
                        mm(ps[0:4, :], wg[:, kc, qi_i * 4:(qi_i + 1) * 4], hT[:, kc, c0:c1], kc == 0, kc == 7, [wgk, hk(kc, blk)], [pk])
                    act(T_li[:, c0:c1], ps[0:4, :], AF.Identity, [pk, "gbias"], [k_li], bias=smallc[0:4, qi_i:qi_i + 1])
                    yield
                    ps2, pk2 = ps_next(ALLPS)
                    for kc in range(8):
                        mm(ps2[0:4, :], wg[:, kc, qi_f * 4:(qi_f + 1) * 4], hT[:, kc, c0:c1], kc == 0, kc == 7, [wgk, hk(kc, blk)], [pk2])
                    act(T_lf[:, c0:c1], ps2[0:4, :], AF.Exp, [pk2, "ngbias"], [k_lf], scale=-1.0, bias=smallc[0:4, 4 + qi_f:5 + qi_f])
                    yield
                act(T_lf[:, :], T_lf[:, :], AF.Ln, [k_lf], [k_lf], bias=1.0)
                yield
                ts("vector", T_lf[:, :], T_lf[:, :], -1.0, ALU.mult, [k_lf], [k_lf])
                yield
                for si, (s0, s1) in enumerate(segs):
                    def dirv(ap):
                        v = ap[:, s0:s1]
                        return v[:, ::-1] if d == 1 else v
                    n_ = s1 - s0
                    onesv = ones4[b_B:b_B + 4, 0:1].to_broadcast([4, n_])
                    P.add("vector", lambda e, o=dirv(T_B), a=onesv, b=dirv(T_lf): e.tensor_tensor_scan(
                        out=o, data0=a, data1=b, initial=0.0, op0=ALU.mult, op1=ALU.add), ["ones4", k_lf], [k_B])
                    yield
                    init = m0t[b_m:b_m + 4, d:d + 1] if is_sample else 0.0
                    P.add("vector", lambda e, o=dirv(T_m), a=dirv(T_lf), b=dirv(T_li), i_=init: e.tensor_tensor_scan(
                        out=o, data0=a, data1=b, initial=i_, op0=ALU.add, op1=ALU.max), [k_lf, k_li, "m0t"], [k_m])
                    yield
                tt("vector", U[:, :], T_B[:, :], T_m[:, :], ALU.subtract, [k_B, k_m], [k_U])
                yield
                tt("vector", Gg[:, :], T_li[:, :], T_B[:, :], ALU.subtract, [k_li, k_B], [k_G])
                yield
                for si, (s0, s1) in enumerate(segs):
                    nch = (s1 - s0) // 128
                    order = list(range(nch)) if d == 0 else list(range(nch - 1, -1, -1))
                    for oi, c in enumerate(order):
                        a0 = s0 + c * 128
                        a1 = a0 + 128
                        endi = (a1 - 1) if d == 0 else a0
                        if oi == 0:
                            if is_sample:
                                ts("vector", T_li[:, a0:a1], U[:, a0:a1], m0t[b_li:b_li + 4, d:d + 1], ALU.add, [k_U, "m0t"], [k_li])
                                yield
                            else:
                                cp("vector", T_li[:, a0:a1], U[:, a0:a1], [k_U], [k_li])
                                yield
                        else:
                            pc = order[oi - 1]
                            pend = (s0 + pc * 128 + 127) if d == 0 else (s0 + pc * 128)
                            ts("vector", T_li[:, a0:a1], U[:, a0:a1], U[:, pend:pend + 1], ALU.subtract, [k_U], [k_li])
                            yield
                        ts("vector", T_lf[:, a0:a1], Gg[:, a0:a1], U[:, endi:endi + 1], ALU.add, [k_G, k_U], [k_lf])
                        yield
                act(T_li[:, :], T_li[:, :], AF.Exp, [k_li], [k_li])
                yield
                cp("vector", G["WIb"][d][0][:, :], T_li[:, :], [k_li], [G["WIb"][d][2]])
                yield
                act(T_lf[:, :], T_lf[:, :], AF.Exp, [k_lf], [k_lf])
                yield
                act(T_B[:, :], T_m[:, :], AF.Exp, [k_m], [k_B], scale=-1.0)
                yield
                for si, (s0, s1) in enumerate(segs):
                    nch = (s1 - s0) // 128
                    e0 = (s0 + 127) if d == 0 else s0
                    cp("vector", w0r[d * 32:d * 32 + 4, s0 // 128:s0 // 128 + nch], T_li[:, e0:s1:128], [k_li], [("w0r", d)])
                    yield
                    if not is_sample:
                        mi = (s1 - 1) if d == 0 else s0
                        dma("sync", D["m_o"][j, si, d, :].rearrange("(p o) -> p o", o=1), T_m[:, mi:mi + 1], [k_m], [], semkey=("mout", d, si), is_out=True)
                        yield
                for cg in range(nchtot):
                    ps, pk = ps_next(ALLPS)
                    for qq, (X, bX, kX) in enumerate([(T_li, b_li, k_li), (T_B, b_B, k_B), (T_lf, b_lf, k_lf), (Gg, b_G, k_G)]):
                        qi = d * 4 + qq
                        mm(ps[:, qi * 4:(qi + 1) * 4], X[:, cg * 128:(cg + 1) * 128], identF[bX:bX + 4, bX:bX + 4], True, True,
                           [kX, "identF"], [pk])
                    cp("vector", cols[:, cg, d * 16:(d + 1) * 16], ps[:, d * 16:(d + 1) * 16], [pk], [("cols", d)])
                    yield

            gens = [gdir(0), gdir(1)]
            alive = [True, True]
            while any(alive):
                for gi_ in range(2):
                    if alive[gi_]:
                        try:
                            next(gens[gi_])
                        except StopIteration:
                            alive[gi_] = False

        def b_unit(j, hb, t0, TP, segs, is_sample, cond):
            P.phase = "B_proj%d" % int(is_sample)
            arena_reset(keep_base=True)
            nqt = TP // 128
            nblk = TP // 512
            nchtot = nqt
            cols = G["cols"]
            w0r = G["w0r"]
            Hsum = carveR([nqt, 256])
            Hsum32 = Hsum.bitcast(F32)
            xbufs = [carve([TP]) for _ in range(2)]
            accs = [carve([TP]) for _ in range(2)]
            qT = carve([2, TP], BF16)
            kT = carve([2, TP], BF16)
            Vaug = carve([nqt, 258], BF16)
            nseg = len(segs)
            C32s = [[carve([2, 258]) for _ in range(2)] for _ in range(nseg)]
            Cbs = [[[carve([2, 258], BF16) for _ in range(2)] for _ in range(2)] for _ in range(nseg)]
            W0bc = [carve([nchtot]) for _ in range(2)]
            nrot = 4 * nseg
            DT = [carve([128], BF16) for _ in range(nrot)]
            PT = [carve([128], BF16) for _ in range(nrot)]
            kw = [carve([256], BF16) for _ in range(nrot)]
            qw = [carve([2, 128], BF16) for _ in range(nrot)]
            ktok_all = xbufs[0].bitcast(BF16).rearrange("p (a b) -> p a b", a=nqt)
            dd = [carve([2]) for _ in range(2 * nseg)]
            hk = lambda kc, blk: ("h", kc, blk)
            memset("gpsimd", Vaug[:, :, 256:257], 1.0, [("V", i) for i in range(nqt)])
            for si in range(nseg):
                for d in range(2):
                    ck = ("C32", si, d)
                    memset("gpsimd", C32s[si][d][:, :, :], 0.0, [ck])
                    if is_sample:
                        for dc in range(2):
                            dma("sync", C32s[si][d][:, dc, 0:256], D["C0%d" % j][d, hb, dc * 128:(dc + 1) * 128, :], (), [ck], semkey=("C0ld", d))
                            dma("sync", C32s[si][d][:, dc, 256:257],
                                D["n0%d" % j][d, hb, dc * 128:(dc + 1) * 128].rearrange("(p o) -> p o", o=1), (), [ck], semkey=("C0ld", d))
            wqk, wk = w_pop("Bqk")
            def conv_a(c4):
                isk = c4 >= 2
                c = c4 % 2
                ci = (8 if isk else 0) + 2 * hb + c
                xbuf, acc = xbufs[c4 % 2], accs[c4 % 2]
                xk, ak = ("xbuf", c4 % 2), ("acc", c4 % 2)
                for blk in range(nblk):
                    c0, c1 = blk * 512, (blk + 1) * 512
                    ps, pk = ps_next(ALLPS)
                    for kc in range(8):
                        mm(ps[:, :], wqk[:, kc, c4 * 128:(c4 + 1) * 128], hT[:, kc, c0:c1], kc == 0, kc == 7, [wk, hk(kc, blk)], [pk])
                    cp("scalar", xbuf[:, c0:c1], ps[:, :], [pk], [xk])
                for (s0, s1) in segs:
                    ts("gpsimd", acc[:, s0:s1], xbuf[:, s0:s1], convt[:, ci, 1:2], ALU.mult, [xk, "convt"], [ak], s2=0.0, op1=ALU.add)
                    stt(acc[:, s0 + 1:s1], xbuf[:, s0:s1 - 1], convt[:, ci, 0:1], acc[:, s0 + 1:s1], ALU.mult, ALU.add,
                        [xk, "convt", ak], [ak])
                    stt(acc[:, s0:s1 - 1], xbuf[:, s0 + 1:s1], convt[:, ci, 2:3], acc[:, s0:s1 - 1], ALU.mult, ALU.add,
                        [xk, "convt", ak], [ak])

            def conv_b(c4):
                isk = c4 >= 2
                c = c4 % 2
                xbuf, acc = xbufs[c4 % 2], accs[c4 % 2]
                xk, ak = ("xbuf", c4 % 2), ("acc", c4 % 2)
                act(xbuf[:, :], acc[:, :], AF.Tanh, [ak], [xk], scale=0.5)
                sf = (1.0 / 32.0) if isk else 0.5
                ts("gpsimd", xbuf[:, :], xbuf[:, :], sf, ALU.mult, [xk], [xk], s2=sf, op1=ALU.add)
                dst = kT if isk else qT
                tt("vector", dst[:, c, :], xbuf[:, :], acc[:, :], ALU.mult, [xk, ak], [("kT" if isk else "qT", c)])

            conv_a(0)
            for c4 in range(4):
                if c4 + 1 < 4:
                    conv_a(c4 + 1)
                conv_b(c4)
            wv, wvk = w_pop("Bv")
            for qt in range(nqt):
                ps, pk = ps_next(ALLPS)
                for kc in range(8):
                    mm(ps[:, 0:256], hT[:, kc, qt * 128:(qt + 1) * 128], wv[:, kc, 0:256], kc == 0, kc == 7, [hk(kc, qt // 4), wvk], [pk])
                cp("scalar", Vaug[:, qt, 0:256], ps[:, 0:256], [pk], [("V", qt)])
            for qt in range(nqt):
                ps, pk = ps_next(ALLPS)
                for dc in range(2):
                    mm(ps[:, dc * 128:(dc + 1) * 128], kT[:, dc, qt * 128:(qt + 1) * 128], identB[:, :], True, True, [("kT", dc), "identB"], [pk])
                cp("scalar" if qt % 2 else "vector", ktok_all[:, qt, :], ps[:, 0:256], [pk], [("ktok", qt), ("xbuf", 0)])
            for d in range(2):
                ps, pk = ps_next(ALLPS)
                mm(ps[:, 0:nchtot], selF[d * 32:d * 32 + 4, hb * 128:(hb + 1) * 128], w0r[d * 32:d * 32 + 4, 0:nchtot], True, True,
                   ["selF", ("w0r", d)], [pk])
                cp("vector", W0bc[d][:, :], ps[:, 0:nchtot], [pk], [("W0bc", d)])
            P.phase = "B_chunks%d" % int(is_sample)
            hs_sets = [set() for _ in segs]
            for si in range(nseg):
                for d in range(2):
                    cp("scalar", Cbs[si][d][1][:, :, :], C32s[si][d][:, :, :], [("C32", si, d)], [("Cb", si, d, 1)])
            if True:
                def step_info(si, oi):
                    s0, s1 = segs[si]
                    nch = (s1 - s0) // 128
                    orders = [list(range(nch)), list(range(nch - 1, -1, -1))]
                    par = oi % 2
                    info = []
                    for d in range(2):
                        c = orders[d][oi]
                        a0 = s0 + c * 128
                        info.append((d, a0, a0 + 128, a0 // 128, (is_sample and oi == nch - 1), si * 4 + d * 2 + par))
                    return par, info

                def front(si, oi):
                    par, info = step_info(si, oi)
                    banks = []
                    for (d, a0, a1, cg, skip, r) in info:
                        U, b_U, k_U = G["U"][d]
                        WI, b_W, k_W = G["WIb"][d]
                        psx, pkx = ps_next(ALLPS)
                        banks.append((psx, pkx))
                        for dc in range(2):
                            mm(psx[:, 0:128], kT[:, dc, a0:a1], qT[:, dc, a0:a1], dc == 0, dc == 1, [("kT", dc), ("qT", dc)], [pkx])
                        mm(psx[:, 128:256], selF[b_U:b_U + 4, hb * 128:(hb + 1) * 128], U[:, a0:a1], True, False, [k_U, "selF"], [pkx])
                        mm(psx[:, 128:256], identB[:, :], maskFB[:, d * 128:(d + 1) * 128], False, True, ["identB", "maskFB"], [pkx])
                        mm(psx[:, 256:384], identB[b_W:b_W + 4, b_W + hb:b_W + hb + 1].to_broadcast([4, 128]), WI[:, a0:a1], True, True,
                           [k_W, "identB"], [pkx])
                    for (d, a0, a1, cg, skip, r), (psx, pkx) in zip(info, banks):
                        colb = d * 16
                        act(DT[r][:, :], psx[:, 128:256], AF.Exp, [pkx, ("cols", d)], [("DT", r)],
                            bias=cols[:, cg, colb + 12 + hb:colb + 12 + hb + 1])
                        if not skip:
                            ts("gpsimd", kw[r][:, :], ktok_all[:, cg, :], cols[:, cg, colb + 8 + hb:colb + 8 + hb + 1], ALU.mult,
                               [("ktok", cg), ("cols", d)], [("kw", r)], s2=0.0, op1=ALU.add)
                        tt("vector", qw[r][:, :, :], qT[:, :, a0:a1], psx[:, 256:384].unsqueeze(1).to_broadcast([128, 2, 128]), ALU.mult,
                           [("qT", 0), ("qT", 1), pkx], [("qw", r)])
                    for (d, a0, a1, cg, skip, r), (psx, pkx) in zip(info, banks):
                        tt("vector", PT[r][:, :], psx[:, 0:128], DT[r][:, :], ALU.mult, [pkx, ("DT", r)], [("PT", r)])

                def back(si, oi):
                    par, info = step_info(si, oi)
                    C32 = C32s[si]
                    Cb = Cbs[si]
                    hs_written = hs_sets[si]
                    res = []
                    for (d, a0, a1, cg, skip, r) in info:
                        ps3, pk3 = ps_next(ALLPS)
                        psd, pkd = (None, None)
                        if not skip:
                            psd, pkd = ps_next(ALLPS)
                            for dc in range(2):
                                mm(psd[:, dc * 256:(dc + 1) * 256], kw[r][:, dc * 128:(dc + 1) * 128], Vaug[:, cg, 0:256], True, True,
                                   [("kw", r), ("V", cg)], [pkd])
                            for dc in range(2):
                                mm(ps3[:, 384 + dc:385 + dc], kw[r][:, dc * 128:(dc + 1) * 128], Vaug[:, cg, 256:257], True, True,
                                   [("kw", r), ("V", cg)], [pk3])
                        res.append((ps3, pk3, psd, pkd))
                    for (d, a0, a1, cg, skip, r), (ps3, pk3, psd, pkd) in zip(info, res):
                        mm(ps3[:, 0:257], PT[r][:, :], Vaug[:, cg, 0:257], True, False, [("PT", r), ("V", cg)], [pk3])
                        for dc in range(2):
                            mm(ps3[:, 0:257], qw[r][:, dc, :], Cb[d][1 - par][:, dc, 0:257], False, dc == 1,
                               [("qw", r), ("Cb", si, d, 1 - par)], [pk3])
                    for (d, a0, a1, cg, skip, r), (ps3, pk3, psd, pkd) in zip(info, res):
                        if skip:
                            continue
                        ck = ("C32", si, d)
                        stt(C32[d][:, :, 0:256], C32[d][:, :, 0:256], W0bc[d][:, cg:cg + 1],
                            psd[:, :].rearrange("p (a b) -> p a b", a=2), ALU.mult, ALU.add, [ck, ("W0bc", d), pkd], [ck])
                        stt(C32[d][:, :, 256], C32[d][:, :, 256], W0bc[d][:, cg:cg + 1], ps3[:, 384:386], ALU.mult, ALU.add,
                            [ck, ("W0bc", d), pk3], [ck])
                        cp("scalar", Cb[d][par][:, :, :], C32[d][:, :, :], [ck], [("Cb", si, d, par)])
                    for (d, a0, a1, cg, skip, r), (ps3, pk3, psd, pkd) in zip(info, res):
                        colb = d * 16
                        ts("vector", dd[si * 2 + d][:, 0:1], ps3[:, 256:257], cols[:, cg, colb + 4 + hb:colb + 4 + hb + 1], ALU.max,
                           [pk3, ("cols", d)], [("dd", si, d)])
                        stt(dd[si * 2 + d][:, 0:1], ps3[:, 256:257], -1.0, dd[si * 2 + d][:, 0:1], ALU.mult, ALU.max, [pk3, ("dd", si, d)], [("dd", si, d)])
                        P.add("vector", lambda e, o=dd[si * 2 + d][:, 1:2], i=dd[si * 2 + d][:, 0:1]: e.reciprocal(o, i), [("dd", si, d)], [("dd", si, d)])
                        if cg not in hs_written:
                            hs_written.add(cg)
                            ts("vector", Hsum[:, cg, :], ps3[:, 0:256], dd[si * 2 + d][:, 1:2], ALU.mult, [pk3, ("dd", si, d)], [("Hs", cg)])
                        else:
                            stt(Hsum[:, cg, :], ps3[:, 0:256], dd[si * 2 + d][:, 1:2], Hsum32[:, cg, :], ALU.mult, ALU.add,
                                [pk3, ("dd", si, d), ("Hs", cg)], [("Hs", cg)])

                def seg_pipe(si):
                    nch_ = (segs[si][1] - segs[si][0]) // 128
                    front(si, 0)
                    yield
                    for oi in range(nch_):
                        if oi + 1 < nch_:
                            front(si, oi + 1)
                            yield
                        back(si, oi)
                        yield
                    if not is_sample:
                        for d in range(2):
                            ck = ("C32", si, d)
                            for dc in range(2):
                                dma("sync", D["C_o"][j, si, d, hb, dc * 128:(dc + 1) * 128, :], C32s[si][d][:, dc, 0:256], [ck], [],
                                    semkey=("Cout", si, d), is_out=True)
                                dma("sync", D["n_o"][j, si, d, hb, dc * 128:(dc + 1) * 128].rearrange("(p o) -> p o", o=1),
                                    C32s[si][d][:, dc, 256:257], [ck], [], semkey=("Cout", si, d), is_out=True)

                gens_ = [seg_pipe(si) for si in range(nseg)]
                alive_ = [True] * nseg
                while any(alive_):
                    for gi_ in range(nseg):
                        if alive_[gi_]:
                            try:
                                next(gens_[gi_])
                            except StopIteration:
                                alive_[gi_] = False
            P.phase = "B_epi%d" % int(is_sample)
            arena_reset(keep_base=True, keepR=True)
            og = carve([nblk, 2, 512])
            zsn = carve([nblk, 2, 512])
            hgT = carve([nblk, 2, 512])
            tq = [carve([512]) for _ in range(2)]
            lnb = carve([nblk, 512])
            rstd = carve([nblk, 512])
            sq = carveR([nblk, 2, 512])
            boT = carveR([2, TP])
            woz, wozk = w_pop("Boz")
            it = 0
            for blk in range(nblk):
                c0, c1 = blk * 512, (blk + 1) * 512
                for c in range(2):
                    ps, pk = ps_next(ALLPS)
                    for kc in range(8):
                        mm(ps[:, :], woz[:, kc, c * 128:(c + 1) * 128], hT[:, kc, c0:c1], kc == 0, kc == 7, [wozk, hk(kc, blk)], [pk])
                    act(og[:, blk, c, :], ps[:, :], AF.Tanh, [pk], [("og", blk, c)], scale=0.5)
                    ts("gpsimd", og[:, blk, c, :], og[:, blk, c, :], 0.5, ALU.mult, [("og", blk, c)], [("og", blk, c)], s2=0.5, op1=ALU.add)
                    ps2, pk2 = ps_next(ALLPS)
                    for kc in range(8):
                        mm(ps2[:, :], woz[:, kc, 256 + c * 128:256 + (c + 1) * 128], hT[:, kc, c0:c1], kc == 0, kc == 7,
                           [wozk, hk(kc, blk)], [pk2])
                    t_ = tq[it % 2]
                    tk = ("tq", it % 2)
                    it += 1
                    act(t_[:, :], ps2[:, :], AF.Tanh, [pk2], [tk], scale=0.5)
                    ts("gpsimd", t_[:, :], t_[:, :], 0.5, ALU.mult, [tk], [tk], s2=0.5, op1=ALU.add)
                    nbc = smallc[:, 8 + 2 * hb + c:8 + 2 * hb + c + 1]
                    stt(zsn[:, blk, c, :], t_[:, :], nbc, ps2[:, :], ALU.mult, ALU.mult, [tk, "normbt", pk2], [("zsn", blk, c)])
            wo, wok = w_pop("Bo")
            for blk in range(nblk):
                for c in range(2):
                    ps, pk = ps_next(ALLPS)
                    for i in range(4):
                        qt = blk * 4 + i
                        mm(ps[:, i * 128:(i + 1) * 128], Hsum[:, qt, c * 128:(c + 1) * 128], identR[:, :], True, True,
                           [("Hs", qt), "identR"], [pk])
                    tt("vector", hgT[:, blk, c, :], ps[:, :], og[:, blk, c, :], ALU.mult, [pk, ("og", blk, c)], [("hgT", blk, c)])
            pss = []
            for blk in range(nblk):
                for c in range(2):
                    act(sq[:, blk, c, :], hgT[:, blk, c, :], AF.Square, [("hgT", blk, c)], [("sq", blk, c)])
                ps2, pk2 = ps_next(ALLPS)
                for c in range(2):
                    mm(ps2[:, :], onesR[:, :], sq[:, blk, c, :], c == 0, c == 1, ["onesR", ("sq", blk, c)], [pk2])
                pss.append((ps2, pk2))
            for blk in range(nblk):
                ps2, pk2 = pss[blk]
                act(lnb[:, blk, :], ps2[:, :], AF.Ln, [pk2, "epsb"], [("lnb", blk)], scale=1.0 / 256.0, bias=epsb[:, 0:1])
            for blk in range(nblk):
                act(rstd[:, blk, :], lnb[:, blk, :], AF.Exp, [("lnb", blk)], [("rstd", blk)], scale=-0.5)
            for blk in range(nblk):
                c0, c1 = blk * 512, (blk + 1) * 512
                for c in range(2):
                    tt("vector", hgT[:, blk, c, :], hgT[:, blk, c, :], rstd[:, blk, :], ALU.mult, [("hgT", blk, c), ("rstd", blk)], [("hgT", blk, c)])
                    tt("vector", boT[:, c, c0:c1], hgT[:, blk, c, :], zsn[:, blk, c, :], ALU.mult, [("hgT", blk, c), ("zsn", blk, c)], [("boT", c, blk)])
            out_proj2(wo, wok, boT, lambda fc, blk: ("boT", fc, blk), t0, TP, cond)


        CS = {}

        def c_pre(j, t0, TP, segs, is_sample):
            P.phase = "C_pre%d" % int(is_sample)
            arena_reset()
            nblk = TP // 512
            NK = TP + (256 if is_sample else 0)
            qnT = carveR([3, TP])
            ckvT = carveR([2, NK])
            KR = carve([NK], BF16)
            CS["qnT"], CS["ckvT"], CS["KR"], CS["NK"] = qnT, ckvT, KR, NK
            ar["base"] = ar["off"]
            ar["baseR"] = ar["offR"]
            sq = [carveR([512]) for _ in range(2)]
            lnb = carve([512])
            rstd = carve([512])
            tmb = [carve([512]) for _ in range(2)]
            stg = [carve([512]) for _ in range(2)]
            ta = [carve([512]) for _ in range(2)]
            tb = [carve([512]) for _ in range(2)]
            hk = lambda kc, blk: ("h", kc, blk)
            it = 0
            for (wname, nch, nfeat, ncol0, dst, dkey, is_kv) in (("Cqa", 3, 384.0, 16, qnT, "qnT", False), ("Ckva", 2, 256.0, 24, ckvT, "ckvT", True)):
                w, wk = w_pop(wname)
                for blk in range(nblk):
                    c0, c1 = blk * 512, (blk + 1) * 512
                    pss = []
                    for c in range(nch):
                        ps, pk = ps_next(ALLPS)
                        for kc in range(8):
                            mm(ps[:, :], w[:, kc, c * 128:(c + 1) * 128], hT[:, kc, c0:c1], kc == 0, kc == 7, [wk, hk(kc, blk)], [pk])
                        pss.append((ps, pk))
                    ps_s, pk_s = ps_next(ALLPS)
                    for c in range(nch):
                        act(sq[c % 2][:, :], pss[c][0][:, :], AF.Square, [pss[c][1]], [("sqc", c % 2)])
                        mm(ps_s[:, :], onesR[:, :], sq[c % 2][:, :], c == 0, c == nch - 1, ["onesR", ("sqc", c % 2)], [pk_s])
                    rstd_from_sumsq(ps_s[:, :], pk_s, nfeat, rstd[:, :], "rstd", 512, lnb[:, :], "lnb")
                    for c in range(nch):
                        t_ = tmb[it % 2]
                        tk = ("tmb", it % 2)
                        it += 1
                        tt("vector", t_[:, :], pss[c][0][:, :], rstd[:, :], ALU.mult, [pss[c][1], "rstd"], [tk])
                        nrm = smallc[:, ncol0 + c:ncol0 + c + 1]
                        act(dst[:, c, c0:c1], t_[:, :], AF.Copy, [tk, "qnormt", "kvnormt"], [(dkey, c, blk)], scale=nrm)
                        if is_kv and not is_sample:
                            sg = stg[c % 2]
                            act(sg[:, :], t_[:, :], AF.Copy, [tk, "kvnormt"], [("stgc", c % 2)], scale=nrm)
                            dma("sync", D["ckvT_o"][j, c * 128:(c + 1) * 128, c0:c1], sg[:, :], [("stgc", c % 2)], [],
                                semkey=("stgc", c % 2), is_out=True)
                if is_kv and is_sample:
                    for c in range(2):
                        dma("gpsimd", ckvT[:, c, TP:TP + 256], D["cckvT%d" % j][c * 128:(c + 1) * 128, :], (), [("ckvT", c, "ctx")])
            wkr, wkrk = w_pop("Ckr")
            for blk in range(nblk):
                cols = (blk * 512, (blk + 1) * 512)
                ps, pk = ps_next(ALLPS)
                for kc in range(8):
                    mm(ps[0:96, :], wkr[:, kc, 0:96], hT[:, kc, cols[0]:cols[1]], kc == 0, kc == 7, [wkrk, hk(kc, blk)], [pk])
                if is_sample:
                    psp, pkp = ps_next(ALLPS)
                    for kc in range(8):
                        mm(psp[0:96, :], wkr[:, kc, 96:192], hT[:, kc, cols[0]:cols[1]], kc == 0, kc == 7, [wkrk, hk(kc, blk)], [pkp])
                    rope_evac(ps, pk, psp, pkp, 64, 96, cols, KR[64:96, cols[0]:cols[1]], ("KR", blk), ta[blk % 2], tb[blk % 2],
                              ("ta", blk % 2), ("tb", blk % 2))
                else:
                    cp("scalar", KR[64:96, cols[0]:cols[1]], ps[64:96, :], [pk], [("KR", blk)])
                    cp("vector", stg[0][64:96, :], ps[64:96, :], [pk], [("stgc", 0)])
                    dma("sync", D["krT_o"][j, :, cols[0]:cols[1]], stg[0][64:96, :], [("stgc", 0)], [], semkey=("stgc", 0), is_out=True)
            if is_sample:
                dma("sync", stg[1][64:96, 0:256], D["ckrT%d" % j], (), [("stgc", 1)])
                cp("vector", KR[64:96, TP:TP + 256], stg[1][64:96, 0:256], [("stgc", 1)], [("KR", "ctx")])

        def c_unit(j, g, t0, TP, segs, is_sample, cond):
            arena_reset(keep_base=True, barrier=(g == 0))
            P.phase = "C_proj%d" % int(is_sample)
            qnT, ckvT, KR, NK = CS["qnT"], CS["ckvT"], CS["KR"], CS["NK"]
            nqt = TP // 128
            nblk = TP // 512
            nkt = NK // 128
            QT = [carve([TP], BF16) for _ in range(4)]
            KTh = [carve([NK], BF16) for _ in range(4)]
            Vaug = carve([nkt, 4, 65], BF16)
            zs = carve([nqt, 256], BF16)
            aoT = carveR([2, TP])
            ta = [carve([512]) for _ in range(2)]
            tb = [carve([512]) for _ in range(2)]
            tz = [carve([256]) for _ in range(2)]
            alloc_attn_rot()
            hk = lambda kc, blk: ("h", kc, blk)
            memset("gpsimd", Vaug[:, :, :, 64:65], 1.0, [("V", i) for i in range(nkt)])
            for hl in range(4):
                memset("gpsimd", QT[hl][64:128, :], 0.0, [("QTr", hl, b) for b in range(nblk)] + [("QTz", hl)])
                memset("gpsimd", KTh[hl][64:128, :], 0.0, [("KThr", hl)])
            qn_keys = lambda blk: [("qnT", c, blk) for c in range(3)]
            wq, wqk = w_pop("Cqb")
            it = 0
            for hl in range(4):
                for blk in range(nblk):
                    cols = (blk * 512, (blk + 1) * 512)
                    ps, pk = ps_next(ALLPS)
                    for kc in range(3):
                        mm(ps[0:96, :], wq[:, kc, hl * 192:hl * 192 + 96], qnT[:, kc, cols[0]:cols[1]], kc == 0, kc == 2,
                           [wqk] + qn_keys(blk), [pk])
                    if is_sample:
                        psp, pkp = ps_next(ALLPS)
                        for kc in range(3):
                            mm(psp[0:96, :], wq[:, kc, hl * 192 + 96:hl * 192 + 192], qnT[:, kc, cols[0]:cols[1]], kc == 0, kc == 2,
                               [wqk] + qn_keys(blk), [pkp])
                        cp("scalar", QT[hl][0:64, cols[0]:cols[1]], ps[0:64, :], [pk], [("QT", hl, blk)])
                        rope_evac(ps, pk, psp, pkp, 64, 96, cols, QT[hl][64:96, cols[0]:cols[1]], ("QTr", hl, blk),
                                  ta[it % 2], tb[it % 2], ("ta", it % 2), ("tb", it % 2))
                        it += 1
                    else:
                        cp("scalar", QT[hl][0:64, cols[0]:cols[1]], ps[0:64, :], [pk], [("QT", hl, blk)])
                        cp("vector", QT[hl][64:96, cols[0]:cols[1]], ps[64:96, :], [pk], [("QTr", hl, blk)])
            wkv, wkvk = w_pop("Ckvb")
            ck_keys = lambda k0, k1: [("ckvT", c, b) for c in range(2) for b in
                                      sorted(set(["ctx" if kk >= TP else kk // 512 for kk in (k0, k1 - 1)]))]
            for hl in range(4):
                k0 = 0
                while k0 < NK:
                    k1 = min(NK, k0 + 512)
                    n_ = k1 - k0
                    ps, pk = ps_next(ALLPS)
                    for kc in range(2):
                        mm(ps[0:64, 0:n_], wkv[:, kc, hl * 128:hl * 128 + 64], ckvT[:, kc, k0:k1], kc == 0, kc == 1,
                           [wkvk] + ck_keys(k0, k1), [pk])
                    cp("vector" if (k0 // 512) % 2 else "scalar", KTh[hl][0:64, k0:k1], ps[0:64, 0:n_], [pk], [("KTh", hl)])
                    k0 = k1
                cp("vector", KTh[hl][64:96, :], KR[64:96, :], [("KR", b) for b in range(nblk)] + ([("KR", "ctx")] if is_sample else []),
                   [("KThr", hl)])
            wv3 = wkv.rearrange("p k (h c) -> p k h c", c=128)
            for kt in range(nkt):
                ps, pk = ps_next(ALLPS)
                for kc in range(2):
                    mm(ps[:, 0:256], ckvT[:, kc, kt * 128:(kt + 1) * 128], wv3[:, kc, :, 64:128], kc == 0, kc == 1,
                       [wkvk] + ck_keys(kt * 128, (kt + 1) * 128), [pk])
                cp("scalar", Vaug[:, kt, :, 0:64], ps[:, 0:256].rearrange("p (h c) -> p h c", c=64), [pk], [("V", kt)])
            wz, wzk = w_pop("Cz")
            for qt in range(nqt):
                ps, pk = ps_next(ALLPS)
                for kc in range(8):
                    mm(ps[:, 0:256], hT[:, kc, qt * 128:(qt + 1) * 128], wz[:, kc, 0:256], kc == 0, kc == 7, [hk(kc, qt // 4), wzk], [pk])
                silu_tok(ps[:, 0:256], pk, zs[:, qt, :], ("zs", qt), tz[qt % 2][:, :], ("tz", qt % 2), 256)
            P.phase = "C_attn%d" % int(is_sample)
            for hl in range(4):
                if is_sample:
                    jobs = [(kt * 128, 0, 8, None, kt) for kt in range(nkt)]
                else:
                    jobs = []
                    for s_ in range(2):
                        for jt in (2 * s_, 2 * s_ + 1):
                            jobs.append((jt * 128, 2 * s_, 2 * s_ + 2, None, jt))
                attn_head(QT[hl], lambda q0, q1, hl=hl: [k_ for b in range(q0 // 4, (q1 - 1) // 4 + 1)
                                                          for k_ in (("QT", hl, b), ("QTr", hl, b), ("QTz", hl))],
                          0, 128, KTh[hl], lambda kc0, hl=hl: [("KTh", hl), ("KThr", hl)],
                          jobs, lambda vid, hl=hl: (Vaug[:, vid, hl, :], ("V", vid)), 96.0 ** -0.5, None,
                          zs, lambda qt: ("zs", qt), hl, nqt, (hl % 2) * 2)
            P.phase = "C_out%d" % int(is_sample)
            transpose_to_aoT(zs, lambda qt: ("zs", qt), aoT, lambda c, b: ("aoT", c, b), nqt)
            out_proj("Co", aoT, lambda fc, blk: ("aoT", fc, blk), t0, TP, cond)

        run_layers(BUILD_WHICH[0])
        P.emit()
        LAST_PROG[0] = P
    return nc


LAST_PROG = [None]
BUILD_WHICH = ["ABC"]
FLAGS = {}
NCORES = [8]
DBG_SPECS = {}
DBG_OUT = {}
_CACHE = {}


def kernel(**inputs):
    inp = {k: np.asarray(v) for k, v in inputs.items()}
    sh, per = prep_inputs(inp)
    specs = input_specs()
    key = "full"
    if key not in _CACHE:
        _CACHE[key] = build()
    nc = _CACHE[key]
    in_maps = []
    for i in range(NCORES[0]):
        d = {}
        for name in specs:
            a = per[i][name] if name in per[i] else sh[name]
            a = np.ascontiguousarray(a, dtype=np.float32)
            assert list(a.shape) == specs[name][0], (name, a.shape, specs[name][0])
            d[name] = a
        in_maps.append(d)
    res = run_bass_kernel_spmd(nc, in_maps, core_ids=list(range(NCORES[0])))
    R = res.results
    for name in DBG_SPECS:
        DBG_OUT[name] = [np.asarray(R[i][name]) for i in range(NCORES[0])]
    f32 = np.float32
    y_prompt = np.zeros((16, 256, 1024), f32)
    y_sample = np.zeros((8, 1024, 1024), f32)
    a_k = np.zeros((16, 2, 256, 4, 64), f32)
    a_v = np.zeros((16, 2, 256, 4, 64), f32)
    b_mem = np.zeros((16, 2, 2, 4, 256, 256), f32)
    b_nrm = np.zeros((16, 2, 2, 4, 256), f32)
    b_max = np.zeros((16, 2, 2, 4), f32)
    c_kv = np.zeros((16, 2, 256, 256), f32)
    c_kr = np.zeros((16, 2, 256, 32), f32)
    for i in range(NCORES[0]):
        r = R[i]
        yT = np.asarray(r["yT_o"])
        y_sample[i] = yT[:, 0:1024].T
        for s in range(2):
            b = 2 * i + s
            y_prompt[b] = yT[:, 1024 + 256 * s:1024 + 256 * (s + 1)].T
            for j in range(2):
                a_k[b, j] = np.asarray(r["kaT_o"])[j][:, 256 * s:256 * (s + 1)].T.reshape(256, 4, 64)
                a_v[b, j] = np.asarray(r["va_o"])[j][256 * s:256 * (s + 1), :].reshape(256, 4, 64)
                b_mem[b, j] = np.asarray(r["C_o"])[j, s]
                b_nrm[b, j] = np.asarray(r["n_o"])[j, s]
                b_max[b, j] = np.asarray(r["m_o"])[j, s]
                c_kv[b, j] = np.asarray(r["ckvT_o"])[j][:, 256 * s:256 * (s + 1)].T
                c_kr[b, j] = np.asarray(r["krT_o"])[j][:, 256 * s:256 * (s + 1)].T
    return (y_prompt, y_sample, a_k, a_v, b_mem, b_nrm, b_max, c_kv, c_kr)
```

```python
import contextlib
import numpy as np
import concourse.bass as bass
import concourse.mybir as mybir
from concourse.bass_utils import run_bass_kernel_spmd

F32 = mybir.dt.float32
F32R = mybir.dt.float32r
BF16 = mybir.dt.bfloat16
ALU = mybir.AluOpType
AF = mybir.ActivationFunctionType
AX = mybir.AxisListType

ENGS = ["sync", "scalar", "vector", "gpsimd", "tensor"]


class Op:
    __slots__ = ("eng", "fn", "deps", "dma", "semkey", "mark", "dmacount", "seq", "phase")

    def __init__(self, eng, fn, dma, semkey):
        self.eng = eng
        self.fn = fn
        self.deps = []
        self.dma = dma
        self.semkey = semkey
        self.mark = None
        self.dmacount = None
        self.seq = None


class Prog:
    def __init__(self, nc):
        self.nc = nc
        self.streams = {e: [] for e in ENGS}
        self.last_writer = {}
        self.readers = {}
        self.all_ops = []
        self.dma_counts = {}
        self.pending_barrier = {e: [] for e in ENGS}
        self.out_dmas = []
        self.phase = "init"

    def add(self, eng, fn, reads=(), writes=(), dma=False, semkey=None, is_out=False):
        if dma and semkey is None:
            semkey = writes[0] if writes else reads[0]
        o = Op(eng, fn, dma, semkey)
        o.phase = self.phase
        deps = {}

        def add_dep(p, raw):
            if p is None or p is o:
                return
            if (not raw) and (not p.dma) and (not dma) and p.eng == eng and eng == "tensor":
                return
            deps[id(p)] = p

        for k in reads:
            add_dep(self.last_writer.get(k), True)
            if isinstance(k, tuple) and k[0] == "ps":
                for r in self.readers.get(k, ()):
                    if r.eng != eng:
                        add_dep(r, True)
        for k in writes:
            add_dep(self.last_writer.get(k), False)
            for r in self.readers.get(k, ()):
                add_dep(r, False)
        for p in self.pending_barrier[eng]:
            add_dep(p, True)
        self.pending_barrier[eng] = []
        for k in writes:
            self.last_writer[k] = o
            self.readers[k] = []
        for k in reads:
            self.readers.setdefault(k, []).append(o)
        o.deps = list(deps.values())
        if dma:
            c = self.dma_counts.get(semkey, 0) + 1
            self.dma_counts[semkey] = c
            o.dmacount = c
            if is_out:
                self.out_dmas.append(o)
        o.seq = len(self.all_ops)
        self.all_ops.append(o)
        self.streams[eng].append(o)
        return o

    def barrier(self):
        lasts = []
        for e in ENGS:
            s = self.streams[e]
            if s:
                lasts.append(s[-1])
        dm = {}
        for o in self.all_ops:
            if o.dma:
                dm[o.semkey] = o
        lasts += list(dm.values())
        for e in ENGS:
            self.pending_barrier[e] = list(lasts)

    def emit(self):
        nc = self.nc
        self.barrier()
        self.add("sync", lambda e: e.nop(), reads=(), writes=())
        for o in self.all_ops:
            for p in o.deps:
                if not p.dma:
                    p.mark = True
        counts = {e: 0 for e in ENGS}
        for e in ENGS:
            for o in self.streams[e]:
                if o.mark:
                    counts[e] += 1
                    o.mark = counts[e]
        with contextlib.ExitStack() as st:
            esem = {e: st.enter_context(nc.semaphore("sem_" + e)) for e in ENGS}
            dsem = {}
            for i, k in enumerate(self.dma_counts):
                dsem[k] = st.enter_context(nc.semaphore("dsem%d" % i))
            block = st.enter_context(nc.Block())
            streams = self.streams

            def run(eng_name, e):
                known = {}
                for o in streams[eng_name]:
                    need = {}
                    for p in o.deps:
                        if p.dma:
                            key = ("d", p.semkey)
                            val = 16 * p.dmacount
                            sem = dsem[p.semkey]
                        else:
                            key = ("e", p.eng)
                            val = p.mark
                            sem = esem[p.eng]
                        if known.get(key, 0) >= val:
                            continue
                        if key not in need or need[key][1] < val:
                            need[key] = (sem, val)
                    for key, (sem, val) in need.items():
                        e.wait_ge(sem, val)
                        known[key] = val
                    ins = o.fn(e)
                    if o.dma:
                        ins.then_inc(dsem[o.semkey], 16)
                    elif o.mark:
                        ins.then_inc(esem[eng_name], 1)

            @block.sync
            def _(e):
                run("sync", e)

            @block.scalar
            def _(e):
                run("scalar", e)

            @block.vector
            def _(e):
                run("vector", e)

            @block.gpsimd
            def _(e):
                run("gpsimd", e)

            @block.tensor
            def _(e):
                run("tensor", e)


DM = 1024
DEPTH = 4
NEG = -30000.0
EPS = 1e-6
T_S = 1024
T_P = 512
NTOK = T_S + T_P
AB_OFF = {"qa": 0, "ka": 1024, "va": 1280, "za": 1536, "qb": 2560, "kb": 3584, "vb": 4608, "ob": 5632, "zb": 6656,
          "gb": 7680}
PASSES = [(0, 1024, [(0, 1024)], True), (1024, 512, [(0, 256), (256, 512)], False)]
ARENA_WORDS = 12800
ARENAR_WORDS = 7680


def _rope_perm(n_rot, d_axis):
    half = d_axis // 2
    idx = np.arange(n_rot)
    within = idx % d_axis
    return np.where(within < half, idx + half, idx - half)


def _rope_tables(n_rot, d_axis):
    half = d_axis // 2
    rows = np.repeat(np.arange(1024 // 64, dtype=np.int32), 64).astype(np.float32)
    cols = np.tile(np.arange(64, dtype=np.int32), 1024 // 64).astype(np.float32)
    freqs = np.power(np.float32(10000.0), -np.arange(half, dtype=np.float32) / np.float32(half)).astype(np.float32)
    cos = np.zeros((n_rot, 1024), np.float32)
    sin = np.zeros((n_rot, 1024), np.float32)
    for d in range(n_rot):
        axis = d // d_axis
        within = d % d_axis
        pos = rows if axis == 0 else cols
        ang = (pos * freqs[within % half]).astype(np.float32)
        cos[d] = np.cos(ang)
        s = np.sin(ang)
        sin[d] = -s if within < half else s
    return cos, sin


def _fm(v, nchunk):
    return np.ascontiguousarray(np.asarray(v, np.float32).reshape(nchunk, 128).T)


def prep_inputs(inp):
    f32 = np.float32
    sh = {}
    cA, sA = _rope_tables(64, 32)
    sh["cosA"] = np.concatenate([cA, cA], 0)
    sh["sinA"] = np.concatenate([sA, sA], 0)
    cC, sC = _rope_tables(32, 16)
    cosC = np.ones((128, 1024), f32)
    sinC = np.zeros((128, 1024), f32)
    cosC[64:96] = cC
    sinC[64:96] = sC
    sh["cosC"] = cosC
    sh["sinC"] = sinC
    kk = np.arange(128)[:, None]
    qq = np.arange(128)[None, :]
    mA = np.zeros((128, 384), f32)
    mA[:, 0:128] = np.where(kk <= qq, 0.0, NEG)
    mA[:, 256:384] = np.where(qq <= kk, 0.0, NEG)
    sh["maskA"] = mA
    sh["maskF"] = np.where(kk <= qq, 0.0, NEG).astype(f32)
    sh["maskB"] = np.where(kk >= qq, 0.0, NEG).astype(f32)
    sh["ident"] = np.eye(128, dtype=f32)
    pA = np.zeros((128, 128), f32)
    pidx = np.arange(128)
    pperm = np.where((pidx % 32) < 16, pidx + 16, pidx - 16)
    pA[pperm, pidx] = 1.0
    sh["permA"] = pA
    sel = np.zeros((128, 4, 128), f32)
    for base in (0, 32, 64):
        for h in range(4):
            sel[base + h, h, :] = 1.0
    sh["selrows"] = sel.reshape(128, 512)
    sh["fnormT"] = _fm(inp["final_norm"], 8)
    permA = _rope_perm(64, 32)
    permC = _rope_perm(32, 16)
    for l in range(DEPTH):
        sh["wmod%d" % l] = np.ascontiguousarray(inp["w_mod"][l])
        sh["bmodT%d" % l] = _fm(inp["b_mod"][l], 24)
        sh["normgT%d" % l] = _fm(inp["norm_g"][l], 8)
    for j in range(2):
        W = np.asarray(inp["w_in_ab"][j])
        WA = np.zeros((1024, 4, 704), f32)
        for g in range(4):
            qcols = AB_OFF["qa"] + g * 256 + np.arange(256)
            kcols = AB_OFF["ka"] + g * 64 + np.arange(64)
            WA[:, g, 0:256] = W[:, qcols]
            WA[:, g, 256:320] = W[:, kcols]
            WA[:, g, 320:384] = W[:, kcols]
            WA[:, g, 384:448] = W[:, AB_OFF["va"] + g * 64 + np.arange(64)]
            WA[:, g, 448:704] = W[:, AB_OFF["za"] + g * 256 + np.arange(256)]
        sh["WA%d" % j] = WA
        WB = np.zeros((1024, 4, 1280), f32)
        for h in range(4):
            for i, nm in enumerate(["qb", "kb", "ob", "zb", "vb"]):
                WB[:, h, i * 256:(i + 1) * 256] = W[:, AB_OFF[nm] + h * 256 + np.arange(256)]
        sh["WB%d" % j] = WB
        sh["WG%d" % j] = np.ascontiguousarray(W[:, AB_OFF["gb"]:AB_OFF["gb"] + 16])
        sh["woutab%d" % j] = np.ascontiguousarray(inp["w_out_ab"][j])
        sh["sinkb%d" % j] = np.ascontiguousarray(np.broadcast_to(np.asarray(inp["sink_a"][j], f32)[None, :], (128, 16)))
        cb = np.asarray(inp["conv_b"][j], f32)
        sh["convT%d" % j] = np.ascontiguousarray(cb.reshape(3, 16, 128).transpose(2, 1, 0))
        gbv = np.asarray(inp["gate_bias_b"][j], f32).reshape(4, 4).T
        gb = np.zeros((128, 4), f32)
        for base in (0, 32, 64):
            gb[base:base + 4] = gbv
        sh["gbias%d" % j] = gb
        sh["normbT%d" % j] = _fm(inp["norm_b"][j], 8)
    for j in range(2):
        W = np.asarray(inp["w_in_c"][j])
        sh["winc%d" % j] = np.ascontiguousarray(W)
        wkr = np.zeros((1024, 192), f32)
        wkr[:, 64:96] = W[:, 640:672]
        wkr[:, 160:192] = W[:, 640 + permC]
        sh["wkr%d" % j] = wkr
        Wq = np.asarray(inp["w_qb_c"][j]).reshape(384, 16, 96)
        wqb = np.zeros((384, 16, 192), f32)
        wqb[:, :, 0:96] = Wq
        wqb[:, :, 96:160] = Wq[:, :, 0:64]
        wqb[:, :, 160:192] = Wq[:, :, 64 + permC]
        sh["wqb%d" % j] = wqb
        sh["wkvb%d" % j] = np.ascontiguousarray(inp["w_kvb_c"][j])
        sh["woutc%d" % j] = np.ascontiguousarray(inp["w_out_c"][j])
        sh["qnormT%d" % j] = _fm(inp["q_norm_c"][j], 3)
        sh["kvnormT%d" % j] = _fm(inp["kv_norm_c"][j], 2)
    per = []
    for i in range(8):
        d = {}
        xs = np.asarray(inp["x_sample"][i])
        xp = np.asarray(inp["x_prompt"][2 * i:2 * i + 2]).reshape(512, 1024)
        d["xT"] = np.ascontiguousarray(np.concatenate([xs, xp], 0).T)
        cond = np.stack([np.asarray(inp["c"][i]), np.asarray(inp["c_ctx"])], -1)
        d["condT"] = np.ascontiguousarray(cond.reshape(8, 128, 2).transpose(1, 0, 2))
        for j in range(2):
            ck = np.asarray(inp["cache_a_k"][i, j])
            ckT = ck.transpose(1, 2, 0)
            d["KctxT%d" % j] = np.ascontiguousarray(np.concatenate([ckT, ckT], 1))
            d["Vctx%d" % j] = np.ascontiguousarray(np.asarray(inp["cache_a_v"][i, j]).reshape(256, 256))
            d["C0%d" % j] = np.ascontiguousarray(inp["state_b_mem"][i, j])
            d["n0%d" % j] = np.ascontiguousarray(inp["state_b_norm"][i, j])
            m0v = np.asarray(inp["state_b_max"][i, j]).T
            m0 = np.zeros((128, 2), f32)
            for base in (0, 32, 64):
                m0[base:base + 4] = m0v
            d["m0%d" % j] = m0
            d["cckvT%d" % j] = np.ascontiguousarray(np.asarray(inp["cache_c_kv"][i, j]).T)
            d["ckrT%d" % j] = np.ascontiguousarray(np.asarray(inp["cache_c_krope"][i, j]).T)
        per.append(d)
    return sh, per


IN_SPECS = None


def input_specs():
    s = {}
    s["xT"] = ([1024, NTOK], False)
    s["condT"] = ([128, 8, 2], False)
    for n in ["cosA", "sinA", "cosC", "sinC"]:
        s[n] = ([128, 1024], False)
    s["maskA"] = ([128, 384], False)
    s["maskF"] = ([128, 128], False)
    s["maskB"] = ([128, 128], False)
    s["ident"] = ([128, 128], False)
    s["permA"] = ([128, 128], False)
    s["selrows"] = ([128, 512], False)
    s["fnormT"] = ([128, 8], False)
    for l in range(DEPTH):
        s["wmod%d" % l] = ([1024, 3072], True)
        s["bmodT%d" % l] = ([128, 24], False)
        s["normgT%d" % l] = ([128, 8], False)
    for j in range(2):
        s["WA%d" % j] = ([1024, 4, 704], True)
        s["WB%d" % j] = ([1024, 4, 1280], True)
        s["WG%d" % j] = ([1024, 16], True)
        s["woutab%d" % j] = ([2048, 1024], True)
        s["sinkb%d" % j] = ([128, 16], False)
        s["convT%d" % j] = ([128, 16, 3], False)
        s["gbias%d" % j] = ([128, 4], False)
        s["normbT%d" % j] = ([128, 8], False)
        s["KctxT%d" % j] = ([4, 128, 256], False)
        s["Vctx%d" % j] = ([256, 256], False)
        s["C0%d" % j] = ([2, 4, 256, 256], False)
        s["n0%d" % j] = ([2, 4, 256], False)
        s["m0%d" % j] = ([128, 2], False)
        s["winc%d" % j] = ([1024, 1696], True)
        s["wkr%d" % j] = ([1024, 192], True)
        s["wqb%d" % j] = ([384, 16, 192], True)
        s["wkvb%d" % j] = ([256, 2048], True)
        s["woutc%d" % j] = ([1024, 1024], True)
        s["qnormT%d" % j] = ([128, 3], False)
        s["kvnormT%d" % j] = ([128, 2], False)
        s["cckvT%d" % j] = ([256, 256], True)
        s["ckrT%d" % j] = ([32, 256], False)
    return s


def output_specs():
    return {
        "yT_o": [1024, NTOK],
        "kaT_o": [2, 256, 512],
        "va_o": [2, 512, 256],
        "C_o": [2, 2, 2, 4, 256, 256],
        "n_o": [2, 2, 2, 4, 256],
        "m_o": [2, 2, 2, 4],
        "ckvT_o": [2, 256, 512],
        "krT_o": [2, 32, 512],
    }


def build(nlayers=DEPTH, do_final_norm=True):
    nc = bass.Bass("TRN2", target_bir_lowering=False)
    D = {}
    for name, (shape, r) in input_specs().items():
        D[name] = nc.dram_tensor(name, list(shape), F32R if r else F32, kind="ExternalInput").ap()
    for name, shape in output_specs().items():
        D[name] = nc.dram_tensor(name, list(shape), F32, kind="ExternalOutput").ap()
    for name, shape in DBG_SPECS.items():
        D[name] = nc.dram_tensor(name, list(shape), F32, kind="ExternalOutput").ap()
    P = Prog(nc)
    st = contextlib.ExitStack()
    with st:
        def sb(name, shape, dt=F32):
            return st.enter_context(nc.sbuf_tensor(name, list(shape), dt))

        psb = [st.enter_context(nc.psum_tensor("psb%d" % i, [128, 512], F32)) for i in range(8)]
        yT = sb("yT", [128, 8, NTOK])
        hT = sb("hT", [128, 8, 1024], F32R)
        wr = [sb("wr%d" % i, [128, 4096], F32R) for i in range(2)]
        tabC = sb("tabC", [128, 1024])
        tabS = sb("tabS", [128, 1024])
        arena = sb("arena", [128, ARENA_WORDS])
        arenaR = sb("arenaR", [128, ARENAR_WORDS], F32R)
        identF = sb("identF", [128, 128])
        identR = sb("identR", [128, 128], F32R)
        identB = sb("identB", [128, 128], BF16)
        permB = sb("permB", [128, 128], BF16)
        onesR = sb("onesR", [128, 128], F32R)
        ones4 = sb("ones4", [128, 128])
        selF = sb("selF", [128, 512])
        maskAf = arena[:, 0:384]
        maskAb = sb("maskAb", [128, 384], BF16)
        maskFf = arena[:, 384:640]
        maskFB = sb("maskFB", [128, 256], BF16)
        condt = sb("condt", [128, 8, 2])
        cond_t = sb("cond_t", [128, 8, 2])
        scond = sb("scond", [128, 8, 2], F32R)
        modv_b = [sb("modv%d" % i, [128, 24, 2]) for i in range(2)]
        avec_b = [sb("avec%d" % i, [128, 8, 2]) for i in range(2)]
        bmodt_b = [sb("bmodt%d" % i, [128, 24]) for i in range(2)]
        normgt_b = [sb("normgt%d" % i, [128, 8]) for i in range(2)]
        cur = {"l": 0}
        smallc = sb("smallc", [128, 96])
        sinkt = sb("sinkt", [128, 16])
        convt = sb("convt", [128, 16, 3])
        m0t = sb("m0t", [128, 2])
        epsb = sb("epsb", [128, 1])

        state = {"ps": 0, "uid": 0}

        def uid(prefix):
            state["uid"] += 1
            return (prefix, state["uid"])

        def ps_next(pool=(4, 5, 6, 7)):
            i = pool[state["ps"] % len(pool)]
            state["ps"] += 1
            return psb[i], ("ps", i)

        ALLPS = (0, 1, 2, 3, 4, 5, 6, 7)

        def mm(out, lhsT, rhs, start, stop, reads, writes):
            op_ = P.add("tensor", lambda e, o=out, l=lhsT, r=rhs, s=start, t=stop: e.matmul(o, l, r, start=s, stop=t),
                        reads, writes)
            op_.seq = 2 if lhsT.dtype == F32 else 1

        def act(out, in_, func, reads, writes, scale=1.0, bias=0.0):
            P.add("scalar", lambda e, o=out, i=in_, f=func, s=scale, b=bias: e.activation(out=o, in_=i, func=f, bias=b, scale=s),
                  reads, writes)

        def tt(eng, out, in0, in1, op, reads, writes):
            if eng == "gpsimd" and FLAGS.get("nopool"):
                eng = "vector"
            P.add(eng, lambda e, o=out, a=in0, b=in1, p=op: e.tensor_tensor(out=o, in0=a, in1=b, op=p), reads, writes)

        def ts(eng, out, in0, s1, op0, reads, writes, s2=None, op1=None):
            if eng == "gpsimd" and FLAGS.get("nopool"):
                eng = "vector"
            if op1 is None:
                P.add(eng, lambda e, o=out, a=in0, x=s1, p=op0: e.tensor_scalar(out=o, in0=a, scalar1=x, scalar2=None, op0=p),
                      reads, writes)
            else:
                P.add(eng, lambda e, o=out, a=in0, x=s1, p=op0, y=s2, q=op1: e.tensor_scalar(out=o, in0=a, scalar1=x, scalar2=y, op0=p, op1=q),
                      reads, writes)

        def stt(out, in0, scalar, in1, op0, op1, reads, writes):
            P.add("vector", lambda e, o=out, a=in0, s=scalar, b=in1, p=op0, q=op1: e.scalar_tensor_tensor(out=o, in0=a, scalar=s, in1=b, op0=p, op1=q),
                  reads, writes)

        def cp(eng, out, in_, reads, writes):
            if eng == "gpsimd" and FLAGS.get("nopool"):
                eng = "vector"
            if eng == "scalar":
                act(out, in_, AF.Copy, reads, writes)
            else:
                P.add(eng, lambda e, o=out, i=in_: e.tensor_copy(o, i), reads, writes)

        def memset(eng, ap, val, writes):
            if eng == "gpsimd" and FLAGS.get("nopool"):
                eng = "vector"
            P.add(eng, lambda e, a=ap, v=val: e.memset(a, v), (), writes)

        def dma(eng, out, in_, reads, writes, semkey=None, is_out=False):
            P.add(eng, lambda e, o=out, i=in_: e.dma_start(out=o, in_=i), reads, writes, dma=True, semkey=semkey, is_out=is_out)

        ar = {"off": 0, "offR": 0}

        def carveR(shape):
            n = 1
            for s_ in shape:
                n *= s_
            off = ar["offR"]
            ar["offR"] = off + n
            assert ar["offR"] <= ARENAR_WORDS, ("arenaR overflow", ar["offR"])
            ap = arenaR[:, off:off + n]
            if len(shape) == 2:
                ap = ap.rearrange("p (a b) -> p a b", a=shape[0])
            elif len(shape) == 3:
                ap = ap.rearrange("p (a b c) -> p a b c", a=shape[0], b=shape[1])
            return ap

        def carve(shape, dt=F32):
            n = 1
            for s_ in shape:
                n *= s_
            words = n if dt in (F32, F32R) else (n + 1) // 2
            off = ar["off"]
            ar["off"] = off + words
            assert ar["off"] <= ARENA_WORDS, ("arena overflow", ar["off"])
            ap = arena[:, off:off + words]
            if dt != F32:
                ap = ap.bitcast(dt)
            if len(shape) == 2:
                ap = ap.rearrange("p (a b) -> p a b", a=shape[0])
            elif len(shape) == 3:
                ap = ap.rearrange("p (a b c) -> p a b c", a=shape[0], b=shape[1])
            return ap

        wq_list = []
        wst = {"issued": 0, "next": 0}

        def w_issue_upto(i):
            while wst["issued"] <= min(i, len(wq_list) - 1):
                t = wst["issued"]
                name, dap, k, n = wq_list[t]
                slot = t % 2
                view = wr[slot][:, 0:k * n].rearrange("p (k n) -> p k n", k=k)
                dma("gpsimd", view, dap, (), [("wr", slot)])
                wst["issued"] += 1

        def w_pop(name):
            i = wst["next"]
            assert wq_list[i][0] == name, (wq_list[i][0], name)
            w_issue_upto(i + 1)
            wst["next"] += 1
            _, _, k, n = wq_list[i]
            slot = i % 2
            return wr[slot][:, 0:k * n].rearrange("p (k n) -> p k n", k=k), ("wr", slot)

        def wtile(name, dram2d, k, n):
            wq_list.append((name, dram2d.rearrange("(k p) n -> p k n", p=128), k, n))

        def nxt_mod(l, ip, b):
            if ip == 1 and l + 1 < nlayers:
                wtile("mod%d_%d" % (l + 1, b), D["wmod%d" % (l + 1)][:, b * 512:(b + 1) * 512], 8, 512)

        def layer_tiles(l, ip):
            j = l // 2
            if ip == 0 and l == 0:
                for b in range(6):
                    wtile("mod%d_%d" % (l, b), D["wmod%d" % l][:, b * 512:(b + 1) * 512], 8, 512)
            if l % 2 == 0:
                for g in range(4):
                    wtile("Aq", D["WA%d" % j][:, g, 0:256], 8, 256)
                    wtile("Ak", D["WA%d" % j][:, g, 256:384], 8, 128)
                    wtile("Avz", D["WA%d" % j][:, g, 384:704], 8, 320)
                    wtile("Ao", D["woutab%d" % j][g * 256:(g + 1) * 256, :], 2, 1024)
                    nxt_mod(l, ip, g)
                wtile("G", D["WG%d" % j][:, :], 8, 16)
                for h in range(4):
                    wtile("Bqk", D["WB%d" % j][:, h, 0:512], 8, 512)
                    wtile("Bv", D["WB%d" % j][:, h, 1024:1280], 8, 256)
                    wtile("Boz", D["WB%d" % j][:, h, 512:1024], 8, 512)
                    wtile("Bo", D["woutab%d" % j][1024 + h * 256:1024 + (h + 1) * 256, :], 2, 1024)
                    if h < 2:
                        nxt_mod(l, ip, 4 + h)
            else:
                wtile("Cqa", D["winc%d" % j][:, 0:384], 8, 384)
                wtile("Ckva", D["winc%d" % j][:, 384:640], 8, 256)
                wtile("Ckr", D["wkr%d" % j][:, :], 8, 192)
                nxt_mod(l, ip, 0)
                nxt_mod(l, ip, 1)
                for g in range(4):
                    wtile("Cqb", D["wqb%d" % j][:, g * 4:(g + 1) * 4, :].rearrange("a h c -> a (h c)"), 3, 768)
                    wtile("Ckvb", D["wkvb%d" % j][:, g * 512:(g + 1) * 512], 2, 512)
                    wtile("Cz", D["winc%d" % j][:, 672 + g * 256:672 + (g + 1) * 256], 8, 256)
                    wtile("Co", D["woutc%d" % j][g * 256:(g + 1) * 256, :], 2, 1024)
                    nxt_mod(l, ip, 2 + g)

        for l in range(nlayers):
            for ip in range(2):
                layer_tiles(l, ip)

        dma("sync", identF[:], D["ident"], (), ["identF"])
        cp("vector", identB[:], identF[:], ["identF"], ["identB"])
        dma("sync", arena[:, 640:768], D["permA"], (), ["permAf"])
        cp("vector", permB[:], arena[:, 640:768], ["permAf"], ["permB"])
        cp("vector", identR[:], identF[:], ["identF"], ["identR"])
        memset("vector", ones4[:], 1.0, ["ones4"])
        cp("vector", onesR[:], ones4[:, :], ["ones4"], ["onesR"])
        memset("vector", epsb[:], EPS, ["epsb"])
        dma("sync", selF[:], D["selrows"], (), ["selF"])
        dma("sync", maskAf[:], D["maskA"], (), ["maskAf"])
        cp("vector", maskAb[:], maskAf[:], ["maskAf"], ["maskAb"])
        dma("sync", maskFf[:, 0:128], D["maskF"], (), ["maskFf"])
        dma("sync", maskFf[:, 128:256], D["maskB"], (), ["maskFf"])
        cp("vector", maskFB[:], maskFf[:], ["maskFf"], ["maskFB"])
        for kc in range(8):
            dma("sync", yT[:, kc, :], D["xT"][kc * 128:(kc + 1) * 128, :], (), [("y", kc, b) for b in range(3)],
                semkey=("yload", kc))
        dma("sync", condt[:], D["condT"], (), ["condt"])
        act(cond_t[:], condt[:], AF.Tanh, ["condt"], ["cond_t"], scale=0.5)
        ts("vector", cond_t[:], cond_t[:], 0.5, ALU.mult, ["cond_t"], ["cond_t"], s2=0.5, op1=ALU.add)
        tt("vector", scond[:], cond_t[:], condt[:], ALU.mult, ["cond_t", "condt"], ["scond"])

        def ykey(kc, t0, blk):
            return ("y", kc, (t0 + blk * 512) // 512)

        def rstd_from_sumsq(ps_ap, pskey, n, out_ap, outkey, ncols, tmp_ap, tmpkey):
            act(tmp_ap, ps_ap, AF.Ln, [pskey, "epsb"], [tmpkey], scale=1.0 / n, bias=epsb[:, 0:1])
            act(out_ap, tmp_ap, AF.Exp, [tmpkey], [outkey], scale=-0.5)

        def y_update(ps_ap, pskey, oc, t0, blk, cond):
            k = ykey(oc, t0, blk)
            ysl = yT[:, oc, t0 + blk * 512:t0 + (blk + 1) * 512]
            mv = modv_b[cur["l"] % 2]
            stt(ysl, ps_ap, mv[:, 16 + oc, cond:cond + 1], ysl, ALU.mult, ALU.add, [pskey, k, ("modv", cur["l"] % 2)], [k])

        def out_proj(wname, aoT, aokey_fn, t0, TP, cond):
            wo, wkey = w_pop(wname)
            out_proj2(wo, wkey, aoT, aokey_fn, t0, TP, cond)

        def out_proj2(wo, wkey, aoT, aokey_fn, t0, TP, cond):
            for blk in range(TP // 512):
                for oc in range(8):
                    ps, pk = ps_next(ALLPS)
                    for fc in range(2):
                        mm(ps[:, :], wo[:, fc, oc * 128:(oc + 1) * 128], aoT[:, fc, blk * 512:(blk + 1) * 512],
                           fc == 0, fc == 1, [wkey, aokey_fn(fc, blk)], [pk])
                    y_update(ps[:, :], pk, oc, t0, blk, cond)

        def transpose_to_aoT(zs, zskey_fn, aoT, aokey_fn, nqt):
            for c in range(2):
                for b4 in range(nqt // 4):
                    ps, pk = ps_next(ALLPS)
                    for i in range(4):
                        qt = b4 * 4 + i
                        mm(ps[:, i * 128:(i + 1) * 128], zs[:, qt, c * 128:(c + 1) * 128], identB[:, :], True, True,
                           [zskey_fn(qt), "identB"], [pk])
                    cp("scalar", aoT[:, c, b4 * 512:(b4 + 1) * 512], ps[:, :], [pk], [aokey_fn(c, b4)])

        def attn_head(QT, qkeys_fn, qbase, qrows, KT, kkeys_fn, jobs, vfn, scale, sink_ap, zs, zskey_fn, hl, nqt, obank0):
            first = {}
            last = {}
            for ji, (kc0, lo, hi, mc, vid) in enumerate(jobs):
                for qt in range(lo, hi):
                    first.setdefault(qt, ji)
                    last[qt] = ji
            nb = (nqt + 3) // 4
            lastpv = {}
            for ji, (kc0, lo, hi, mc, vid) in enumerate(jobs):
                for qt in range(lo, hi):
                    lastpv[qt // 4] = (ji, qt)
            obanks = [(psb[obank0 + b], ("ps", obank0 + b)) for b in range(nb)]
            firstpv = {}
            for ji, (kc0, lo, hi, mc, vid) in enumerate(jobs):
                for qt in range(lo, hi):
                    firstpv.setdefault(qt // 4, (ji, qt))
            pieces = []
            for ji, (kc0, lo, hi, mc, vid) in enumerate(jobs):
                q0 = lo
                while q0 < hi:
                    q1 = min(hi, q0 + 4)
                    pieces.append((ji, kc0, lo, mc, vid, q0, q1))
                    q0 = q1

            def emit_score(pc):
                ji, kc0, lo, mc, vid, q0, q1 = pc
                ncol = (q1 - q0) * 128
                ps, pk = ps_next()
                mm(ps[:, 0:ncol], KT[qbase:qbase + qrows, kc0:kc0 + 128], QT[qbase:qbase + qrows, q0 * 128:q1 * 128],
                   True, mc is None, kkeys_fn(kc0) + qkeys_fn(q0, q1), [pk])
                if mc is not None:
                    m0_ = mc + (q0 - lo) * 128
                    mm(ps[:, 0:ncol], identB[:, :], maskAb[:, m0_:m0_ + ncol], False, True, ["identB", "maskAb"], [pk])
                eb, ek = e_next()
                act(eb[:, 0:ncol], ps[:, 0:ncol], AF.Exp, [pk], [ek], scale=scale)
                return eb, ek

            def emit_pv(pc, eb, ek):
                ji, kc0, lo, mc, vid, q0, q1 = pc
                vap, vkey = vfn(vid)
                for qt in range(q0, q1):
                    ob, ok = obanks[qt // 4]
                    c0 = (qt % 4) * 128
                    P.add("tensor", lambda e, o=ob[:, c0:c0 + 65], l=eb[:, (qt - q0) * 128:(qt - q0 + 1) * 128], r=vap,
                          s_=(firstpv[qt // 4] == (ji, qt)), t_=(lastpv[qt // 4] == (ji, qt)):
                          e.matmul(o, l, r, start=s_, stop=t_, skip_group_check=True), [ek, vkey], [ok])

            LA = 2
            q_ = [emit_score(pieces[i]) for i in range(min(LA, len(pieces)))]
            for i in range(len(pieces)):
                if i + LA < len(pieces):
                    q_.append(emit_score(pieces[i + LA]))
                eb_, ek_ = q_.pop(0)
                emit_pv(pieces[i], eb_, ek_)
            for b in range(nb):
                ob, ok = obanks[b]
                n4 = min(4, nqt - b * 4)
                ov = ob[:, 0:n4 * 128].rearrange("p (a c) -> p a c", a=n4)
                dsum, dk = small_next()
                if sink_ap is not None:
                    ts("vector", dsum[:, 0:n4], ov[:, :, 64], sink_ap, ALU.add, [ok, "sinkt"], [dk])
                else:
                    cp("vector", dsum[:, 0:n4], ov[:, :, 64], [ok], [dk])
                P.add("vector", lambda e, o=dsum[:, 4:4 + n4], i=dsum[:, 0:n4]: e.reciprocal(o, i), [dk], [dk])
                at, atk = atmp_next()
                tt("vector", at[:, 0:n4, :], ov[:, :, 0:64], dsum[:, 4:4 + n4].to_broadcast([128, n4, 64]), ALU.mult,
                   [ok, dk], [atk])
                zsl = zs[:, b * 4:b * 4 + n4, hl * 64:(hl + 1) * 64]
                zk = [zskey_fn(qt) for qt in range(b * 4, b * 4 + n4)]
                tt("gpsimd", zsl, zsl, at[:, 0:n4, :], ALU.mult, zk + [atk], zk)

        rot = {}

        def e_next():
            r = rot["E"]
            i = r["i"] % len(r["bufs"])
            r["i"] += 1
            return r["bufs"][i], ("E", i)

        def small_next():
            r = rot["S"]
            i = r["i"] % len(r["bufs"])
            r["i"] += 1
            return r["bufs"][i], ("dsum", i)

        def atmp_next():
            r = rot["A"]
            i = r["i"] % len(r["bufs"])
            r["i"] += 1
            return r["bufs"][i], ("atmp", i)

        def alloc_attn_rot():
            rot["E"] = {"i": 0, "bufs": [carve([512], BF16) for _ in range(4)]}
            rot["S"] = {"i": 0, "bufs": [carve([8]) for _ in range(2)]}
            rot["A"] = {"i": 0, "bufs": [carve([4, 64], BF16) for _ in range(2)]}

        def silu_tok(ps_ap, pskey, out_ap, outkey, tmp_ap, tmpkey, ncols):
            act(tmp_ap, ps_ap, AF.Tanh, [pskey], [tmpkey], scale=0.5)
            ts("gpsimd", tmp_ap, tmp_ap, 0.5, ALU.mult, [tmpkey], [tmpkey], s2=0.5, op1=ALU.add)
            tt("vector", out_ap, tmp_ap, ps_ap, ALU.mult, [tmpkey, pskey], [outkey])

        def rope_evac(psx, pkx, psp, pkp, r0, r1, cols, out_ap, outkey, tmpa, tmpb, tka, tkb):
            tt("vector", tmpa[r0:r1, :], psp[r0:r1, :], tabS[r0:r1, cols[0]:cols[1]], ALU.mult, [pkp, "tabS"], [tka])
            tt("vector", tmpb[r0:r1, :], psx[r0:r1, :], tabC[r0:r1, cols[0]:cols[1]], ALU.mult, [pkx, "tabC"], [tkb])
            tt("gpsimd", out_ap, tmpa[r0:r1, :], tmpb[r0:r1, :], ALU.add, [tka, tkb], [outkey])

        def dbg(name, ap, reads):
            if name in DBG_SPECS:
                P.add("sync", lambda e, o=D[name], i=ap: e.dma_start(out=o, in_=i), reads, [], dma=True, semkey=("dbg", name), is_out=True)

        def arena_reset(keep_base=False, keepR=False, barrier=True):
            if barrier:
                P.barrier()
            ar["off"] = ar.get("base", 0) if keep_base else 0
            if not keep_base:
                ar["base"] = 0
                ar["baseR"] = 0
            if not keepR:
                ar["offR"] = ar.get("baseR", 0) if keep_base else 0

        def mod_load_small(l):
            i = l % 2
            dma("sync", bmodt_b[i][:], D["bmodT%d" % l], (), [("bmodt", i)])
            dma("sync", normgt_b[i][:], D["normgT%d" % l], (), [("normgt", i)])

        def mod_tile(l, b):
            i = l % 2
            modrow = arena[:, ARENA_WORDS - 512:ARENA_WORDS]
            w, wk = w_pop("mod%d_%d" % (l, b))
            psr, pkr = ps_next(ALLPS)
            for kc in range(8):
                mm(psr[0:2, :], scond[:, kc, :], w[:, kc, :], kc == 0, kc == 7, [wk, "scond"], [pkr])
            cp("scalar", modrow[0:2, :], psr[0:2, :], [pkr], ["modrow"])
            ps, pk = ps_next(ALLPS)
            for oc4 in range(4):
                mm(ps[:, 2 * oc4:2 * oc4 + 2], modrow[0:2, oc4 * 128:(oc4 + 1) * 128], identF[0:2, 0:2], True, True,
                   ["modrow", "identF"], [pk])
            psv = ps[:, 0:8].rearrange("p (c j) -> p c j", j=2)
            for jj in range(2):
                tt("vector", modv_b[i][:, b * 4:(b + 1) * 4, jj], psv[:, :, jj], bmodt_b[i][:, b * 4:(b + 1) * 4], ALU.add,
                   [pk, ("bmodt", i)], [("modv", i)])

        def mod_finish(l):
            i = l % 2
            for jj in range(2):
                stt(avec_b[i][:, :, jj], modv_b[i][:, 8:16, jj], 1.0, normgt_b[i][:, :], ALU.add, ALU.mult,
                    [("modv", i), ("normgt", i)], [("avec", i)])

        def layer_mod(l):
            P.phase = "mod"
            mod_load_small(l)
            for b in range(6):
                mod_tile(l, b)
            mod_finish(l)

        def norm_blocks(t0, nblk, scale_fn, bias_fn, out_fn, extra_reads):
            sqb = [carveR([512]) for _ in range(2)]
            lnb = carve([512])
            rstd = carve([512])
            tmb = [carve([512]) for _ in range(2)]
            for blk in range(nblk):
                c0 = t0 + blk * 512
                ps, pk = ps_next(ALLPS)
                for kc in range(8):
                    act(sqb[kc % 2][:, :], yT[:, kc, c0:c0 + 512], AF.Square, [ykey(kc, t0, blk)], [("sqb", kc % 2)])
                    mm(ps[:, :], onesR[:, :], sqb[kc % 2][:, :], kc == 0, kc == 7, ["onesR", ("sqb", kc % 2)], [pk])
                rstd_from_sumsq(ps[:, :], pk, 1024.0, rstd[:, :], "rstd", 512, lnb[:, :], "lnb")
                for kc in range(8):
                    tt("vector", tmb[kc % 2][:, :], yT[:, kc, c0:c0 + 512], rstd[:, :], ALU.mult,
                       [ykey(kc, t0, blk), "rstd"], [("tmb", kc % 2)])
                    oap, okey = out_fn(kc, blk)
                    P.add("scalar", lambda e, o=oap, i=tmb[kc % 2][:, :], s=scale_fn(kc), b=bias_fn(kc):
                          e.activation(out=o, in_=i, func=AF.Identity, bias=b, scale=s),
                          [("tmb", kc % 2)] + extra_reads, [okey])

        def layer_h(t0, TP, cond):
            P.phase = "h"
            arena_reset()
            i_ = cur["l"] % 2
            norm_blocks(t0, TP // 512,
                        lambda kc: avec_b[i_][:, kc, cond:cond + 1], lambda kc: modv_b[i_][:, kc, cond:cond + 1],
                        lambda kc, blk: (hT[:, kc, blk * 512:(blk + 1) * 512], ("h", kc, blk)), [("avec", i_), ("modv", i_)])

        def final_out():
            P.phase = "final"
            arena_reset()
            fn = carve([8])
            dma("sync", fn[:, :], D["fnormT"], (), ["fnorm"])
            stg = [carve([512]) for _ in range(3)]
            cnt = {"i": 0}

            def out_fn(kc, blk):
                i = cnt["i"] % 3
                cnt["i"] += 1
                cnt["last"] = (i, kc, blk)
                return stg[i][:, :], ("stg", i)
            if do_final_norm:
                sqb = [carveR([512]) for _ in range(2)]
                lnb = carve([512])
                rstd = carve([512])
                tmb = [carve([512]) for _ in range(2)]
                for blk in range(3):
                    c0 = blk * 512
                    ps, pk = ps_next(ALLPS)
                    for kc in range(8):
                        act(sqb[kc % 2][:, :], yT[:, kc, c0:c0 + 512], AF.Square, [("y", kc, blk)], [("sqb", kc % 2)])
                        mm(ps[:, :], onesR[:, :], sqb[kc % 2][:, :], kc == 0, kc == 7, ["onesR", ("sqb", kc % 2)], [pk])
                    rstd_from_sumsq(ps[:, :], pk, 1024.0, rstd[:, :], "rstd", 512, lnb[:, :], "lnb")
                    for kc in range(8):
                        tt("vector", tmb[kc % 2][:, :], yT[:, kc, c0:c0 + 512], rstd[:, :], ALU.mult,
                           [("y", kc, blk), "rstd"], [("tmb", kc % 2)])
                        oap, okey = out_fn(kc, blk)
                        act(oap, tmb[kc % 2][:, :], AF.Copy, [("tmb", kc % 2), "fnorm"], [okey], scale=fn[:, kc:kc + 1])
                        dma("sync", D["yT_o"][kc * 128:(kc + 1) * 128, c0:c0 + 512], oap, [okey], [], semkey=okey, is_out=True)
            else:
                for kc in range(8):
                    dma("sync", D["yT_o"][kc * 128:(kc + 1) * 128, :], yT[:, kc, :], [("y", kc, b) for b in range(3)], [],
                        semkey=("yout", kc), is_out=True)

        def a_unit(j, g, t0, TP, segs, is_sample, cond):
            arena_reset(barrier=(g == 0))
            P.phase = "A_proj%d" % int(is_sample)
            nqt = TP // 128
            nblk = TP // 512
            NK = TP + (256 if is_sample else 0)
            nvt = nqt + (2 if is_sample else 0)
            qT = carve([2, TP], BF16)
            KT2 = [carve([NK], BF16) for _ in range(2)]
            Vaug = carve([nvt, 65], BF16)
            zs = carve([nqt, 256], BF16)
            aoT = carveR([2, TP])
            ta = [carve([512]) for _ in range(2)]
            tb = [carve([512]) for _ in range(2)]
            tz = [carve([256]) for _ in range(2)]
            stg = carve([512])
            stv = [carve([64]) for _ in range(2)]
            xbr = [carve([512], BF16) for _ in range(2)]
            alloc_attn_rot()
            vkeys = [("V", i) for i in range(nvt)]
            memset("gpsimd", Vaug[:, :, 64:65], 1.0, vkeys)
            allkt = [("KT", b) for b in range(nblk)] + ([("KT", "ctx")] if is_sample else [])
            memset("gpsimd", KT2[0][64:128, :], 0.0, allkt + ["KTz"])
            memset("gpsimd", KT2[1][0:64, :], 0.0, allkt + ["KTz"])
            hk = lambda kc, blk: ("h", kc, blk)
            wq, wk = w_pop("Aq")
            tiles = [("q", c, blk) for c in range(2) for blk in range(nblk)] + [("k", 0, blk) for blk in range(nblk)]
            st_ = {}
            wkh = {}

            def proj_x(i):
                kind, c, blk = tiles[i]
                cols = (blk * 512, (blk + 1) * 512)
                if kind == "k" and "w" not in wkh:
                    wkh["w"] = w_pop("Ak")
                psx, pkx = ps_next(ALLPS)
                for kc in range(8):
                    if kind == "q":
                        lhs, wkey_ = wq[:, kc, c * 128:(c + 1) * 128], wk
                    else:
                        lhs, wkey_ = wkh["w"][0][:, kc, 0:128], wkh["w"][1]
                    mm(psx[:, :], lhs, hT[:, kc, cols[0]:cols[1]], kc == 0, kc == 7, [wkey_, hk(kc, blk)], [pkx])
                if is_sample:
                    xb_, xk_ = xbr[i % 2], ("xb", i % 2)
                    cp("scalar", xb_[:, :], psx[:, :], [pkx], [xk_])
                st_[i] = (psx, pkx)

            def proj_y(i):
                kind, c, blk = tiles[i]
                cols = (blk * 512, (blk + 1) * 512)
                psx, pkx = st_.pop(i)
                if is_sample:
                    xb_, xk_ = xbr[i % 2], ("xb", i % 2)
                    psp, pkp = ps_next(ALLPS)
                    mm(psp[:, :], permB[:, :], xb_[:, :], True, True, ["permB", xk_], [pkp])
                    tA, tB, kA, kB = ta[i % 2], tb[i % 2], ("ta", i % 2), ("tb", i % 2)
                    tt("vector", tA[:, :], psp[:, :], tabS[:, cols[0]:cols[1]], ALU.mult, [pkp, "tabS"], [kA])
                    tt("vector", tB[:, :], psx[:, :], tabC[:, cols[0]:cols[1]], ALU.mult, [pkx, "tabC"], [kB])
                    if kind == "q":
                        tt("gpsimd", qT[:, c, cols[0]:cols[1]], tA[:, :], tB[:, :], ALU.add, [kA, kB], [("qT", c, blk)])
                    else:
                        tt("gpsimd", KT2[0][0:64, cols[0]:cols[1]], tA[0:64, :], tB[0:64, :], ALU.add, [kA, kB], [("KT", blk)])
                        tt("gpsimd", KT2[1][64:128, cols[0]:cols[1]], tA[64:128, :], tB[64:128, :], ALU.add, [kA, kB], [("KT", blk)])
                else:
                    if kind == "q":
                        cp("scalar", qT[:, c, cols[0]:cols[1]], psx[:, :], [pkx], [("qT", c, blk)])
                    else:
                        cp("scalar", KT2[0][0:64, cols[0]:cols[1]], psx[0:64, :], [pkx], [("KT", blk)])
                        cp("scalar", KT2[1][64:128, cols[0]:cols[1]], psx[64:128, :], [pkx], [("KT", blk)])
                        if not FLAGS.get("noka"):
                            cp("vector", stg[0:64, :], psx[0:64, :], [pkx], ["stg"])
                            dma("sync", D["kaT_o"][j, g * 64:(g + 1) * 64, :], stg[0:64, :], ["stg"], [], semkey=("kaout", g), is_out=True)

            proj_x(0)
            for i in range(len(tiles)):
                if i + 1 < len(tiles):
                    proj_x(i + 1)
                proj_y(i)
            if is_sample and not FLAGS.get("noctx"):
                dma("sync", stg[:, 0:256], D["KctxT%d" % j][g], (), ["stg"])
                cp("vector", KT2[0][0:64, TP:TP + 256], stg[0:64, 0:256], ["stg"], [("KT", "ctx")])
                cp("vector", KT2[1][64:128, TP:TP + 256], stg[64:128, 0:256], ["stg"], [("KT", "ctx")])
            if FLAGS.get("a_stop", 9) <= 2:
                for nm in ("Avz", "Ao"):
                    w_pop(nm)
                return
            wv, wvk = w_pop("Avz")
            for qt in range(nqt):
                ps, pk = ps_next(ALLPS)
                for kc in range(8):
                    mm(ps[:, 0:320], hT[:, kc, qt * 128:(qt + 1) * 128], wv[:, kc, 0:320], kc == 0, kc == 7,
                       [hk(kc, qt // 4), wvk], [pk])
                cp("scalar", Vaug[:, qt, 0:64], ps[:, 0:64], [pk], [("V", qt)])
                silu_tok(ps[:, 64:320], pk, zs[:, qt, :], ("zs", qt), tz[qt % 2][:, :], ("tz", qt % 2), 256)
                if not is_sample:
                    cp("vector", stv[qt % 2][:, :], ps[:, 0:64], [pk], [("stv", qt % 2)])
                    dma("sync", D["va_o"][j, qt * 128:(qt + 1) * 128, g * 64:(g + 1) * 64], stv[qt % 2][:, :],
                        [("stv", qt % 2)], [], semkey=("stv", qt % 2), is_out=True)
            if is_sample:
                for c in range(2):
                    dma("sync", stv[c][:, :], D["Vctx%d" % j][c * 128:(c + 1) * 128, g * 64:(g + 1) * 64], (), [("stv", c)])
                    cp("vector", Vaug[:, nqt + c, 0:64], stv[c][:, :], [("stv", c)], [("V", nqt + c)])
            if FLAGS.get("a_stop", 9) <= 3:
                w_pop("Ao")
                return
            P.phase = "A_attn%d" % int(is_sample)
            for hl in range(4):
                p = hl % 2
                c = hl // 2
                h = g * 4 + hl
                if is_sample:
                    jobs = [(TP, 0, 8, None, nqt), (TP + 128, 0, 8, None, nqt + 1)]
                    for jt in range(8):
                        lo = max(0, jt - 1)
                        hi = min(8, jt + 2)
                        jobs.append((jt * 128, lo, hi, (lo - (jt - 1)) * 128, jt))
                else:
                    jobs = []
                    for s_ in range(2):
                        for jt in (2 * s_, 2 * s_ + 1):
                            jobs.append((jt * 128, 2 * s_, 2 * s_ + 2, None, jt))
                attn_head(qT[:, c, :], lambda q0, q1, c=c: [("qT", c, b) for b in range(q0 // 4, (q1 - 1) // 4 + 1)],
                          0, 128, KT2[p], lambda kc0: ["KTz"] + ([("KT", "ctx")] if kc0 >= TP else [("KT", kc0 // 512)]),
                          jobs, lambda vid: (Vaug[:, vid, :], ("V", vid)), 0.125, sinkt[:, h:h + 1],
                          zs, lambda qt: ("zs", qt), hl, nqt, (hl % 2) * 2)
            if FLAGS.get("a_stop", 9) <= 4:
                w_pop("Ao")
                return
            P.phase = "A_out%d" % int(is_sample)
            transpose_to_aoT(zs, lambda qt: ("zs", qt), aoT, lambda c, b: ("aoT", c, b), nqt)
            if FLAGS.get("a_stop", 9) <= 5:
                w_pop("Ao")
                return
            out_proj("Ao", aoT, lambda fc, blk: ("aoT", fc, blk), t0, TP, cond)

        def run_layers(which):
            for l in range(nlayers):
                j = l // 2
                even = (l % 2 == 0)
                cur["l"] = l
                if l == 0:
                    layer_mod(l)
                else:
                    mod_finish(l)
                if even:
                    dma("sync", tabC[:], D["cosA"], (), ["tabC"])
                    dma("sync", tabS[:], D["sinA"], (), ["tabS"])
                    dma("sync", sinkt[:], D["sinkb%d" % j], (), ["sinkt_raw"])
                    act(sinkt[:], sinkt[:], AF.Exp, ["sinkt_raw"], ["sinkt"])
                    dma("sync", convt[:], D["convT%d" % j], (), ["convt"])
                    dma("sync", smallc[:, 0:4], D["gbias%d" % j], (), ["gbias"])
                    dma("sync", smallc[:, 8:16], D["normbT%d" % j], (), ["normbt"])
                    dma("sync", m0t[:], D["m0%d" % j], (), ["m0t"])
                else:
                    dma("sync", tabC[:], D["cosC"], (), ["tabC"])
                    dma("sync", tabS[:], D["sinC"], (), ["tabS"])
                    dma("sync", smallc[:, 16:19], D["qnormT%d" % j], (), ["qnormt"])
                    dma("sync", smallc[:, 24:26], D["kvnormT%d" % j], (), ["kvnormt"])
                for ip, (t0, TP, segs, is_sample) in enumerate(PASSES):
                    cond = 0 if is_sample else 1
                    pre = (ip == 1 and l + 1 < nlayers)
                    if pre:
                        mod_load_small(l + 1)

                    def premod(b):
                        if pre:
                            ph = P.phase
                            P.phase = "mod"
                            mod_tile(l + 1, b)
                            P.phase = ph
                    layer_h(t0, TP, cond)
                    if l == 0:
                        dbg("hT%d" % ip, hT[:, :, 0:TP].bitcast(F32), [("h", kc, b) for kc in range(8) for b in range(TP // 512)])
                    if even:
                        for g in range(4):
                            if "A" in which:
                                a_unit(j, g, t0, TP, segs, is_sample, cond)
                            else:
                                for nm in ("Aq", "Ak", "Avz", "Ao"):
                                    w_pop(nm)
                            premod(g)
                        if "B" in which:
                            gates_stage(j, t0, TP, segs, is_sample)
                            for hb in range(4):
                                b_unit(j, hb, t0, TP, segs, is_sample, cond)
                                if hb < 2:
                                    premod(4 + hb)
                        else:
                            w_pop("G")
                            for hb in range(4):
                                for nm in ("Bqk", "Bv", "Boz", "Bo"):
                                    w_pop(nm)
                                if hb < 2:
                                    premod(4 + hb)
                    else:
                        if "C" in which:
                            c_pre(j, t0, TP, segs, is_sample)
                            premod(0)
                            premod(1)
                            for g in range(4):
                                c_unit(j, g, t0, TP, segs, is_sample, cond)
                                premod(2 + g)
                        else:
                            for nm in ("Cqa", "Ckva", "Ckr"):
                                w_pop(nm)
                            premod(0)
                            premod(1)
                            for g in range(4):
                                for nm in ("Cqb", "Ckvb", "Cz", "Co"):
                                    w_pop(nm)
                                premod(2 + g)
            final_out()


        G = {}

        def gates_stage(j, t0, TP, segs, is_sample):
            P.phase = "gates%d" % int(is_sample)
            arena_reset()
            nchtot = TP // 128
            ranges = [carve([TP]) for _ in range(1)]
            wib = carve([TP], BF16)

            def garr(idx, d_):
                return ranges[idx][d_ * 32:d_ * 32 + 4, :], d_ * 32, ("ga", idx, d_)
            cols = carve([nchtot, 32])
            w0r = carve([nchtot])
            G["cols"] = cols
            G["w0r"] = w0r
            G["U"] = [garr(0, 0), garr(0, 1)]
            G["WIb"] = [(wib[d_ * 32:d_ * 32 + 4, :], d_ * 32, ("wib", d_)) for d_ in range(2)]
            ar["base"] = ar["off"]
            ranges += [carve([TP]) for _ in range(5)]
            G["G"] = [garr(2, 0), garr(2, 1)]
            ts("vector", smallc[:, 4:8], smallc[:, 0:4], -1.0, ALU.mult, ["gbias"], ["ngbias"])
            wg, wgk = w_pop("G")
            hk = lambda kc, blk: ("h", kc, blk)
            def gdir(d):
                T_li, b_li, k_li = garr(1, d)
                T_lf, b_lf, k_lf = garr(3, d)
                T_B, b_B, k_B = garr(4, d)
                T_m, b_m, k_m = garr(5, d)
                U, b_U, k_U = G["U"][d]
                Gg, b_G, k_G = G["G"][d]
                qi_i = 2 * d
                qi_f = 2 * d + 1
                for blk in range(TP // 512):
                    c0, c1 = blk * 512, (blk + 1) * 512
                    ps, pk = ps_next(ALLPS)
                    for kc in range(8):
                        mm(ps[0:4, :], wg[:, kc, qi_i * 4:(qi_i + 1) * 4], hT[:, kc, c0:c1], kc == 0, kc == 7, [wgk, hk(kc, blk)], [pk])
                    act(T_li[:, c0:c1], ps[0:4, :], AF.Identity, [pk, "gbias"], [k_li], bias=smallc[0:4, qi_i:qi_i + 1])
                    yield
                    ps2, pk2 = ps_next(ALLPS)
                    for kc in range(8):
                        mm(ps2[0:4, :], wg[:, kc, qi_f * 4:(qi_f + 1) * 4], hT[:, kc, c0:c1], kc == 0, kc == 7, [wgk, hk(kc, blk)], [pk2])
                    act(T_lf[:, c0:c1], ps2[0:4, :], AF.Exp, [pk2, "ngbias"], [k_lf], scale=-1.0, bias=smallc[0:4, 4 + qi_f:5 + qi_f])
                    yield
                act(T_lf[:, :], T_lf[:, :], AF.Ln, [k_lf], [k_lf], bias=1.0)
                yield
                ts("vector", T_lf[:, :], T_lf[:, :], -1.0, ALU.mult, [k_lf], [k_lf])
                yield
                for si, (s0, s1) in enumerate(segs):
                    def dirv(ap):
                        v = ap[:, s0:s1]
                        return v[:, ::-1] if d == 1 else v
                    n_ = s1 - s0
                    onesv = ones4[b_B:b_B + 4, 0:1].to_broadcast([4, n_])
                    P.add("vector", lambda e, o=dirv(T_B), a=onesv, b=dirv(T_lf): e.tensor_tensor_scan(
                        out=o, data0=a, data1=b, initial=0.0, op0=ALU.mult, op1=ALU.add), ["ones4", k_lf], [k_B])
                    yield
                    init = m0t[b_m:b_m + 4, d:d + 1] if is_sample else 0.0
                    P.add("vector", lambda e, o=dirv(T_m), a=dirv(T_lf), b=dirv(T_li), i_=init: e.tensor_tensor_scan(
                        out=o, data0=a, data1=b, initial=i_, op0=ALU.add, op1=ALU.max), [k_lf, k_li, "m0t"], [k_m])
                    yield
                tt("vector", U[:, :], T_B[:, :], T_m[:, :], ALU.subtract, [k_B, k_m], [k_U])
                yield
                tt("vector", Gg[:, :], T_li[:, :], T_B[:, :], ALU.subtract, [k_li, k_B], [k_G])
                yield
                for si, (s0, s1) in enumerate(segs):
                    nch = (s1 - s0) // 128
                    order = list(range(nch)) if d == 0 else list(range(nch - 1, -1, -1))
                    for oi, c in enumerate(order):
                        a0 = s0 + c * 128
                        a1 = a0 + 128
                        endi = (a1 - 1) if d == 0 else a0
                        if oi == 0:
                            if is_sample:
                                ts("vector", T_li[:, a0:a1], U[:, a0:a1], m0t[b_li:b_li + 4, d:d + 1], ALU.add, [k_U, "m0t"], [k_li])
                                yield
                            else:
                                cp("vector", T_li[:, a0:a1], U[:, a0:a1], [k_U], [k_li])
                                yield
                        else:
                            pc = order[oi - 1]
                            pend = (s0 + pc * 128 + 127) if d == 0 else (s0 + pc * 128)
                            ts("vector", T_li[:, a0:a1], U[:, a0:a1], U[:, pend:pend + 1], ALU.subtract, [k_U], [k_li])
                            yield
                        ts("vector", T_lf[:, a0:a1], Gg[:, a0:a1], U[:, endi:endi + 1], ALU.add, [k_G, k_U], [k_lf])
                        yield
                act(T_li[:, :], T_li[:, :], AF.Exp, [k_li], [k_li])
                yield
                cp("vector", G["WIb"][d][0][:, :], T_li[:, :], [k_li], [G["WIb"][d][2]])
                yield
                act(T_lf[:, :], T_lf[:, :], AF.Exp, [k_lf], [k_lf])
                yield
                act(T_B[:, :], T_m[:, :], AF.Exp, [k_m], [k_B], scale=-1.0)
                yield
                for si, (s0, s1) in enumerate(segs):
                    nch = (s1 - s0) // 128
                    e0 = (s0 + 127) if d == 0 else s0
                    cp("vector", w0r[d * 32:d * 32 + 4, s0 // 128:s0 // 128 + nch], T_li[:, e0:s1:128], [k_li], [("w0r", d)])
                    yield
                    if not is_sample:
                        mi = (s1 - 1) if d == 0 else s0
                        dma("sync", D["m_o"][j, si, d, :].rearrange("(p o) -> p o", o=1), T_m[:, mi:mi + 1], [k_m], [], semkey=("mout", d, si), is_out=True)
                        yield
                for cg in range(nchtot):
                    ps, pk = ps_next(ALLPS)
                    for qq, (X, bX, kX) in enumerate([(T_li, b_li, k_li), (T_B, b_B, k_B), (T_lf, b_lf, k_lf), (Gg, b_G, k_G)]):
                        qi = d * 4 + qq
                        mm(ps[:, qi * 4:(qi + 1) * 4], X[:, cg * 128:(cg + 1) * 128], identF[bX:bX + 4, bX:bX + 4], True, True,
                           [kX, "identF"], [pk])
                    cp("vector", cols[:, cg, d * 16:(d + 1) * 16], ps[:, d * 16:(d + 1) * 16], [pk], [("cols", d)])
                    yield

            gens = [gdir(0), gdir(1)]
            alive = [True, True]
            while any(alive):
                for gi_ in range(2):
                    if alive[gi_]:
                        try:
                            next(gens[gi_])
                        except StopIteration:
                            alive[gi_] = False

        def b_unit(j, hb, t0, TP, segs, is_sample, cond):
            P.phase = "B_proj%d" % int(is_sample)
            arena_reset(keep_base=True)
            nqt = TP // 128
            nblk = TP // 512
            nchtot = nqt
            cols = G["cols"]
            w0r = G["w0r"]
            Hsum = carveR([nqt, 256])
            Hsum32 = Hsum.bitcast(F32)
            xbufs = [carve([TP]) for _ in range(2)]
            accs = [carve([TP]) for _ in range(2)]
            qT = carve([2, TP], BF16)
            kT = carve([2, TP], BF16)
            Vaug = carve([nqt, 258], BF16)
            nseg = len(segs)
            C32s = [[carve([2, 258]) for _ in range(2)] for _ in range(nseg)]
            Cbs = [[[carve([2, 258], BF16) for _ in range(2)] for _ in range(2)] for _ in range(nseg)]
            W0bc = [carve([nchtot]) for _ in range(2)]
            nrot = 4 * nseg
            DT = [carve([128], BF16) for _ in range(nrot)]
            PT = [carve([128], BF16) for _ in range(nrot)]
            kw = [carve([256], BF16) for _ in range(nrot)]
            qw = [carve([2, 128], BF16) for _ in range(nrot)]
            ktok_all = xbufs[0].bitcast(BF16).rearrange("p (a b) -> p a b", a=nqt)
            dd = [carve([2]) for _ in range(2 * nseg)]
            hk = lambda kc, blk: ("h", kc, blk)
            memset("gpsimd", Vaug[:, :, 256:257], 1.0, [("V", i) for i in range(nqt)])
            for si in range(nseg):
                for d in range(2):
                    ck = ("C32", si, d)
                    memset("gpsimd", C32s[si][d][:, :, :], 0.0, [ck])
                    if is_sample:
                        for dc in range(2):
                            dma("sync", C32s[si][d][:, dc, 0:256], D["C0%d" % j][d, hb, dc * 128:(dc + 1) * 128, :], (), [ck], semkey=("C0ld", d))
                            dma("sync", C32s[si][d][:, dc, 256:257],
                                D["n0%d" % j][d, hb, dc * 128:(dc + 1) * 128].rearrange("(p o) -> p o", o=1), (), [ck], semkey=("C0ld", d))
            wqk, wk = w_pop("Bqk")
            def conv_a(c4):
                isk = c4 >= 2
                c = c4 % 2
                ci = (8 if isk else 0) + 2 * hb + c
                xbuf, acc = xbufs[c4 % 2], accs[c4 % 2]
                xk, ak = ("xbuf", c4 % 2), ("acc", c4 % 2)
                for blk in range(nblk):
                    c0, c1 = blk * 512, (blk + 1) * 512
                    ps, pk = ps_next(ALLPS)
                    for kc in range(8):
                        mm(ps[:, :], wqk[:, kc, c4 * 128:(c4 + 1) * 128], hT[:, kc, c0:c1], kc == 0, kc == 7, [wk, hk(kc, blk)], [pk])
                    cp("scalar", xbuf[:, c0:c1], ps[:, :], [pk], [xk])
                for (s0, s1) in segs:
                    ts("gpsimd", acc[:, s0:s1], xbuf[:, s0:s1], convt[:, ci, 1:2], ALU.mult, [xk, "convt"], [ak], s2=0.0, op1=ALU.add)
                    stt(acc[:, s0 + 1:s1], xbuf[:, s0:s1 - 1], convt[:, ci, 0:1], acc[:, s0 + 1:s1], ALU.mult, ALU.add,
                        [xk, "convt", ak], [ak])
                    stt(acc[:, s0:s1 - 1], xbuf[:, s0 + 1:s1], convt[:, ci, 2:3], acc[:, s0:s1 - 1], ALU.mult, ALU.add,
                        [xk, "convt", ak], [ak])

            def conv_b(c4):
                isk = c4 >= 2
                c = c4 % 2
                xbuf, acc = xbufs[c4 % 2], accs[c4 % 2]
                xk, ak = ("xbuf", c4 % 2), ("acc", c4 % 2)
                act(xbuf[:, :], acc[:, :], AF.Tanh, [ak], [xk], scale=0.5)
                sf = (1.0 / 32.0) if isk else 0.5
                ts("gpsimd", xbuf[:, :], xbuf[:, :], sf, ALU.mult, [xk], [xk], s2=sf, op1=ALU.add)
                dst = kT if isk else qT
                tt("vector", dst[:, c, :], xbuf[:, :], acc[:, :], ALU.mult, [xk, ak], [("kT" if isk else "qT", c)])

            conv_a(0)
            for c4 in range(4):
                if c4 + 1 < 4:
                    conv_a(c4 + 1)
                conv_b(c4)
            wv, wvk = w_pop("Bv")
            for qt in range(nqt):
                ps, pk = ps_next(ALLPS)
                for kc in range(8):
                    mm(ps[:, 0:256], hT[:, kc, qt * 128:(qt + 1) * 128], wv[:, kc, 0:256], kc == 0, kc == 7, [hk(kc, qt // 4), wvk], [pk])
                cp("scalar", Vaug[:, qt, 0:256], ps[:, 0:256], [pk], [("V", qt)])
            for qt in range(nqt):
                ps, pk = ps_next(ALLPS)
                for dc in range(2):
                    mm(ps[:, dc * 128:(dc + 1) * 128], kT[:, dc, qt * 128:(qt + 1) * 128], identB[:, :], True, True, [("kT", dc), "identB"], [pk])
                cp("scalar" if qt % 2 else "vector", ktok_all[:, qt, :], ps[:, 0:256], [pk], [("ktok", qt), ("xbuf", 0)])
            for d in range(2):
                ps, pk = ps_next(ALLPS)
                mm(ps[:, 0:nchtot], selF[d * 32:d * 32 + 4, hb * 128:(hb + 1) * 128], w0r[d * 32:d * 32 + 4, 0:nchtot], True, True,
                   ["selF", ("w0r", d)], [pk])
                cp("vector", W0bc[d][:, :], ps[:, 0:nchtot], [pk], [("W0bc", d)])
            P.phase = "B_chunks%d" % int(is_sample)
            hs_sets = [set() for _ in segs]
            for si in range(nseg):
                for d in range(2):
                    cp("scalar", Cbs[si][d][1][:, :, :], C32s[si][d][:, :, :], [("C32", si, d)], [("Cb", si, d, 1)])
            if True:
                def step_info(si, oi):
                    s0, s1 = segs[si]
                    nch = (s1 - s0) // 128
                    orders = [list(range(nch)), list(range(nch - 1, -1, -1))]
                    par = oi % 2
                    info = []
                    for d in range(2):
                        c = orders[d][oi]
                        a0 = s0 + c * 128
                        info.append((d, a0, a0 + 128, a0 // 128, (is_sample and oi == nch - 1), si * 4 + d * 2 + par))
                    return par, info

                def front(si, oi):
                    par, info = step_info(si, oi)
                    banks = []
                    for (d, a0, a1, cg, skip, r) in info:
                        U, b_U, k_U = G["U"][d]
                        WI, b_W, k_W = G["WIb"][d]
                        psx, pkx = ps_next(ALLPS)
                        banks.append((psx, pkx))
                        for dc in range(2):
                            mm(psx[:, 0:128], kT[:, dc, a0:a1], qT[:, dc, a0:a1], dc == 0, dc == 1, [("kT", dc), ("qT", dc)], [pkx])
                        mm(psx[:, 128:256], selF[b_U:b_U + 4, hb * 128:(hb + 1) * 128], U[:, a0:a1], True, False, [k_U, "selF"], [pkx])
                        mm(psx[:, 128:256], identB[:, :], maskFB[:, d * 128:(d + 1) * 128], False, True, ["identB", "maskFB"], [pkx])
                        mm(psx[:, 256:384], identB[b_W:b_W + 4, b_W + hb:b_W + hb + 1].to_broadcast([4, 128]), WI[:, a0:a1], True, True,
                           [k_W, "identB"], [pkx])
                    for (d, a0, a1, cg, skip, r), (psx, pkx) in zip(info, banks):
                        colb = d * 16
                        act(DT[r][:, :], psx[:, 128:256], AF.Exp, [pkx, ("cols", d)], [("DT", r)],
                            bias=cols[:, cg, colb + 12 + hb:colb + 12 + hb + 1])
                        if not skip:
                            ts("gpsimd", kw[r][:, :], ktok_all[:, cg, :], cols[:, cg, colb + 8 + hb:colb + 8 + hb + 1], ALU.mult,
                               [("ktok", cg), ("cols", d)], [("kw", r)], s2=0.0, op1=ALU.add)
                        tt("vector", qw[r][:, :, :], qT[:, :, a0:a1], psx[:, 256:384].unsqueeze(1).to_broadcast([128, 2, 128]), ALU.mult,
                           [("qT", 0), ("qT", 1), pkx], [("qw", r)])
                    for (d, a0, a1, cg, skip, r), (psx, pkx) in zip(info, banks):
                        tt("vector", PT[r][:, :], psx[:, 0:128], DT[r][:, :], ALU.mult, [pkx, ("DT", r)], [("PT", r)])

                def back(si, oi):
                    par, info = step_info(si, oi)
                    C32 = C32s[si]
                    Cb = Cbs[si]
                    hs_written = hs_sets[si]
                    res = []
                    for (d, a0, a1, cg, skip, r) in info:
                        ps3, pk3 = ps_next(ALLPS)
                        mm(ps3[:, 0:257], PT[r][:, :], Vaug[:, cg, 0:257], True, False, [("PT", r), ("V", cg)], [pk3])
                        for dc in range(2):
                            mm(ps3[:, 0:257], qw[r][:, dc, :], Cb[d][1 - par][:, dc, 0:257], False, dc == 1,
                               [("qw", r), ("Cb", si, d, 1 - par)], [pk3])
                        psd, pkd = (None, None)
                        if not skip:
                            for dc in range(2):
                                mm(ps3[:, 384 + dc:385 + dc], kw[r][:, dc * 128:(dc + 1) * 128], Vaug[:, cg, 256:257], True, True,
                                   [("kw", r), ("V", cg)], [pk3])
                            psd, pkd = ps_next(ALLPS)
                            for dc in range(2):
                                mm(psd[:, dc * 256:(dc + 1) * 256], kw[r][:, dc * 128:(dc + 1) * 128], Vaug[:, cg, 0:256], True, True,
                                   [("kw", r), ("V", cg)], [pkd])
                        res.append((ps3, pk3, psd, pkd))
                    for (d, a0, a1, cg, skip, r), (ps3, pk3, psd, pkd) in zip(info, res):
                        if skip:
                            continue
                        ck = ("C32", si, d)
                        stt(C32[d][:, :, 0:256], C32[d][:, :, 0:256], W0bc[d][:, cg:cg + 1],
                            psd[:, :].rearrange("p (a b) -> p a b", a=2), ALU.mult, ALU.add, [ck, ("W0bc", d), pkd], [ck])
                        stt(C32[d][:, :, 256], C32[d][:, :, 256], W0bc[d][:, cg:cg + 1], ps3[:, 384:386], ALU.mult, ALU.add,
                            [ck, ("W0bc", d), pk3], [ck])
                        cp("scalar", Cb[d][par][:, :, :], C32[d][:, :, :], [ck], [("Cb", si, d, par)])
                    for (d, a0, a1, cg, skip, r), (ps3, pk3, psd, pkd) in zip(info, res):
                        colb = d * 16
                        ts("vector", dd[si * 2 + d][:, 0:1], ps3[:, 256:257], cols[:, cg, colb + 4 + hb:colb + 4 + hb + 1], ALU.max,
                           [pk3, ("cols", d)], [("dd", si, d)])
                        stt(dd[si * 2 + d][:, 0:1], ps3[:, 256:257], -1.0, dd[si * 2 + d][:, 0:1], ALU.mult, ALU.max, [pk3, ("dd", si, d)], [("dd", si, d)])
                        P.add("vector", lambda e, o=dd[si * 2 + d][:, 1:2], i=dd[si * 2 + d][:, 0:1]: e.reciprocal(o, i), [("dd", si, d)], [("dd", si, d)])
                        if cg not in hs_written:
                            hs_written.add(cg)
                            act(Hsum[:, cg, :], ps3[:, 0:256], AF.Copy, [pk3, ("dd", si, d)], [("Hs", cg)], scale=dd[si * 2 + d][:, 1:2])
                        else:
                            stt(Hsum[:, cg, :], ps3[:, 0:256], dd[si * 2 + d][:, 1:2], Hsum32[:, cg, :], ALU.mult, ALU.add,
                                [pk3, ("dd", si, d), ("Hs", cg)], [("Hs", cg)])

                def seg_pipe(si):
                    nch_ = (segs[si][1] - segs[si][0]) // 128
                    front(si, 0)
                    yield
                    for oi in range(nch_):
                        if oi + 1 < nch_:
                            front(si, oi + 1)
                            yield
                        back(si, oi)
                        yield
                    if not is_sample:
                        for d in range(2):
                            ck = ("C32", si, d)
                            for dc in range(2):
                                dma("sync", D["C_o"][j, si, d, hb, dc * 128:(dc + 1) * 128, :], C32s[si][d][:, dc, 0:256], [ck], [],
                                    semkey=("Cout", si, d), is_out=True)
                                dma("sync", D["n_o"][j, si, d, hb, dc * 128:(dc + 1) * 128].rearrange("(p o) -> p o", o=1),
                                    C32s[si][d][:, dc, 256:257], [ck], [], semkey=("Cout", si, d), is_out=True)

                gens_ = [seg_pipe(si) for si in range(nseg)]
                alive_ = [True] * nseg
                while any(alive_):
                    for gi_ in range(nseg):
                        if alive_[gi_]:
                            try:
                                next(gens_[gi_])
                            except StopIteration:
                                alive_[gi_] = False
            P.phase = "B_epi%d" % int(is_sample)
            arena_reset(keep_base=True, keepR=True)
            og = carve([nblk, 2, 512])
            zsn = carve([nblk, 2, 512])
            hgT = carve([nblk, 2, 512])
            tq = [carve([512]) for _ in range(2)]
            lnb = carve([nblk, 512])
            rstd = carve([nblk, 512])
            sq = carveR([nblk, 2, 512])
            boT = carveR([2, TP])
            woz, wozk = w_pop("Boz")
            it = 0
            for blk in range(nblk):
                c0, c1 = blk * 512, (blk + 1) * 512
                for c in range(2):
                    ps, pk = ps_next(ALLPS)
                    for kc in range(8):
                        mm(ps[:, :], woz[:, kc, c * 128:(c + 1) * 128], hT[:, kc, c0:c1], kc == 0, kc == 7, [wozk, hk(kc, blk)], [pk])
                    act(og[:, blk, c, :], ps[:, :], AF.Tanh, [pk], [("og", blk, c)], scale=0.5)
                    ts("gpsimd", og[:, blk, c, :], og[:, blk, c, :], 0.5, ALU.mult, [("og", blk, c)], [("og", blk, c)], s2=0.5, op1=ALU.add)
                    ps2, pk2 = ps_next(ALLPS)
                    for kc in range(8):
                        mm(ps2[:, :], woz[:, kc, 256 + c * 128:256 + (c + 1) * 128], hT[:, kc, c0:c1], kc == 0, kc == 7,
                           [wozk, hk(kc, blk)], [pk2])
                    t_ = tq[it % 2]
                    tk = ("tq", it % 2)
                    it += 1
                    act(t_[:, :], ps2[:, :], AF.Tanh, [pk2], [tk], scale=0.5)
                    ts("gpsimd", t_[:, :], t_[:, :], 0.5, ALU.mult, [tk], [tk], s2=0.5, op1=ALU.add)
                    nbc = smallc[:, 8 + 2 * hb + c:8 + 2 * hb + c + 1]
                    stt(zsn[:, blk, c, :], t_[:, :], nbc, ps2[:, :], ALU.mult, ALU.mult, [tk, "normbt", pk2], [("zsn", blk, c)])
            wo, wok = w_pop("Bo")
            for blk in range(nblk):
                for c in range(2):
                    ps, pk = ps_next(ALLPS)
                    for i in range(4):
                        qt = blk * 4 + i
                        mm(ps[:, i * 128:(i + 1) * 128], Hsum[:, qt, c * 128:(c + 1) * 128], identR[:, :], True, True,
                           [("Hs", qt), "identR"], [pk])
                    tt("vector", hgT[:, blk, c, :], ps[:, :], og[:, blk, c, :], ALU.mult, [pk, ("og", blk, c)], [("hgT", blk, c)])
            pss = []
            for blk in range(nblk):
                for c in range(2):
                    act(sq[:, blk, c, :], hgT[:, blk, c, :], AF.Square, [("hgT", blk, c)], [("sq", blk, c)])
                ps2, pk2 = ps_next(ALLPS)
                for c in range(2):
                    mm(ps2[:, :], onesR[:, :], sq[:, blk, c, :], c == 0, c == 1, ["onesR", ("sq", blk, c)], [pk2])
                pss.append((ps2, pk2))
            for blk in range(nblk):
                ps2, pk2 = pss[blk]
                act(lnb[:, blk, :], ps2[:, :], AF.Ln, [pk2, "epsb"], [("lnb", blk)], scale=1.0 / 256.0, bias=epsb[:, 0:1])
            for blk in range(nblk):
                act(rstd[:, blk, :], lnb[:, blk, :], AF.Exp, [("lnb", blk)], [("rstd", blk)], scale=-0.5)
            for blk in range(nblk):
                c0, c1 = blk * 512, (blk + 1) * 512
                for c in range(2):
                    tt("vector", hgT[:, blk, c, :], hgT[:, blk, c, :], rstd[:, blk, :], ALU.mult, [("hgT", blk, c), ("rstd", blk)], [("hgT", blk, c)])
                    tt("vector", boT[:, c, c0:c1], hgT[:, blk, c, :], zsn[:, blk, c, :], ALU.mult, [("hgT", blk, c), ("zsn", blk, c)], [("boT", c, blk)])
            out_proj2(wo, wok, boT, lambda fc, blk: ("boT", fc, blk), t0, TP, cond)


        CS = {}

        def c_pre(j, t0, TP, segs, is_sample):
            P.phase = "C_pre%d" % int(is_sample)
            arena_reset()
            nblk = TP // 512
            NK = TP + (256 if is_sample else 0)
            qnT = carveR([3, TP])
            ckvT = carveR([2, NK])
            KR = carve([NK], BF16)
            CS["qnT"], CS["ckvT"], CS["KR"], CS["NK"] = qnT, ckvT, KR, NK
            ar["base"] = ar["off"]
            ar["baseR"] = ar["offR"]
            sq = [carveR([512]) for _ in range(2)]
            lnb = carve([512])
            rstd = carve([512])
            tmb = [carve([512]) for _ in range(2)]
            stg = [carve([512]) for _ in range(2)]
            ta = [carve([512]) for _ in range(2)]
            tb = [carve([512]) for _ in range(2)]
            hk = lambda kc, blk: ("h", kc, blk)
            it = 0
            for (wname, nch, nfeat, ncol0, dst, dkey, is_kv) in (("Cqa", 3, 384.0, 16, qnT, "qnT", False), ("Ckva", 2, 256.0, 24, ckvT, "ckvT", True)):
                w, wk = w_pop(wname)
                for blk in range(nblk):
                    c0, c1 = blk * 512, (blk + 1) * 512
                    pss = []
                    for c in range(nch):
                        ps, pk = ps_next(ALLPS)
                        for kc in range(8):
                            mm(ps[:, :], w[:, kc, c * 128:(c + 1) * 128], hT[:, kc, c0:c1], kc == 0, kc == 7, [wk, hk(kc, blk)], [pk])
                        pss.append((ps, pk))
                    ps_s, pk_s = ps_next(ALLPS)
                    for c in range(nch):
                        act(sq[c % 2][:, :], pss[c][0][:, :], AF.Square, [pss[c][1]], [("sqc", c % 2)])
                        mm(ps_s[:, :], onesR[:, :], sq[c % 2][:, :], c == 0, c == nch - 1, ["onesR", ("sqc", c % 2)], [pk_s])
                    rstd_from_sumsq(ps_s[:, :], pk_s, nfeat, rstd[:, :], "rstd", 512, lnb[:, :], "lnb")
                    for c in range(nch):
                        t_ = tmb[it % 2]
                        tk = ("tmb", it % 2)
                        it += 1
                        tt("vector", t_[:, :], pss[c][0][:, :], rstd[:, :], ALU.mult, [pss[c][1], "rstd"], [tk])
                        nrm = smallc[:, ncol0 + c:ncol0 + c + 1]
                        act(dst[:, c, c0:c1], t_[:, :], AF.Copy, [tk, "qnormt", "kvnormt"], [(dkey, c, blk)], scale=nrm)
                        if is_kv and not is_sample:
                            sg = stg[c % 2]
                            act(sg[:, :], t_[:, :], AF.Copy, [tk, "kvnormt"], [("stgc", c % 2)], scale=nrm)
                            dma("sync", D["ckvT_o"][j, c * 128:(c + 1) * 128, c0:c1], sg[:, :], [("stgc", c % 2)], [],
                                semkey=("stgc", c % 2), is_out=True)
                if is_kv and is_sample:
                    for c in range(2):
                        dma("gpsimd", ckvT[:, c, TP:TP + 256], D["cckvT%d" % j][c * 128:(c + 1) * 128, :], (), [("ckvT", c, "ctx")])
            wkr, wkrk = w_pop("Ckr")
            for blk in range(nblk):
                cols = (blk * 512, (blk + 1) * 512)
                ps, pk = ps_next(ALLPS)
                for kc in range(8):
                    mm(ps[0:96, :], wkr[:, kc, 0:96], hT[:, kc, cols[0]:cols[1]], kc == 0, kc == 7, [wkrk, hk(kc, blk)], [pk])
                if is_sample:
                    psp, pkp = ps_next(ALLPS)
                    for kc in range(8):
                        mm(psp[0:96, :], wkr[:, kc, 96:192], hT[:, kc, cols[0]:cols[1]], kc == 0, kc == 7, [wkrk, hk(kc, blk)], [pkp])
                    rope_evac(ps, pk, psp, pkp, 64, 96, cols, KR[64:96, cols[0]:cols[1]], ("KR", blk), ta[blk % 2], tb[blk % 2],
                              ("ta", blk % 2), ("tb", blk % 2))
                else:
                    cp("scalar", KR[64:96, cols[0]:cols[1]], ps[64:96, :], [pk], [("KR", blk)])
                    cp("vector", stg[0][64:96, :], ps[64:96, :], [pk], [("stgc", 0)])
                    dma("sync", D["krT_o"][j, :, cols[0]:cols[1]], stg[0][64:96, :], [("stgc", 0)], [], semkey=("stgc", 0), is_out=True)
            if is_sample:
                dma("sync", stg[1][64:96, 0:256], D["ckrT%d" % j], (), [("stgc", 1)])
                cp("vector", KR[64:96, TP:TP + 256], stg[1][64:96, 0:256], [("stgc", 1)], [("KR", "ctx")])

        def c_unit(j, g, t0, TP, segs, is_sample, cond):
            arena_reset(keep_base=True, barrier=(g == 0))
            P.phase = "C_proj%d" % int(is_sample)
            qnT, ckvT, KR, NK = CS["qnT"], CS["ckvT"], CS["KR"], CS["NK"]
            nqt = TP // 128
            nblk = TP // 512
            nkt = NK // 128
            QT = [carve([TP], BF16) for _ in range(4)]
            KTh = [carve([NK], BF16) for _ in range(4)]
            Vaug = carve([nkt, 4, 65], BF16)
            zs = carve([nqt, 256], BF16)
            aoT = carveR([2, TP])
            ta = [carve([512]) for _ in range(2)]
            tb = [carve([512]) for _ in range(2)]
            tz = [carve([256]) for _ in range(2)]
            alloc_attn_rot()
            hk = lambda kc, blk: ("h", kc, blk)
            memset("gpsimd", Vaug[:, :, :, 64:65], 1.0, [("V", i) for i in range(nkt)])
            for hl in range(4):
                memset("gpsimd", QT[hl][64:128, :], 0.0, [("QTr", hl, b) for b in range(nblk)] + [("QTz", hl)])
                memset("gpsimd", KTh[hl][64:128, :], 0.0, [("KThr", hl)])
            qn_keys = lambda blk: [("qnT", c, blk) for c in range(3)]
            wq, wqk = w_pop("Cqb")
            it = 0
            for hl in range(4):
                for blk in range(nblk):
                    cols = (blk * 512, (blk + 1) * 512)
                    ps, pk = ps_next(ALLPS)
                    for kc in range(3):
                        mm(ps[0:96, :], wq[:, kc, hl * 192:hl * 192 + 96], qnT[:, kc, cols[0]:cols[1]], kc == 0, kc == 2,
                           [wqk] + qn_keys(blk), [pk])
                    if is_sample:
                        psp, pkp = ps_next(ALLPS)
                        for kc in range(3):
                            mm(psp[0:96, :], wq[:, kc, hl * 192 + 96:hl * 192 + 192], qnT[:, kc, cols[0]:cols[1]], kc == 0, kc == 2,
                               [wqk] + qn_keys(blk), [pkp])
                        cp("scalar", QT[hl][0:64, cols[0]:cols[1]], ps[0:64, :], [pk], [("QT", hl, blk)])
                        rope_evac(ps, pk, psp, pkp, 64, 96, cols, QT[hl][64:96, cols[0]:cols[1]], ("QTr", hl, blk),
                                  ta[it % 2], tb[it % 2], ("ta", it % 2), ("tb", it % 2))
                        it += 1
                    else:
                        cp("scalar", QT[hl][0:64, cols[0]:cols[1]], ps[0:64, :], [pk], [("QT", hl, blk)])
                        cp("vector", QT[hl][64:96, cols[0]:cols[1]], ps[64:96, :], [pk], [("QTr", hl, blk)])
            wkv, wkvk = w_pop("Ckvb")
            ck_keys = lambda k0, k1: [("ckvT", c, b) for c in range(2) for b in
                                      sorted(set(["ctx" if kk >= TP else kk // 512 for kk in (k0, k1 - 1)]))]
            for hl in range(4):
                k0 = 0
                while k0 < NK:
                    k1 = min(NK, k0 + 512)
                    n_ = k1 - k0
                    ps, pk = ps_next(ALLPS)
                    for kc in range(2):
                        mm(ps[0:64, 0:n_], wkv[:, kc, hl * 128:hl * 128 + 64], ckvT[:, kc, k0:k1], kc == 0, kc == 1,
                           [wkvk] + ck_keys(k0, k1), [pk])
                    cp("vector" if (k0 // 512) % 2 else "scalar", KTh[hl][0:64, k0:k1], ps[0:64, 0:n_], [pk], [("KTh", hl)])
                    k0 = k1
                cp("vector", KTh[hl][64:96, :], KR[64:96, :], [("KR", b) for b in range(nblk)] + ([("KR", "ctx")] if is_sample else []),
                   [("KThr", hl)])
            wv3 = wkv.rearrange("p k (h c) -> p k h c", c=128)
            for kt in range(nkt):
                ps, pk = ps_next(ALLPS)
                for kc in range(2):
                    mm(ps[:, 0:256], ckvT[:, kc, kt * 128:(kt + 1) * 128], wv3[:, kc, :, 64:128], kc == 0, kc == 1,
                       [wkvk] + ck_keys(kt * 128, (kt + 1) * 128), [pk])
                cp("scalar", Vaug[:, kt, :, 0:64], ps[:, 0:256].rearrange("p (h c) -> p h c", c=64), [pk], [("V", kt)])
            wz, wzk = w_pop("Cz")
            for qt in range(nqt):
                ps, pk = ps_next(ALLPS)
                for kc in range(8):
                    mm(ps[:, 0:256], hT[:, kc, qt * 128:(qt + 1) * 128], wz[:, kc, 0:256], kc == 0, kc == 7, [hk(kc, qt // 4), wzk], [pk])
                silu_tok(ps[:, 0:256], pk, zs[:, qt, :], ("zs", qt), tz[qt % 2][:, :], ("tz", qt % 2), 256)
            P.phase = "C_attn%d" % int(is_sample)
            for hl in range(4):
                if is_sample:
                    jobs = [(kt * 128, 0, 8, None, kt) for kt in range(nkt)]
                else:
                    jobs = []
                    for s_ in range(2):
                        for jt in (2 * s_, 2 * s_ + 1):
                            jobs.append((jt * 128, 2 * s_, 2 * s_ + 2, None, jt))
                attn_head(QT[hl], lambda q0, q1, hl=hl: [k_ for b in range(q0 // 4, (q1 - 1) // 4 + 1)
                                                          for k_ in (("QT", hl, b), ("QTr", hl, b), ("QTz", hl))],
                          0, 128, KTh[hl], lambda kc0, hl=hl: [("KTh", hl), ("KThr", hl)],
                          jobs, lambda vid, hl=hl: (Vaug[:, vid, hl, :], ("V", vid)), 96.0 ** -0.5, None,
                          zs, lambda qt: ("zs", qt), hl, nqt, (hl % 2) * 2)
            P.phase = "C_out%d" % int(is_sample)
            transpose_to_aoT(zs, lambda qt: ("zs", qt), aoT, lambda c, b: ("aoT", c, b), nqt)
            out_proj("Co", aoT, lambda fc, blk: ("aoT", fc, blk), t0, TP, cond)

        run_layers(BUILD_WHICH[0])
        P.emit()
        LAST_PROG[0] = P
    return nc


LAST_PROG = [None]
BUILD_WHICH = ["ABC"]
FLAGS = {}
NCORES = [8]
DBG_SPECS = {}
DBG_OUT = {}
_CACHE = {}


def kernel(**inputs):
    inp = {k: np.asarray(v) for k, v in inputs.items()}
    sh, per = prep_inputs(inp)
    specs = input_specs()
    key = "full"
    if key not in _CACHE:
        _CACHE[key] = build()
    nc = _CACHE[key]
    in_maps = []
    for i in range(NCORES[0]):
        d = {}
        for name in specs:
            a = per[i][name] if name in per[i] else sh[name]
            a = np.ascontiguousarray(a, dtype=np.float32)
            assert list(a.shape) == specs[name][0], (name, a.shape, specs[name][0])
            d[name] = a
        in_maps.append(d)
    res = run_bass_kernel_spmd(nc, in_maps, core_ids=list(range(NCORES[0])))
    R = res.results
    for name in DBG_SPECS:
        DBG_OUT[name] = [np.asarray(R[i][name]) for i in range(NCORES[0])]
    f32 = np.float32
    y_prompt = np.zeros((16, 256, 1024), f32)
    y_sample = np.zeros((8, 1024, 1024), f32)
    a_k = np.zeros((16, 2, 256, 4, 64), f32)
    a_v = np.zeros((16, 2, 256, 4, 64), f32)
    b_mem = np.zeros((16, 2, 2, 4, 256, 256), f32)
    b_nrm = np.zeros((16, 2, 2, 4, 256), f32)
    b_max = np.zeros((16, 2, 2, 4), f32)
    c_kv = np.zeros((16, 2, 256, 256), f32)
    c_kr = np.zeros((16, 2, 256, 32), f32)
    for i in range(NCORES[0]):
        r = R[i]
        yT = np.asarray(r["yT_o"])
        y_sample[i] = yT[:, 0:1024].T
        for s in range(2):
            b = 2 * i + s
            y_prompt[b] = yT[:, 1024 + 256 * s:1024 + 256 * (s + 1)].T
            for j in range(2):
                a_k[b, j] = np.asarray(r["kaT_o"])[j][:, 256 * s:256 * (s + 1)].T.reshape(256, 4, 64)
                a_v[b, j] = np.asarray(r["va_o"])[j][256 * s:256 * (s + 1), :].reshape(256, 4, 64)
                b_mem[b, j] = np.asarray(r["C_o"])[j, s]
                b_nrm[b, j] = np.asarray(r["n_o"])[j, s]
                b_max[b, j] = np.asarray(r["m_o"])[j, s]
                c_kv[b, j] = np.asarray(r["ckvT_o"])[j][:, 256 * s:256 * (s + 1)].T
                c_kr[b, j] = np.asarray(r["krT_o"])[j][:, 256 * s:256 * (s + 1)].T
    return (y_prompt, y_sample, a_k, a_v, b_mem, b_nrm, b_max, c_kv, c_kr)
```

```python
import contextlib
import numpy as np
import concourse.bass as bass
import concourse.mybir as mybir
from concourse.bass_utils import run_bass_kernel_spmd

F32 = mybir.dt.float32
F32R = mybir.dt.float32r
BF16 = mybir.dt.bfloat16
ALU = mybir.AluOpType
AF = mybir.ActivationFunctionType
AX = mybir.AxisListType

ENGS = ["sync", "scalar", "vector", "gpsimd", "tensor"]


class Op:
    __slots__ = ("eng", "fn", "deps", "dma", "semkey", "mark", "dmacount", "seq", "phase")

    def __init__(self, eng, fn, dma, semkey):
        self.eng = eng
        self.fn = fn
        self.deps = []
        self.dma = dma
        self.semkey = semkey
        self.mark = None
        self.dmacount = None
        self.seq = None


class Prog:
    def __init__(self, nc):
        self.nc = nc
        self.streams = {e: [] for e in ENGS}
        self.last_writer = {}
        self.readers = {}
        self.all_ops = []
        self.dma_counts = {}
        self.pending_barrier = {e: [] for e in ENGS}
        self.out_dmas = []
        self.phase = "init"

    def add(self, eng, fn, reads=(), writes=(), dma=False, semkey=None, is_out=False):
        if dma and semkey is None:
            semkey = writes[0] if writes else reads[0]
        o = Op(eng, fn, dma, semkey)
        o.phase = self.phase
        deps = {}

        def add_dep(p, raw):
            if p is None or p is o:
                return
            if (not raw) and (not p.dma) and (not dma) and p.eng == eng and eng == "tensor":
                return
            deps[id(p)] = p

        for k in reads:
            add_dep(self.last_writer.get(k), True)
            if isinstance(k, tuple) and k[0] == "ps":
                for r in self.readers.get(k, ()):
                    if r.eng != eng:
                        add_dep(r, True)
        for k in writes:
            add_dep(self.last_writer.get(k), False)
            for r in self.readers.get(k, ()):
                add_dep(r, False)
        for p in self.pending_barrier[eng]:
            add_dep(p, True)
        self.pending_barrier[eng] = []
        for k in writes:
            self.last_writer[k] = o
            self.readers[k] = []
        for k in reads:
            self.readers.setdefault(k, []).append(o)
        o.deps = list(deps.values())
        if dma:
            c = self.dma_counts.get(semkey, 0) + 1
            self.dma_counts[semkey] = c
            o.dmacount = c
            if is_out:
                self.out_dmas.append(o)
        o.seq = len(self.all_ops)
        self.all_ops.append(o)
        self.streams[eng].append(o)
        return o

    def barrier(self):
        lasts = []
        for e in ENGS:
            s = self.streams[e]
            if s:
                lasts.append(s[-1])
        dm = {}
        for o in self.all_ops:
            if o.dma:
                dm[o.semkey] = o
        lasts += list(dm.values())
        for e in ENGS:
            self.pending_barrier[e] = list(lasts)

    def emit(self):
        nc = self.nc
        self.barrier()
        self.add("sync", lambda e: e.nop(), reads=(), writes=())
        for o in self.all_ops:
            for p in o.deps:
                if not p.dma:
                    p.mark = True
        counts = {e: 0 for e in ENGS}
        for e in ENGS:
            for o in self.streams[e]:
                if o.mark:
                    counts[e] += 1
                    o.mark = counts[e]
        with contextlib.ExitStack() as st:
            esem = {e: st.enter_context(nc.semaphore("sem_" + e)) for e in ENGS}
            dsem = {}
            for i, k in enumerate(self.dma_counts):
                dsem[k] = st.enter_context(nc.semaphore("dsem%d" % i))
            block = st.enter_context(nc.Block())
            streams = self.streams

            def run(eng_name, e):
                known = {}
                for o in streams[eng_name]:
                    need = {}
                    for p in o.deps:
                        if p.dma:
                            key = ("d", p.semkey)
                            val = 16 * p.dmacount
                            sem = dsem[p.semkey]
                        else:
                            key = ("e", p.eng)
                            val = p.mark
                            sem = esem[p.eng]
                        if known.get(key, 0) >= val:
                            continue
                        if key not in need or need[key][1] < val:
                            need[key] = (sem, val)
                    for key, (sem, val) in need.items():
                        e.wait_ge(sem, val)
                        known[key] = val
                    ins = o.fn(e)
                    if o.dma:
                        ins.then_inc(dsem[o.semkey], 16)
                    elif o.mark:
                        ins.then_inc(esem[eng_name], 1)

            @block.sync
            def _(e):
                run("sync", e)

            @block.scalar
            def _(e):
                run("scalar", e)

            @block.vector
            def _(e):
                run("vector", e)

            @block.gpsimd
            def _(e):
                run("gpsimd", e)

            @block.tensor
            def _(e):
                run("tensor", e)


DM = 1024
DEPTH = 4
NEG = -30000.0
EPS = 1e-6
T_S = 1024
T_P = 512
NTOK = T_S + T_P
AB_OFF = {"qa": 0, "ka": 1024, "va": 1280, "za": 1536, "qb": 2560, "kb": 3584, "vb": 4608, "ob": 5632, "zb": 6656,
          "gb": 7680}
PASSES = [(0, 1024, [(0, 1024)], True), (1024, 512, [(0, 256), (256, 512)], False)]
ARENA_WORDS = 12800
ARENAR_WORDS = 7680


def _rope_perm(n_rot, d_axis):
    half = d_axis // 2
    idx = np.arange(n_rot)
    within = idx % d_axis
    return np.where(within < half, idx + half, idx - half)


def _rope_tables(n_rot, d_axis):
    half = d_axis // 2
    rows = np.repeat(np.arange(1024 // 64, dtype=np.int32), 64).astype(np.float32)
    cols = np.tile(np.arange(64, dtype=np.int32), 1024 // 64).astype(np.float32)
    freqs = np.power(np.float32(10000.0), -np.arange(half, dtype=np.float32) / np.float32(half)).astype(np.float32)
    cos = np.zeros((n_rot, 1024), np.float32)
    sin = np.zeros((n_rot, 1024), np.float32)
    for d in range(n_rot):
        axis = d // d_axis
        within = d % d_axis
        pos = rows if axis == 0 else cols
        ang = (pos * freqs[within % half]).astype(np.float32)
        cos[d] = np.cos(ang)
        s = np.sin(ang)
        sin[d] = -s if within < half else s
    return cos, sin


def _fm(v, nchunk):
    return np.ascontiguousarray(np.asarray(v, np.float32).reshape(nchunk, 128).T)


def prep_inputs(inp):
    f32 = np.float32
    sh = {}
    cA, sA = _rope_tables(64, 32)
    sh["cosA"] = np.concatenate([cA, cA], 0)
    sh["sinA"] = np.concatenate([sA, sA], 0)
    cC, sC = _rope_tables(32, 16)
    cosC = np.ones((128, 1024), f32)
    sinC = np.zeros((128, 1024), f32)
    cosC[64:96] = cC
    sinC[64:96] = sC
    sh["cosC"] = cosC
    sh["sinC"] = sinC
    kk = np.arange(128)[:, None]
    qq = np.arange(128)[None, :]
    mA = np.zeros((128, 384), f32)
    mA[:, 0:128] = np.where(kk <= qq, 0.0, NEG)
    mA[:, 256:384] = np.where(qq <= kk, 0.0, NEG)
    sh["maskA"] = mA
    sh["maskF"] = np.where(kk <= qq, 0.0, NEG).astype(f32)
    sh["maskB"] = np.where(kk >= qq, 0.0, NEG).astype(f32)
    sh["ident"] = np.eye(128, dtype=f32)
    pA = np.zeros((128, 128), f32)
    pidx = np.arange(128)
    pperm = np.where((pidx % 32) < 16, pidx + 16, pidx - 16)
    pA[pperm, pidx] = 1.0
    sh["permA"] = pA
    sel = np.zeros((128, 4, 128), f32)
    for base in (0, 32, 64):
        for h in range(4):
            sel[base + h, h, :] = 1.0
    sh["selrows"] = sel.reshape(128, 512)
    sh["fnormT"] = _fm(inp["final_norm"], 8)
    permA = _rope_perm(64, 32)
    permC = _rope_perm(32, 16)
    for l in range(DEPTH):
        sh["wmod%d" % l] = np.ascontiguousarray(inp["w_mod"][l])
        sh["bmodT%d" % l] = _fm(inp["b_mod"][l], 24)
        sh["normgT%d" % l] = _fm(inp["norm_g"][l], 8)
    for j in range(2):
        W = np.asarray(inp["w_in_ab"][j])
        WA = np.zeros((1024, 4, 704), f32)
        for g in range(4):
            qcols = AB_OFF["qa"] + g * 256 + np.arange(256)
            kcols = AB_OFF["ka"] + g * 64 + np.arange(64)
            WA[:, g, 0:256] = W[:, qcols]
            WA[:, g, 256:320] = W[:, kcols]
            WA[:, g, 320:384] = W[:, kcols]
            WA[:, g, 384:448] = W[:, AB_OFF["va"] + g * 64 + np.arange(64)]
            WA[:, g, 448:704] = W[:, AB_OFF["za"] + g * 256 + np.arange(256)]
        sh["WA%d" % j] = WA
        WB = np.zeros((1024, 4, 1280), f32)
        for h in range(4):
            for i, nm in enumerate(["qb", "kb", "ob", "zb", "vb"]):
                WB[:, h, i * 256:(i + 1) * 256] = W[:, AB_OFF[nm] + h * 256 + np.arange(256)]
        sh["WB%d" % j] = WB
        sh["WG%d" % j] = np.ascontiguousarray(W[:, AB_OFF["gb"]:AB_OFF["gb"] + 16])
        sh["woutab%d" % j] = np.ascontiguousarray(inp["w_out_ab"][j])
        sh["sinkb%d" % j] = np.ascontiguousarray(np.broadcast_to(np.asarray(inp["sink_a"][j], f32)[None, :], (128, 16)))
        cb = np.asarray(inp["conv_b"][j], f32)
        sh["convT%d" % j] = np.ascontiguousarray(cb.reshape(3, 16, 128).transpose(2, 1, 0))
        gbv = np.asarray(inp["gate_bias_b"][j], f32).reshape(4, 4).T
        gb = np.zeros((128, 4), f32)
        for base in (0, 32, 64):
            gb[base:base + 4] = gbv
        sh["gbias%d" % j] = gb
        sh["normbT%d" % j] = _fm(inp["norm_b"][j], 8)
    for j in range(2):
        W = np.asarray(inp["w_in_c"][j])
        sh["winc%d" % j] = np.ascontiguousarray(W)
        wkr = np.zeros((1024, 192), f32)
        wkr[:, 64:96] = W[:, 640:672]
        wkr[:, 160:192] = W[:, 640 + permC]
        sh["wkr%d" % j] = wkr
        Wq = np.asarray(inp["w_qb_c"][j]).reshape(384, 16, 96)
        wqb = np.zeros((384, 16, 192), f32)
        wqb[:, :, 0:96] = Wq
        wqb[:, :, 96:160] = Wq[:, :, 0:64]
        wqb[:, :, 160:192] = Wq[:, :, 64 + permC]
        sh["wqb%d" % j] = wqb
        sh["wkvb%d" % j] = np.ascontiguousarray(inp["w_kvb_c"][j])
        sh["woutc%d" % j] = np.ascontiguousarray(inp["w_out_c"][j])
        sh["qnormT%d" % j] = _fm(inp["q_norm_c"][j], 3)
        sh["kvnormT%d" % j] = _fm(inp["kv_norm_c"][j], 2)
    per = []
    for i in range(8):
        d = {}
        xs = np.asarray(inp["x_sample"][i])
        xp = np.asarray(inp["x_prompt"][2 * i:2 * i + 2]).reshape(512, 1024)
        d["xT"] = np.ascontiguousarray(np.concatenate([xs, xp], 0).T)
        cond = np.stack([np.asarray(inp["c"][i]), np.asarray(inp["c_ctx"])], -1)
        d["condT"] = np.ascontiguousarray(cond.reshape(8, 128, 2).transpose(1, 0, 2))
        for j in range(2):
            ck = np.asarray(inp["cache_a_k"][i, j])
            ckT = ck.transpose(1, 2, 0)
            d["KctxT%d" % j] = np.ascontiguousarray(np.concatenate([ckT, ckT], 1))
            d["Vctx%d" % j] = np.ascontiguousarray(np.asarray(inp["cache_a_v"][i, j]).reshape(256, 256))
            d["C0%d" % j] = np.ascontiguousarray(inp["state_b_mem"][i, j])
            d["n0%d" % j] = np.ascontiguousarray(inp["state_b_norm"][i, j])
            m0v = np.asarray(inp["state_b_max"][i, j]).T
            m0 = np.zeros((128, 2), f32)
            for base in (0, 32, 64):
                m0[base:base + 4] = m0v
            d["m0%d" % j] = m0
            d["cckvT%d" % j] = np.ascontiguousarray(np.asarray(inp["cache_c_kv"][i, j]).T)
            d["ckrT%d" % j] = np.ascontiguousarray(np.asarray(inp["cache_c_krope"][i, j]).T)
        per.append(d)
    return sh, per


IN_SPECS = None


def input_specs():
    s = {}
    s["xT"] = ([1024, NTOK], False)
    s["condT"] = ([128, 8, 2], False)
    for n in ["cosA", "sinA", "cosC", "sinC"]:
        s[n] = ([128, 1024], False)
    s["maskA"] = ([128, 384], False)
    s["maskF"] = ([128, 128], False)
    s["maskB"] = ([128, 128], False)
    s["ident"] = ([128, 128], False)
    s["permA"] = ([128, 128], False)
    s["selrows"] = ([128, 512], False)
    s["fnormT"] = ([128, 8], False)
    for l in range(DEPTH):
        s["wmod%d" % l] = ([1024, 3072], True)
        s["bmodT%d" % l] = ([128, 24], False)
        s["normgT%d" % l] = ([128, 8], False)
    for j in range(2):
        s["WA%d" % j] = ([1024, 4, 704], True)
        s["WB%d" % j] = ([1024, 4, 1280], True)
        s["WG%d" % j] = ([1024, 16], True)
        s["woutab%d" % j] = ([2048, 1024], True)
        s["sinkb%d" % j] = ([128, 16], False)
        s["convT%d" % j] = ([128, 16, 3], False)
        s["gbias%d" % j] = ([128, 4], False)
        s["normbT%d" % j] = ([128, 8], False)
        s["KctxT%d" % j] = ([4, 128, 256], False)
        s["Vctx%d" % j] = ([256, 256], False)
        s["C0%d" % j] = ([2, 4, 256, 256], False)
        s["n0%d" % j] = ([2, 4, 256], False)
        s["m0%d" % j] = ([128, 2], False)
        s["winc%d" % j] = ([1024, 1696], True)
        s["wkr%d" % j] = ([1024, 192], True)
        s["wqb%d" % j] = ([384, 16, 192], True)
        s["wkvb%d" % j] = ([256, 2048], True)
        s["woutc%d" % j] = ([1024, 1024], True)
        s["qnormT%d" % j] = ([128, 3], False)
        s["kvnormT%d" % j] = ([128, 2], False)
        s["cckvT%d" % j] = ([256, 256], True)
        s["ckrT%d" % j] = ([32, 256], False)
    return s


def output_specs():
    return {
        "yT_o": [1024, NTOK],
        "kaT_o": [2, 256, 512],
        "va_o": [2, 512, 256],
        "C_o": [2, 2, 2, 4, 256, 256],
        "n_o": [2, 2, 2, 4, 256],
        "m_o": [2, 2, 2, 4],
        "ckvT_o": [2, 256, 512],
        "krT_o": [2, 32, 512],
    }


def build(nlayers=DEPTH, do_final_norm=True):
    nc = bass.Bass("TRN2", target_bir_lowering=False)
    D = {}
    for name, (shape, r) in input_specs().items():
        D[name] = nc.dram_tensor(name, list(shape), F32R if r else F32, kind="ExternalInput").ap()
    for name, shape in output_specs().items():
        D[name] = nc.dram_tensor(name, list(shape), F32, kind="ExternalOutput").ap()
    for name, shape in DBG_SPECS.items():
        D[name] = nc.dram_tensor(name, list(shape), F32, kind="ExternalOutput").ap()
    P = Prog(nc)
    st = contextlib.ExitStack()
    with st:
        def sb(name, shape, dt=F32):
            return st.enter_context(nc.sbuf_tensor(name, list(shape), dt))

        psb = [st.enter_context(nc.psum_tensor("psb%d" % i, [128, 512], F32)) for i in range(8)]
        yT = sb("yT", [128, 8, NTOK])
        hT = sb("hT", [128, 8, 1024], F32R)
        wr = [sb("wr%d" % i, [128, 4096], F32R) for i in range(2)]
        tabC = sb("tabC", [128, 1024])
        tabS = sb("tabS", [128, 1024])
        arena = sb("arena", [128, ARENA_WORDS])
        arenaR = sb("arenaR", [128, ARENAR_WORDS], F32R)
        identF = sb("identF", [128, 128])
        identR = sb("identR", [128, 128], F32R)
        identB = sb("identB", [128, 128], BF16)
        permB = sb("permB", [128, 128], BF16)
        onesR = sb("onesR", [128, 128], F32R)
        ones4 = sb("ones4", [128, 128])
        selF = sb("selF", [128, 512])
        maskAf = arena[:, 0:384]
        maskAb = sb("maskAb", [128, 384], BF16)
        maskFf = arena[:, 384:640]
        maskFB = sb("maskFB", [128, 256], BF16)
        condt = sb("condt", [128, 8, 2])
        cond_t = sb("cond_t", [128, 8, 2])
        scond = sb("scond", [128, 8, 2], F32R)
        modv_b = [sb("modv%d" % i, [128, 24, 2]) for i in range(2)]
        avec_b = [sb("avec%d" % i, [128, 8, 2]) for i in range(2)]
        bmodt_b = [sb("bmodt%d" % i, [128, 24]) for i in range(2)]
        normgt_b = [sb("normgt%d" % i, [128, 8]) for i in range(2)]
        cur = {"l": 0}
        smallc = sb("smallc", [128, 96])
        sinkt = sb("sinkt", [128, 16])
        convt = sb("convt", [128, 16, 3])
        m0t = sb("m0t", [128, 2])
        epsb = sb("epsb", [128, 1])

        state = {"ps": 0, "uid": 0}

        def uid(prefix):
            state["uid"] += 1
            return (prefix, state["uid"])

        def ps_next(pool=(4, 5, 6, 7)):
            i = pool[state["ps"] % len(pool)]
            state["ps"] += 1
            return psb[i], ("ps", i)

        ALLPS = (0, 1, 2, 3, 4, 5, 6, 7)

        def mm(out, lhsT, rhs, start, stop, reads, writes):
            op_ = P.add("tensor", lambda e, o=out, l=lhsT, r=rhs, s=start, t=stop: e.matmul(o, l, r, start=s, stop=t),
                        reads, writes)
            op_.seq = 2 if lhsT.dtype == F32 else 1

        def act(out, in_, func, reads, writes, scale=1.0, bias=0.0):
            P.add("scalar", lambda e, o=out, i=in_, f=func, s=scale, b=bias: e.activation(out=o, in_=i, func=f, bias=b, scale=s),
                  reads, writes)

        def tt(eng, out, in0, in1, op, reads, writes):
            if eng == "gpsimd" and FLAGS.get("nopool"):
                eng = "vector"
            P.add(eng, lambda e, o=out, a=in0, b=in1, p=op: e.tensor_tensor(out=o, in0=a, in1=b, op=p), reads, writes)

        def ts(eng, out, in0, s1, op0, reads, writes, s2=None, op1=None):
            if eng == "gpsimd" and FLAGS.get("nopool"):
                eng = "vector"
            if op1 is None:
                P.add(eng, lambda e, o=out, a=in0, x=s1, p=op0: e.tensor_scalar(out=o, in0=a, scalar1=x, scalar2=None, op0=p),
                      reads, writes)
            else:
                P.add(eng, lambda e, o=out, a=in0, x=s1, p=op0, y=s2, q=op1: e.tensor_scalar(out=o, in0=a, scalar1=x, scalar2=y, op0=p, op1=q),
                      reads, writes)

        def stt(out, in0, scalar, in1, op0, op1, reads, writes):
            P.add("vector", lambda e, o=out, a=in0, s=scalar, b=in1, p=op0, q=op1: e.scalar_tensor_tensor(out=o, in0=a, scalar=s, in1=b, op0=p, op1=q),
                  reads, writes)

        def cp(eng, out, in_, reads, writes):
            if eng == "gpsimd" and FLAGS.get("nopool"):
                eng = "vector"
            if eng == "scalar":
                act(out, in_, AF.Copy, reads, writes)
            else:
                P.add(eng, lambda e, o=out, i=in_: e.tensor_copy(o, i), reads, writes)

        def memset(eng, ap, val, writes):
            if eng == "gpsimd" and FLAGS.get("nopool"):
                eng = "vector"
            P.add(eng, lambda e, a=ap, v=val: e.memset(a, v), (), writes)

        def dma(eng, out, in_, reads, writes, semkey=None, is_out=False):
            P.add(eng, lambda e, o=out, i=in_: e.dma_start(out=o, in_=i), reads, writes, dma=True, semkey=semkey, is_out=is_out)

        ar = {"off": 0, "offR": 0}

        def carveR(shape):
            n = 1
            for s_ in shape:
                n *= s_
            off = ar["offR"]
            ar["offR"] = off + n
            assert ar["offR"] <= ARENAR_WORDS, ("arenaR overflow", ar["offR"])
            ap = arenaR[:, off:off + n]
            if len(shape) == 2:
                ap = ap.rearrange("p (a b) -> p a b", a=shape[0])
            elif len(shape) == 3:
                ap = ap.rearrange("p (a b c) -> p a b c", a=shape[0], b=shape[1])
            return ap

        def carve(shape, dt=F32):
            n = 1
            for s_ in shape:
                n *= s_
            words = n if dt in (F32, F32R) else (n + 1) // 2
            off = ar["off"]
            ar["off"] = off + words
            assert ar["off"] <= ARENA_WORDS, ("arena overflow", ar["off"])
            ap = arena[:, off:off + words]
            if dt != F32:
                ap = ap.bitcast(dt)
            if len(shape) == 2:
                ap = ap.rearrange("p (a b) -> p a b", a=shape[0])
            elif len(shape) == 3:
                ap = ap.rearrange("p (a b c) -> p a b c", a=shape[0], b=shape[1])
            return ap

        wq_list = []
        wst = {"issued": 0, "next": 0}

        def w_issue_upto(i):
            while wst["issued"] <= min(i, len(wq_list) - 1):
                t = wst["issued"]
                name, dap, k, n = wq_list[t]
                slot = t % 2
                view = wr[slot][:, 0:k * n].rearrange("p (k n) -> p k n", k=k)
                dma("gpsimd", view, dap, (), [("wr", slot)])
                wst["issued"] += 1

        def w_pop(name):
            i = wst["next"]
            assert wq_list[i][0] == name, (wq_list[i][0], name)
            w_issue_upto(i + 1)
            wst["next"] += 1
            _, _, k, n = wq_list[i]
            slot = i % 2
            return wr[slot][:, 0:k * n].rearrange("p (k n) -> p k n", k=k), ("wr", slot)

        def wtile(name, dram2d, k, n):
            wq_list.append((name, dram2d.rearrange("(k p) n -> p k n", p=128), k, n))

        def nxt_mod(l, ip, b):
            if ip == 1 and l + 1 < nlayers:
                wtile("mod%d_%d" % (l + 1, b), D["wmod%d" % (l + 1)][:, b * 512:(b + 1) * 512], 8, 512)

        def layer_tiles(l, ip):
            j = l // 2
            if ip == 0 and l == 0:
                for b in range(6):
                    wtile("mod%d_%d" % (l, b), D["wmod%d" % l][:, b * 512:(b + 1) * 512], 8, 512)
            if l % 2 == 0:
                for g in range(4):
                    wtile("Aq", D["WA%d" % j][:, g, 0:256], 8, 256)
                    wtile("Ak", D["WA%d" % j][:, g, 256:384], 8, 128)
                    wtile("Avz", D["WA%d" % j][:, g, 384:704], 8, 320)
                    wtile("Ao", D["woutab%d" % j][g * 256:(g + 1) * 256, :], 2, 1024)
                    nxt_mod(l, ip, g)
                wtile("G", D["WG%d" % j][:, :], 8, 16)
                for h in range(4):
                    wtile("Bqk", D["WB%d" % j][:, h, 0:512], 8, 512)
                    wtile("Bv", D["WB%d" % j][:, h, 1024:1280], 8, 256)
                    wtile("Boz", D["WB%d" % j][:, h, 512:1024], 8, 512)
                    wtile("Bo", D["woutab%d" % j][1024 + h * 256:1024 + (h + 1) * 256, :], 2, 1024)
                    if h < 2:
                        nxt_mod(l, ip, 4 + h)
            else:
                wtile("Cqa", D["winc%d" % j][:, 0:384], 8, 384)
                wtile("Ckva", D["winc%d" % j][:, 384:640], 8, 256)
                wtile("Ckr", D["wkr%d" % j][:, :], 8, 192)
                nxt_mod(l, ip, 0)
                nxt_mod(l, ip, 1)
                for g in range(4):
                    wtile("Cqb", D["wqb%d" % j][:, g * 4:(g + 1) * 4, :].rearrange("a h c -> a (h c)"), 3, 768)
                    wtile("Ckvb", D["wkvb%d" % j][:, g * 512:(g + 1) * 512], 2, 512)
                    wtile("Cz", D["winc%d" % j][:, 672 + g * 256:672 + (g + 1) * 256], 8, 256)
                    wtile("Co", D["woutc%d" % j][g * 256:(g + 1) * 256, :], 2, 1024)
                    nxt_mod(l, ip, 2 + g)

        for l in range(nlayers):
            for ip in range(2):
                layer_tiles(l, ip)

        dma("sync", identF[:], D["ident"], (), ["identF"])
        cp("vector", identB[:], identF[:], ["identF"], ["identB"])
        dma("sync", arena[:, 640:768], D["permA"], (), ["permAf"])
        cp("vector", permB[:], arena[:, 640:768], ["permAf"], ["permB"])
        cp("vector", identR[:], identF[:], ["identF"], ["identR"])
        memset("vector", ones4[:], 1.0, ["ones4"])
        cp("vector", onesR[:], ones4[:, :], ["ones4"], ["onesR"])
        memset("vector", epsb[:], EPS, ["epsb"])
        dma("sync", selF[:], D["selrows"], (), ["selF"])
        dma("sync", maskAf[:], D["maskA"], (), ["maskAf"])
        cp("vector", maskAb[:], maskAf[:], ["maskAf"], ["maskAb"])
        dma("sync", maskFf[:, 0:128], D["maskF"], (), ["maskFf"])
        dma("sync", maskFf[:, 128:256], D["maskB"], (), ["maskFf"])
        cp("vector", maskFB[:], maskFf[:], ["maskFf"], ["maskFB"])
        for kc in range(8):
            dma("sync", yT[:, kc, :], D["xT"][kc * 128:(kc + 1) * 128, :], (), [("y", kc, b) for b in range(3)],
                semkey=("yload", kc))
        dma("sync", condt[:], D["condT"], (), ["condt"])
        act(cond_t[:], condt[:], AF.Tanh, ["condt"], ["cond_t"], scale=0.5)
        ts("vector", cond_t[:], cond_t[:], 0.5, ALU.mult, ["cond_t"], ["cond_t"], s2=0.5, op1=ALU.add)
        tt("vector", scond[:], cond_t[:], condt[:], ALU.mult, ["cond_t", "condt"], ["scond"])

        def ykey(kc, t0, blk):
            return ("y", kc, (t0 + blk * 512) // 512)

        def rstd_from_sumsq(ps_ap, pskey, n, out_ap, outkey, ncols, tmp_ap, tmpkey):
            act(tmp_ap, ps_ap, AF.Ln, [pskey, "epsb"], [tmpkey], scale=1.0 / n, bias=epsb[:, 0:1])
            act(out_ap, tmp_ap, AF.Exp, [tmpkey], [outkey], scale=-0.5)

        def y_update(ps_ap, pskey, oc, t0, blk, cond):
            k = ykey(oc, t0, blk)
            ysl = yT[:, oc, t0 + blk * 512:t0 + (blk + 1) * 512]
            mv = modv_b[cur["l"] % 2]
            stt(ysl, ps_ap, mv[:, 16 + oc, cond:cond + 1], ysl, ALU.mult, ALU.add, [pskey, k, ("modv", cur["l"] % 2)], [k])

        def out_proj(wname, aoT, aokey_fn, t0, TP, cond):
            wo, wkey = w_pop(wname)
            out_proj2(wo, wkey, aoT, aokey_fn, t0, TP, cond)

        def out_proj2(wo, wkey, aoT, aokey_fn, t0, TP, cond):
            for blk in range(TP // 512):
                for oc in range(8):
                    ps, pk = ps_next(ALLPS)
                    for fc in range(2):
                        mm(ps[:, :], wo[:, fc, oc * 128:(oc + 1) * 128], aoT[:, fc, blk * 512:(blk + 1) * 512],
                           fc == 0, fc == 1, [wkey, aokey_fn(fc, blk)], [pk])
                    y_update(ps[:, :], pk, oc, t0, blk, cond)

        def transpose_to_aoT(zs, zskey_fn, aoT, aokey_fn, nqt):
            for c in range(2):
                for b4 in range(nqt // 4):
                    ps, pk = ps_next(ALLPS)
                    for i in range(4):
                        qt = b4 * 4 + i
                        mm(ps[:, i * 128:(i + 1) * 128], zs[:, qt, c * 128:(c + 1) * 128], identB[:, :], True, True,
                           [zskey_fn(qt), "identB"], [pk])
                    cp("scalar", aoT[:, c, b4 * 512:(b4 + 1) * 512], ps[:, :], [pk], [aokey_fn(c, b4)])

        def attn_head(QT, qkeys_fn, qbase, qrows, KT, kkeys_fn, jobs, vfn, scale, sink_ap, zs, zskey_fn, hl, nqt, obank0):
            first = {}
            last = {}
            for ji, (kc0, lo, hi, mc, vid) in enumerate(jobs):
                for qt in range(lo, hi):
                    first.setdefault(qt, ji)
                    last[qt] = ji
            nb = (nqt + 3) // 4
            lastpv = {}
            for ji, (kc0, lo, hi, mc, vid) in enumerate(jobs):
                for qt in range(lo, hi):
                    lastpv[qt // 4] = (ji, qt)
            obanks = [(psb[obank0 + b], ("ps", obank0 + b)) for b in range(nb)]
            firstpv = {}
            for ji, (kc0, lo, hi, mc, vid) in enumerate(jobs):
                for qt in range(lo, hi):
                    firstpv.setdefault(qt // 4, (ji, qt))
            pieces = []
            for ji, (kc0, lo, hi, mc, vid) in enumerate(jobs):
                q0 = lo
                while q0 < hi:
                    q1 = min(hi, q0 + 4)
                    pieces.append((ji, kc0, lo, mc, vid, q0, q1))
                    q0 = q1

            def emit_score(pc):
                ji, kc0, lo, mc, vid, q0, q1 = pc
                ncol = (q1 - q0) * 128
                ps, pk = ps_next()
                mm(ps[:, 0:ncol], KT[qbase:qbase + qrows, kc0:kc0 + 128], QT[qbase:qbase + qrows, q0 * 128:q1 * 128],
                   True, mc is None, kkeys_fn(kc0) + qkeys_fn(q0, q1), [pk])
                if mc is not None:
                    m0_ = mc + (q0 - lo) * 128
                    mm(ps[:, 0:ncol], identB[:, :], maskAb[:, m0_:m0_ + ncol], False, True, ["identB", "maskAb"], [pk])
                eb, ek = e_next()
                act(eb[:, 0:ncol], ps[:, 0:ncol], AF.Exp, [pk], [ek], scale=scale)
                return eb, ek

            def emit_pv(pc, eb, ek):
                ji, kc0, lo, mc, vid, q0, q1 = pc
                vap, vkey = vfn(vid)
                for qt in range(q0, q1):
                    ob, ok = obanks[qt // 4]
                    c0 = (qt % 4) * 128
                    P.add("tensor", lambda e, o=ob[:, c0:c0 + 65], l=eb[:, (qt - q0) * 128:(qt - q0 + 1) * 128], r=vap,
                          s_=(firstpv[qt // 4] == (ji, qt)), t_=(lastpv[qt // 4] == (ji, qt)):
                          e.matmul(o, l, r, start=s_, stop=t_, skip_group_check=True), [ek, vkey], [ok])

            LA = 3
            q_ = [emit_score(pieces[i]) for i in range(min(LA, len(pieces)))]
            for i in range(len(pieces)):
                if i + LA < len(pieces):
                    q_.append(emit_score(pieces[i + LA]))
                eb_, ek_ = q_.pop(0)
                emit_pv(pieces[i], eb_, ek_)
            for b in range(nb):
                ob, ok = obanks[b]
                n4 = min(4, nqt - b * 4)
                ov = ob[:, 0:n4 * 128].rearrange("p (a c) -> p a c", a=n4)
                dsum, dk = small_next()
                if sink_ap is not None:
                    ts("vector", dsum[:, 0:n4], ov[:, :, 64], sink_ap, ALU.add, [ok, "sinkt"], [dk])
                else:
                    cp("vector", dsum[:, 0:n4], ov[:, :, 64], [ok], [dk])
                P.add("vector", lambda e, o=dsum[:, 4:4 + n4], i=dsum[:, 0:n4]: e.reciprocal(o, i), [dk], [dk])
                at, atk = atmp_next()
                tt("vector", at[:, 0:n4, :], ov[:, :, 0:64], dsum[:, 4:4 + n4].to_broadcast([128, n4, 64]), ALU.mult,
                   [ok, dk], [atk])
                zsl = zs[:, b * 4:b * 4 + n4, hl * 64:(hl + 1) * 64]
                zk = [zskey_fn(qt) for qt in range(b * 4, b * 4 + n4)]
                tt("gpsimd", zsl, zsl, at[:, 0:n4, :], ALU.mult, zk + [atk], zk)

        rot = {}

        def e_next():
            r = rot["E"]
            i = r["i"] % len(r["bufs"])
            r["i"] += 1
            return r["bufs"][i], ("E", i)

        def small_next():
            r = rot["S"]
            i = r["i"] % len(r["bufs"])
            r["i"] += 1
            return r["bufs"][i], ("dsum", i)

        def atmp_next():
            r = rot["A"]
            i = r["i"] % len(r["bufs"])
            r["i"] += 1
            return r["bufs"][i], ("atmp", i)

        def alloc_attn_rot():
            rot["E"] = {"i": 0, "bufs": [carve([512], BF16) for _ in range(5)]}
            rot["S"] = {"i": 0, "bufs": [carve([8]) for _ in range(2)]}
            rot["A"] = {"i": 0, "bufs": [carve([4, 64], BF16) for _ in range(2)]}

        def silu_tok(ps_ap, pskey, out_ap, outkey, tmp_ap, tmpkey, ncols):
            act(tmp_ap, ps_ap, AF.Tanh, [pskey], [tmpkey], scale=0.5)
            ts("gpsimd", tmp_ap, tmp_ap, 0.5, ALU.mult, [tmpkey], [tmpkey], s2=0.5, op1=ALU.add)
            tt("vector", out_ap, tmp_ap, ps_ap, ALU.mult, [tmpkey, pskey], [outkey])

        def rope_evac(psx, pkx, psp, pkp, r0, r1, cols, out_ap, outkey, tmpa, tmpb, tka, tkb):
            tt("vector", tmpa[r0:r1, :], psp[r0:r1, :], tabS[r0:r1, cols[0]:cols[1]], ALU.mult, [pkp, "tabS"], [tka])
            tt("vector", tmpb[r0:r1, :], psx[r0:r1, :], tabC[r0:r1, cols[0]:cols[1]], ALU.mult, [pkx, "tabC"], [tkb])
            tt("gpsimd", out_ap, tmpa[r0:r1, :], tmpb[r0:r1, :], ALU.add, [tka, tkb], [outkey])

        def dbg(name, ap, reads):
            if name in DBG_SPECS:
                P.add("sync", lambda e, o=D[name], i=ap: e.dma_start(out=o, in_=i), reads, [], dma=True, semkey=("dbg", name), is_out=True)

        def arena_reset(keep_base=False, keepR=False, barrier=True):
            if barrier:
                P.barrier()
            ar["off"] = ar.get("base", 0) if keep_base else 0
            if not keep_base:
                ar["base"] = 0
                ar["baseR"] = 0
            if not keepR:
                ar["offR"] = ar.get("baseR", 0) if keep_base else 0

        def mod_load_small(l):
            i = l % 2
            dma("sync", bmodt_b[i][:], D["bmodT%d" % l], (), [("bmodt", i)])
            dma("sync", normgt_b[i][:], D["normgT%d" % l], (), [("normgt", i)])

        def mod_tile(l, b):
            i = l % 2
            modrow = arena[:, ARENA_WORDS - 512:ARENA_WORDS]
            w, wk = w_pop("mod%d_%d" % (l, b))
            psr, pkr = ps_next(ALLPS)
            for kc in range(8):
                mm(psr[0:2, :], scond[:, kc, :], w[:, kc, :], kc == 0, kc == 7, [wk, "scond"], [pkr])
            cp("scalar", modrow[0:2, :], psr[0:2, :], [pkr], ["modrow"])
            ps, pk = ps_next(ALLPS)
            for oc4 in range(4):
                mm(ps[:, 2 * oc4:2 * oc4 + 2], modrow[0:2, oc4 * 128:(oc4 + 1) * 128], identF[0:2, 0:2], True, True,
                   ["modrow", "identF"], [pk])
            psv = ps[:, 0:8].rearrange("p (c j) -> p c j", j=2)
            for jj in range(2):
                tt("vector", modv_b[i][:, b * 4:(b + 1) * 4, jj], psv[:, :, jj], bmodt_b[i][:, b * 4:(b + 1) * 4], ALU.add,
                   [pk, ("bmodt", i)], [("modv", i)])

        def mod_finish(l):
            i = l % 2
            for jj in range(2):
                stt(avec_b[i][:, :, jj], modv_b[i][:, 8:16, jj], 1.0, normgt_b[i][:, :], ALU.add, ALU.mult,
                    [("modv", i), ("normgt", i)], [("avec", i)])

        def layer_mod(l):
            P.phase = "mod"
            mod_load_small(l)
            for b in range(6):
                mod_tile(l, b)
            mod_finish(l)

        def norm_blocks(t0, nblk, scale_fn, bias_fn, out_fn, extra_reads):
            sqb = [carveR([512]) for _ in range(2)]
            lnb = carve([512])
            rstd = carve([512])
            tmb = [carve([512]) for _ in range(2)]
            for blk in range(nblk):
                c0 = t0 + blk * 512
                ps, pk = ps_next(ALLPS)
                for kc in range(8):
                    act(sqb[kc % 2][:, :], yT[:, kc, c0:c0 + 512], AF.Square, [ykey(kc, t0, blk)], [("sqb", kc % 2)])
                    mm(ps[:, :], onesR[:, :], sqb[kc % 2][:, :], kc == 0, kc == 7, ["onesR", ("sqb", kc % 2)], [pk])
                rstd_from_sumsq(ps[:, :], pk, 1024.0, rstd[:, :], "rstd", 512, lnb[:, :], "lnb")
                for kc in range(8):
                    tt("vector", tmb[kc % 2][:, :], yT[:, kc, c0:c0 + 512], rstd[:, :], ALU.mult,
                       [ykey(kc, t0, blk), "rstd"], [("tmb", kc % 2)])
                    oap, okey = out_fn(kc, blk)
                    P.add("scalar", lambda e, o=oap, i=tmb[kc % 2][:, :], s=scale_fn(kc), b=bias_fn(kc):
                          e.activation(out=o, in_=i, func=AF.Identity, bias=b, scale=s),
                          [("tmb", kc % 2)] + extra_reads, [okey])

        def layer_h(t0, TP, cond):
            P.phase = "h"
            arena_reset()
            i_ = cur["l"] % 2
            norm_blocks(t0, TP // 512,
                        lambda kc: avec_b[i_][:, kc, cond:cond + 1], lambda kc: modv_b[i_][:, kc, cond:cond + 1],
                        lambda kc, blk: (hT[:, kc, blk * 512:(blk + 1) * 512], ("h", kc, blk)), [("avec", i_), ("modv", i_)])

        def final_out():
            P.phase = "final"
            arena_reset()
            fn = carve([8])
            dma("sync", fn[:, :], D["fnormT"], (), ["fnorm"])
            stg = [carve([512]) for _ in range(3)]
            cnt = {"i": 0}

            def out_fn(kc, blk):
                i = cnt["i"] % 3
                cnt["i"] += 1
                cnt["last"] = (i, kc, blk)
                return stg[i][:, :], ("stg", i)
            if do_final_norm:
                sqb = [carveR([512]) for _ in range(2)]
                lnb = carve([512])
                rstd = carve([512])
                tmb = [carve([512]) for _ in range(2)]
                for blk in range(3):
                    c0 = blk * 512
                    ps, pk = ps_next(ALLPS)
                    for kc in range(8):
                        act(sqb[kc % 2][:, :], yT[:, kc, c0:c0 + 512], AF.Square, [("y", kc, blk)], [("sqb", kc % 2)])
                        mm(ps[:, :], onesR[:, :], sqb[kc % 2][:, :], kc == 0, kc == 7, ["onesR", ("sqb", kc % 2)], [pk])
                    rstd_from_sumsq(ps[:, :], pk, 1024.0, rstd[:, :], "rstd", 512, lnb[:, :], "lnb")
                    for kc in range(8):
                        tt("vector", tmb[kc % 2][:, :], yT[:, kc, c0:c0 + 512], rstd[:, :], ALU.mult,
                           [("y", kc, blk), "rstd"], [("tmb", kc % 2)])
                        oap, okey = out_fn(kc, blk)
                        act(oap, tmb[kc % 2][:, :], AF.Copy, [("tmb", kc % 2), "fnorm"], [okey], scale=fn[:, kc:kc + 1])
                        dma("sync", D["yT_o"][kc * 128:(kc + 1) * 128, c0:c0 + 512], oap, [okey], [], semkey=okey, is_out=True)
            else:
                for kc in range(8):
                    dma("sync", D["yT_o"][kc * 128:(kc + 1) * 128, :], yT[:, kc, :], [("y", kc, b) for b in range(3)], [],
                        semkey=("yout", kc), is_out=True)

        def a_unit(j, g, t0, TP, segs, is_sample, cond):
            arena_reset(barrier=(g == 0))
            P.phase = "A_proj%d" % int(is_sample)
            nqt = TP // 128
            nblk = TP // 512
            NK = TP + (256 if is_sample else 0)
            nvt = nqt + (2 if is_sample else 0)
            qT = carve([2, TP], BF16)
            KT2 = [carve([NK], BF16) for _ in range(2)]
            Vaug = carve([nvt, 65], BF16)
            zs = carve([nqt, 256], BF16)
            aoT = carveR([2, TP])
            ta = [carve([512]) for _ in range(2)]
            tb = [carve([512]) for _ in range(2)]
            tz = [carve([256]) for _ in range(2)]
            stg = carve([512])
            stv = [carve([64]) for _ in range(2)]
            xbr = [carve([512], BF16) for _ in range(2)]
            alloc_attn_rot()
            vkeys = [("V", i) for i in range(nvt)]
            memset("gpsimd", Vaug[:, :, 64:65], 1.0, vkeys)
            allkt = [("KT", b) for b in range(nblk)] + ([("KT", "ctx")] if is_sample else [])
            memset("gpsimd", KT2[0][64:128, :], 0.0, allkt + ["KTz"])
            memset("gpsimd", KT2[1][0:64, :], 0.0, allkt + ["KTz"])
            hk = lambda kc, blk: ("h", kc, blk)
            wq, wk = w_pop("Aq")
            tiles = [("q", c, blk) for c in range(2) for blk in range(nblk)] + [("k", 0, blk) for blk in range(nblk)]
            st_ = {}
            wkh = {}

            def proj_x(i):
                kind, c, blk = tiles[i]
                cols = (blk * 512, (blk + 1) * 512)
                if kind == "k" and "w" not in wkh:
                    wkh["w"] = w_pop("Ak")
                psx, pkx = ps_next(ALLPS)
                for kc in range(8):
                    if kind == "q":
                        lhs, wkey_ = wq[:, kc, c * 128:(c + 1) * 128], wk
                    else:
                        lhs, wkey_ = wkh["w"][0][:, kc, 0:128], wkh["w"][1]
                    mm(psx[:, :], lhs, hT[:, kc, cols[0]:cols[1]], kc == 0, kc == 7, [wkey_, hk(kc, blk)], [pkx])
                if is_sample:
                    xb_, xk_ = xbr[i % 2], ("xb", i % 2)
                    cp("scalar", xb_[:, :], psx[:, :], [pkx], [xk_])
                st_[i] = (psx, pkx)

            def proj_y(i):
                kind, c, blk = tiles[i]
                cols = (blk * 512, (blk + 1) * 512)
                psx, pkx = st_.pop(i)
                if is_sample:
                    xb_, xk_ = xbr[i % 2], ("xb", i % 2)
                    psp, pkp = ps_next(ALLPS)
                    mm(psp[:, :], permB[:, :], xb_[:, :], True, True, ["permB", xk_], [pkp])
                    tA, tB, kA, kB = ta[i % 2], tb[i % 2], ("ta", i % 2), ("tb", i % 2)
                    tt("vector", tA[:, :], psp[:, :], tabS[:, cols[0]:cols[1]], ALU.mult, [pkp, "tabS"], [kA])
                    tt("vector", tB[:, :], psx[:, :], tabC[:, cols[0]:cols[1]], ALU.mult, [pkx, "tabC"], [kB])
                    if kind == "q":
                        tt("gpsimd", qT[:, c, cols[0]:cols[1]], tA[:, :], tB[:, :], ALU.add, [kA, kB], [("qT", c, blk)])
                    else:
                        tt("gpsimd", KT2[0][0:64, cols[0]:cols[1]], tA[0:64, :], tB[0:64, :], ALU.add, [kA, kB], [("KT", blk)])
                        tt("gpsimd", KT2[1][64:128, cols[0]:cols[1]], tA[64:128, :], tB[64:128, :], ALU.add, [kA, kB], [("KT", blk)])
                else:
                    if kind == "q":
                        cp("scalar", qT[:, c, cols[0]:cols[1]], psx[:, :], [pkx], [("qT", c, blk)])
                    else:
                        cp("scalar", KT2[0][0:64, cols[0]:cols[1]], psx[0:64, :], [pkx], [("KT", blk)])
                        cp("scalar", KT2[1][64:128, cols[0]:cols[1]], psx[64:128, :], [pkx], [("KT", blk)])
                        if not FLAGS.get("noka"):
                            cp("vector", stg[0:64, :], psx[0:64, :], [pkx], ["stg"])
                            dma("sync", D["kaT_o"][j, g * 64:(g + 1) * 64, :], stg[0:64, :], ["stg"], [], semkey=("kaout", g), is_out=True)

            proj_x(0)
            for i in range(len(tiles)):
                if i + 1 < len(tiles):
                    proj_x(i + 1)
                proj_y(i)
            if is_sample and not FLAGS.get("noctx"):
                dma("sync", stg[:, 0:256], D["KctxT%d" % j][g], (), ["stg"])
                cp("vector", KT2[0][0:64, TP:TP + 256], stg[0:64, 0:256], ["stg"], [("KT", "ctx")])
                cp("vector", KT2[1][64:128, TP:TP + 256], stg[64:128, 0:256], ["stg"], [("KT", "ctx")])
            if FLAGS.get("a_stop", 9) <= 2:
                for nm in ("Avz", "Ao"):
                    w_pop(nm)
                return
            wv, wvk = w_pop("Avz")
            for qt in range(nqt):
                ps, pk = ps_next(ALLPS)
                for kc in range(8):
                    mm(ps[:, 0:320], hT[:, kc, qt * 128:(qt + 1) * 128], wv[:, kc, 0:320], kc == 0, kc == 7,
                       [hk(kc, qt // 4), wvk], [pk])
                cp("scalar", Vaug[:, qt, 0:64], ps[:, 0:64], [pk], [("V", qt)])
                silu_tok(ps[:, 64:320], pk, zs[:, qt, :], ("zs", qt), tz[qt % 2][:, :], ("tz", qt % 2), 256)
                if not is_sample:
                    cp("vector", stv[qt % 2][:, :], ps[:, 0:64], [pk], [("stv", qt % 2)])
                    dma("sync", D["va_o"][j, qt * 128:(qt + 1) * 128, g * 64:(g + 1) * 64], stv[qt % 2][:, :],
                        [("stv", qt % 2)], [], semkey=("stv", qt % 2), is_out=True)
            if is_sample:
                for c in range(2):
                    dma("sync", stv[c][:, :], D["Vctx%d" % j][c * 128:(c + 1) * 128, g * 64:(g + 1) * 64], (), [("stv", c)])
                    cp("vector", Vaug[:, nqt + c, 0:64], stv[c][:, :], [("stv", c)], [("V", nqt + c)])
            if FLAGS.get("a_stop", 9) <= 3:
                w_pop("Ao")
                return
            P.phase = "A_attn%d" % int(is_sample)
            for hl in range(4):
                p = hl % 2
                c = hl // 2
                h = g * 4 + hl
                if is_sample:
                    jobs = [(TP, 0, 8, None, nqt), (TP + 128, 0, 8, None, nqt + 1)]
                    for jt in range(8):
                        lo = max(0, jt - 1)
                        hi = min(8, jt + 2)
                        jobs.append((jt * 128, lo, hi, (lo - (jt - 1)) * 128, jt))
                else:
                    jobs = []
                    for s_ in range(2):
                        for jt in (2 * s_, 2 * s_ + 1):
                            jobs.append((jt * 128, 2 * s_, 2 * s_ + 2, None, jt))
                attn_head(qT[:, c, :], lambda q0, q1, c=c: [("qT", c, b) for b in range(q0 // 4, (q1 - 1) // 4 + 1)],
                          0, 128, KT2[p], lambda kc0: ["KTz"] + ([("KT", "ctx")] if kc0 >= TP else [("KT", kc0 // 512)]),
                          jobs, lambda vid: (Vaug[:, vid, :], ("V", vid)), 0.125, sinkt[:, h:h + 1],
                          zs, lambda qt: ("zs", qt), hl, nqt, (hl % 2) * 2)
            if FLAGS.get("a_stop", 9) <= 4:
                w_pop("Ao")
                return
            P.phase = "A_out%d" % int(is_sample)
            transpose_to_aoT(zs, lambda qt: ("zs", qt), aoT, lambda c, b: ("aoT", c, b), nqt)
            if FLAGS.get("a_stop", 9) <= 5:
                w_pop("Ao")
                return
            out_proj("Ao", aoT, lambda fc, blk: ("aoT", fc, blk), t0, TP, cond)

        def run_layers(which):
            for l in range(nlayers):
                j = l // 2
                even = (l % 2 == 0)
                cur["l"] = l
                if l == 0:
                    layer_mod(l)
                else:
                    mod_finish(l)
                if even:
                    dma("sync", tabC[:], D["cosA"], (), ["tabC"])
                    dma("sync", tabS[:], D["sinA"], (), ["tabS"])
                    dma("sync", sinkt[:], D["sinkb%d" % j], (), ["sinkt_raw"])
                    act(sinkt[:], sinkt[:], AF.Exp, ["sinkt_raw"], ["sinkt"])
                    dma("sync", convt[:], D["convT%d" % j], (), ["convt"])
                    dma("sync", smallc[:, 0:4], D["gbias%d" % j], (), ["gbias"])
                    dma("sync", smallc[:, 8:16], D["normbT%d" % j], (), ["normbt"])
                    dma("sync", m0t[:], D["m0%d" % j], (), ["m0t"])
                else:
                    dma("sync", tabC[:], D["cosC"], (), ["tabC"])
                    dma("sync", tabS[:], D["sinC"], (), ["tabS"])
                    dma("sync", smallc[:, 16:19], D["qnormT%d" % j], (), ["qnormt"])
                    dma("sync", smallc[:, 24:26], D["kvnormT%d" % j], (), ["kvnormt"])
                for ip, (t0, TP, segs, is_sample) in enumerate(PASSES):
                    cond = 0 if is_sample else 1
                    pre = (ip == 1 and l + 1 < nlayers)
                    if pre:
                        mod_load_small(l + 1)

                    def premod(b):
                        if pre:
                            ph = P.phase
                            P.phase = "mod"
                            mod_tile(l + 1, b)
                            P.phase = ph
                    layer_h(t0, TP, cond)
                    if l == 0:
                        dbg("hT%d" % ip, hT[:, :, 0:TP].bitcast(F32), [("h", kc, b) for kc in range(8) for b in range(TP // 512)])
                    if even:
                        for g in range(4):
                            if "A" in which:
                                a_unit(j, g, t0, TP, segs, is_sample, cond)
                            else:
                                for nm in ("Aq", "Ak", "Avz", "Ao"):
                                    w_pop(nm)
                            premod(g)
                        if "B" in which:
                            gates_stage(j, t0, TP, segs, is_sample)
                            for hb in range(4):
                                b_unit(j, hb, t0, TP, segs, is_sample, cond)
                                if hb < 2:
                                    premod(4 + hb)
                        else:
                            w_pop("G")
                            for hb in range(4):
                                for nm in ("Bqk", "Bv", "Boz", "Bo"):
                                    w_pop(nm)
                                if hb < 2:
                                    premod(4 + hb)
                    else:
                        if "C" in which:
                            c_pre(j, t0, TP, segs, is_sample)
                            premod(0)
                            premod(1)
                            for g in range(4):
                                c_unit(j, g, t0, TP, segs, is_sample, cond)
                                premod(2 + g)
                        else:
                            for nm in ("Cqa", "Ckva", "Ckr"):
                                w_pop(nm)
                            premod(0)
                            premod(1)
                            for g in range(4):
                                for nm in ("Cqb", "Ckvb", "Cz", "Co"):
                                    w_pop(nm)
                                premod(2 + g)
            final_out()


        G = {}

        def gates_stage(j, t0, TP, segs, is_sample):
            P.phase = "gates%d" % int(is_sample)
            arena_reset()
            nchtot = TP // 128
            ranges = [carve([TP]) for _ in range(1)]
            wib = carve([TP], BF16)

            def garr(idx, d_):
                return ranges[idx][d_ * 32:d_ * 32 + 4, :], d_ * 32, ("ga", idx, d_)
            cols = carve([nchtot, 32])
            w0r = carve([nchtot])
            G["cols"] = cols
            G["w0r"] = w0r
            G["U"] = [garr(0, 0), garr(0, 1)]
            G["WIb"] = [(wib[d_ * 32:d_ * 32 + 4, :], d_ * 32, ("wib", d_)) for d_ in range(2)]
            ar["base"] = ar["off"]
            ranges += [carve([TP]) for _ in range(5)]
            G["G"] = [garr(2, 0), garr(2, 1)]
            ts("vector", smallc[:, 4:8], smallc[:, 0:4], -1.0, ALU.mult, ["gbias"], ["ngbias"])
            wg, wgk = w_pop("G")
            hk = lambda kc, blk: ("h", kc, blk)
            def gdir(d):
                T_li, b_li, k_li = garr(1, d)
                T_lf, b_lf, k_lf = garr(3, d)
                T_B, b_B, k_B = garr(4, d)
                T_m, b_m, k_m = garr(5, d)
                U, b_U, k_U = G["U"][d]
                Gg, b_G, k_G = G["G"][d]
                qi_i = 2 * d
                qi_f = 2 * d + 1
                for blk in range(TP // 512):
                    c0, c1 = blk * 512, (blk + 1) * 512
                    ps, pk = ps_next(ALLPS)
                    for kc in range(8):
                        mm(ps[0:4, :], wg[:, kc, qi_i * 4:(qi_i + 1) * 4], hT[:, kc, c0:c1], kc == 0, kc == 7, [wgk, hk(kc, blk)], [pk])
                    act(T_li[:, c0:c1], ps[0:4, :], AF.Identity, [pk, "gbias"], [k_li], bias=smallc[0:4, qi_i:qi_i + 1])
                    yield
                    ps2, pk2 = ps_next(ALLPS)
                    for kc in range(8):
                        mm(ps2[0:4, :], wg[:, kc, qi_f * 4:(qi_f + 1) * 4], hT[:, kc, c0:c1], kc == 0, kc == 7, [wgk, hk(kc, blk)], [pk2])
                    act(T_lf[:, c0:c1], ps2[0:4, :], AF.Exp, [pk2, "ngbias"], [k_lf], scale=-1.0, bias=smallc[0:4, 4 + qi_f:5 + qi_f])
                    yield
                act(T_lf[:, :], T_lf[:, :], AF.Ln, [k_lf], [k_lf], bias=1.0)
                yield
                ts("vector", T_lf[:, :], T_lf[:, :], -1.0, ALU.mult, [k_lf], [k_lf])
                yield
                for si, (s0, s1) in enumerate(segs):
                    def dirv(ap):
                        v = ap[:, s0:s1]
                        return v[:, ::-1] if d == 1 else v
                    n_ = s1 - s0
                    onesv = ones4[b_B:b_B + 4, 0:1].to_broadcast([4, n_])
                    P.add("vector", lambda e, o=dirv(T_B), a=onesv, b=dirv(T_lf): e.tensor_tensor_scan(
                        out=o, data0=a, data1=b, initial=0.0, op0=ALU.mult, op1=ALU.add), ["ones4", k_lf], [k_B])
                    yield
                    init = m0t[b_m:b_m + 4, d:d + 1] if is_sample else 0.0
                    P.add("vector", lambda e, o=dirv(T_m), a=dirv(T_lf), b=dirv(T_li), i_=init: e.tensor_tensor_scan(
                        out=o, data0=a, data1=b, initial=i_, op0=ALU.add, op1=ALU.max), [k_lf, k_li, "m0t"], [k_m])
                    yield
                tt("vector", U[:, :], T_B[:, :], T_m[:, :], ALU.subtract, [k_B, k_m], [k_U])
                yield
                tt("vector", Gg[:, :], T_li[:, :], T_B[:, :], ALU.subtract, [k_li, k_B], [k_G])
                yield
                for si, (s0, s1) in enumerate(segs):
                    nch = (s1 - s0) // 128
                    order = list(range(nch)) if d == 0 else list(range(nch - 1, -1, -1))
                    for oi, c in enumerate(order):
                        a0 = s0 + c * 128
                        a1 = a0 + 128
                        endi = (a1 - 1) if d == 0 else a0
                        if oi == 0:
                            if is_sample:
                                ts("vector", T_li[:, a0:a1], U[:, a0:a1], m0t[b_li:b_li + 4, d:d + 1], ALU.add, [k_U, "m0t"], [k_li])
                                yield
                            else:
                                cp("vector", T_li[:, a0:a1], U[:, a0:a1], [k_U], [k_li])
                                yield
                        else:
                            pc = order[oi - 1]
                            pend = (s0 + pc * 128 + 127) if d == 0 else (s0 + pc * 128)
                            ts("vector", T_li[:, a0:a1], U[:, a0:a1], U[:, pend:pend + 1], ALU.subtract, [k_U], [k_li])
                            yield
                        ts("vector", T_lf[:, a0:a1], Gg[:, a0:a1], U[:, endi:endi + 1], ALU.add, [k_G, k_U], [k_lf])
                        yield
                act(T_li[:, :], T_li[:, :], AF.Exp, [k_li], [k_li])
                yield
                cp("vector", G["WIb"][d][0][:, :], T_li[:, :], [k_li], [G["WIb"][d][2]])
                yield
                act(T_lf[:, :], T_lf[:, :], AF.Exp, [k_lf], [k_lf])
                yield
                act(T_B[:, :], T_m[:, :], AF.Exp, [k_m], [k_B], scale=-1.0)
                yield
                for si, (s0, s1) in enumerate(segs):
                    nch = (s1 - s0) // 128
                    e0 = (s0 + 127) if d == 0 else s0
                    cp("vector", w0r[d * 32:d * 32 + 4, s0 // 128:s0 // 128 + nch], T_li[:, e0:s1:128], [k_li], [("w0r", d)])
                    yield
                    if not is_sample:
                        mi = (s1 - 1) if d == 0 else s0
                        dma("sync", D["m_o"][j, si, d, :].rearrange("(p o) -> p o", o=1), T_m[:, mi:mi + 1], [k_m], [], semkey=("mout", d, si), is_out=True)
                        yield
                for cg in range(nchtot):
                    ps, pk = ps_next(ALLPS)
                    for qq, (X, bX, kX) in enumerate([(T_li, b_li, k_li), (T_B, b_B, k_B), (T_lf, b_lf, k_lf), (Gg, b_G, k_G)]):
                        qi = d * 4 + qq
                        mm(ps[:, qi * 4:(qi + 1) * 4], X[:, cg * 128:(cg + 1) * 128], identF[bX:bX + 4, bX:bX + 4], True, True,
                           [kX, "identF"], [pk])
                    cp("vector", cols[:, cg, d * 16:(d + 1) * 16], ps[:, d * 16:(d + 1) * 16], [pk], [("cols", d)])
                    yield

            gens = [gdir(0), gdir(1)]
            alive = [True, True]
            while any(alive):
                for gi_ in range(2):
                    if alive[gi_]:
                        try:
                            next(gens[gi_])
                        except StopIteration:
                            alive[gi_] = False

        def b_unit(j, hb, t0, TP, segs, is_sample, cond):
            P.phase = "B_proj%d" % int(is_sample)
            arena_reset(keep_base=True)
            nqt = TP // 128
            nblk = TP // 512
            nchtot = nqt
            cols = G["cols"]
            w0r = G["w0r"]
            Hsum = carveR([nqt, 256])
            Hsum32 = Hsum.bitcast(F32)
            xbufs = [carve([TP]) for _ in range(2)]
            accs = [carve([TP]) for _ in range(2)]
            qT = carve([2, TP], BF16)
            kT = carve([2, TP], BF16)
            Vaug = carve([nqt, 258], BF16)
            nseg = len(segs)
            C32s = [[carve([2, 258]) for _ in range(2)] for _ in range(nseg)]
            Cbs = [[[carve([2, 258], BF16) for _ in range(2)] for _ in range(2)] for _ in range(nseg)]
            W0bc = [carve([nchtot]) for _ in range(2)]
            nrot = 4 * nseg
            DT = [carve([128], BF16) for _ in range(nrot)]
            PT = [carve([128], BF16) for _ in range(nrot)]
            kw = [carve([256], BF16) for _ in range(nrot)]
            qw = [carve([2, 128], BF16) for _ in range(nrot)]
            ktok_all = xbufs[0].bitcast(BF16).rearrange("p (a b) -> p a b", a=nqt)
            dd = [carve([2]) for _ in range(2 * nseg)]
            hk = lambda kc, blk: ("h", kc, blk)
            memset("gpsimd", Vaug[:, :, 256:257], 1.0, [("V", i) for i in range(nqt)])
            for si in range(nseg):
                for d in range(2):
                    ck = ("C32", si, d)
                    memset("gpsimd", C32s[si][d][:, :, :], 0.0, [ck])
                    if is_sample:
                        for dc in range(2):
                            dma("sync", C32s[si][d][:, dc, 0:256], D["C0%d" % j][d, hb, dc * 128:(dc + 1) * 128, :], (), [ck], semkey=("C0ld", d))
                            dma("sync", C32s[si][d][:, dc, 256:257],
                                D["n0%d" % j][d, hb, dc * 128:(dc + 1) * 128].rearrange("(p o) -> p o", o=1), (), [ck], semkey=("C0ld", d))
            wqk, wk = w_pop("Bqk")
            def conv_a(c4):
                isk = c4 >= 2
                c = c4 % 2
                ci = (8 if isk else 0) + 2 * hb + c
                xbuf, acc = xbufs[c4 % 2], accs[c4 % 2]
                xk, ak = ("xbuf", c4 % 2), ("acc", c4 % 2)
                for blk in range(nblk):
                    c0, c1 = blk * 512, (blk + 1) * 512
                    ps, pk = ps_next(ALLPS)
                    for kc in range(8):
                        mm(ps[:, :], wqk[:, kc, c4 * 128:(c4 + 1) * 128], hT[:, kc, c0:c1], kc == 0, kc == 7, [wk, hk(kc, blk)], [pk])
                    cp("scalar", xbuf[:, c0:c1], ps[:, :], [pk], [xk])
                for (s0, s1) in segs:
                    ts("gpsimd", acc[:, s0:s1], xbuf[:, s0:s1], convt[:, ci, 1:2], ALU.mult, [xk, "convt"], [ak], s2=0.0, op1=ALU.add)
                    stt(acc[:, s0 + 1:s1], xbuf[:, s0:s1 - 1], convt[:, ci, 0:1], acc[:, s0 + 1:s1], ALU.mult, ALU.add,
                        [xk, "convt", ak], [ak])
                    stt(acc[:, s0:s1 - 1], xbuf[:, s0 + 1:s1], convt[:, ci, 2:3], acc[:, s0:s1 - 1], ALU.mult, ALU.add,
                        [xk, "convt", ak], [ak])

            def conv_b(c4):
                isk = c4 >= 2
                c = c4 % 2
                xbuf, acc = xbufs[c4 % 2], accs[c4 % 2]
                xk, ak = ("xbuf", c4 % 2), ("acc", c4 % 2)
                act(xbuf[:, :], acc[:, :], AF.Tanh, [ak], [xk], scale=0.5)
                sf = (1.0 / 32.0) if isk else 0.5
                ts("gpsimd", xbuf[:, :], xbuf[:, :], sf, ALU.mult, [xk], [xk], s2=sf, op1=ALU.add)
                dst = kT if isk else qT
                tt("vector", dst[:, c, :], xbuf[:, :], acc[:, :], ALU.mult, [xk, ak], [("kT" if isk else "qT", c)])

            conv_a(0)
            for c4 in range(4):
                if c4 + 1 < 4:
                    conv_a(c4 + 1)
                conv_b(c4)
            wv, wvk = w_pop("Bv")
            for qt in range(nqt):
                ps, pk = ps_next(ALLPS)
                for kc in range(8):
                    mm(ps[:, 0:256], hT[:, kc, qt * 128:(qt + 1) * 128], wv[:, kc, 0:256], kc == 0, kc == 7, [hk(kc, qt // 4), wvk], [pk])
                cp("scalar", Vaug[:, qt, 0:256], ps[:, 0:256], [pk], [("V", qt)])
            for qt in range(nqt):
                ps, pk = ps_next(ALLPS)
                for dc in range(2):
                    mm(ps[:, dc * 128:(dc + 1) * 128], kT[:, dc, qt * 128:(qt + 1) * 128], identB[:, :], True, True, [("kT", dc), "identB"], [pk])
                cp("scalar" if qt % 2 else "vector", ktok_all[:, qt, :], ps[:, 0:256], [pk], [("ktok", qt), ("xbuf", 0)])
            for d in range(2):
                ps, pk = ps_next(ALLPS)
                mm(ps[:, 0:nchtot], selF[d * 32:d * 32 + 4, hb * 128:(hb + 1) * 128], w0r[d * 32:d * 32 + 4, 0:nchtot], True, True,
                   ["selF", ("w0r", d)], [pk])
                cp("vector", W0bc[d][:, :], ps[:, 0:nchtot], [pk], [("W0bc", d)])
            P.phase = "B_chunks%d" % int(is_sample)
            hs_sets = [set() for _ in segs]
            for si in range(nseg):
                for d in range(2):
                    cp("scalar", Cbs[si][d][1][:, :, :], C32s[si][d][:, :, :], [("C32", si, d)], [("Cb", si, d, 1)])
            if True:
                def step_info(si, oi):
                    s0, s1 = segs[si]
                    nch = (s1 - s0) // 128
                    orders = [list(range(nch)), list(range(nch - 1, -1, -1))]
                    par = oi % 2
                    info = []
                    for d in range(2):
                        c = orders[d][oi]
                        a0 = s0 + c * 128
                        info.append((d, a0, a0 + 128, a0 // 128, (is_sample and oi == nch - 1), si * 4 + d * 2 + par))
                    return par, info

                def front(si, oi):
                    par, info = step_info(si, oi)
                    banks = []
                    for (d, a0, a1, cg, skip, r) in info:
                        U, b_U, k_U = G["U"][d]
                        WI, b_W, k_W = G["WIb"][d]
                        psx, pkx = ps_next(ALLPS)
                        banks.append((psx, pkx))
                        for dc in range(2):
                            mm(psx[:, 0:128], kT[:, dc, a0:a1], qT[:, dc, a0:a1], dc == 0, dc == 1, [("kT", dc), ("qT", dc)], [pkx])
                        mm(psx[:, 128:256], selF[b_U:b_U + 4, hb * 128:(hb + 1) * 128], U[:, a0:a1], True, False, [k_U, "selF"], [pkx])
                        mm(psx[:, 128:256], identB[:, :], maskFB[:, d * 128:(d + 1) * 128], False, True, ["identB", "maskFB"], [pkx])
                        mm(psx[:, 256:384], identB[b_W:b_W + 4, b_W + hb:b_W + hb + 1].to_broadcast([4, 128]), WI[:, a0:a1], True, True,
                           [k_W, "identB"], [pkx])
                    for (d, a0, a1, cg, skip, r), (psx, pkx) in zip(info, banks):
                        colb = d * 16
                        act(DT[r][:, :], psx[:, 128:256], AF.Exp, [pkx, ("cols", d)], [("DT", r)],
                            bias=cols[:, cg, colb + 12 + hb:colb + 12 + hb + 1])
                        if not skip:
                            ts("gpsimd", kw[r][:, :], ktok_all[:, cg, :], cols[:, cg, colb + 8 + hb:colb + 8 + hb + 1], ALU.mult,
                               [("ktok", cg), ("cols", d)], [("kw", r)], s2=0.0, op1=ALU.add)
                        tt("vector", qw[r][:, :, :], qT[:, :, a0:a1], psx[:, 256:384].unsqueeze(1).to_broadcast([128, 2, 128]), ALU.mult,
                           [("qT", 0), ("qT", 1), pkx], [("qw", r)])
                    for (d, a0, a1, cg, skip, r), (psx, pkx) in zip(info, banks):
                        tt("vector", PT[r][:, :], psx[:, 0:128], DT[r][:, :], ALU.mult, [pkx, ("DT", r)], [("PT", r)])

                def back(si, oi):
                    par, info = step_info(si, oi)
                    C32 = C32s[si]
                    Cb = Cbs[si]
                    hs_written = hs_sets[si]
                    res = []
                    for (d, a0, a1, cg, skip, r) in info:
                        ps3, pk3 = ps_next(ALLPS)
                        mm(ps3[:, 0:257], PT[r][:, :], Vaug[:, cg, 0:257], True, False, [("PT", r), ("V", cg)], [pk3])
                        for dc in range(2):
                            mm(ps3[:, 0:257], qw[r][:, dc, :], Cb[d][1 - par][:, dc, 0:257], False, dc == 1,
                               [("qw", r), ("Cb", si, d, 1 - par)], [pk3])
                        psd, pkd = (None, None)
                        if not skip:
                            for dc in range(2):
                                mm(ps3[:, 384 + dc:385 + dc], kw[r][:, dc * 128:(dc + 1) * 128], Vaug[:, cg, 256:257], True, True,
                                   [("kw", r), ("V", cg)], [pk3])
                            psd, pkd = ps_next(ALLPS)
                            for dc in range(2):
                                mm(psd[:, dc * 256:(dc + 1) * 256], kw[r][:, dc * 128:(dc + 1) * 128], Vaug[:, cg, 0:256], True, True,
                                   [("kw", r), ("V", cg)], [pkd])
                        res.append((ps3, pk3, psd, pkd))
                    for (d, a0, a1, cg, skip, r), (ps3, pk3, psd, pkd) in zip(info, res):
                        if skip:
                            continue
                        ck = ("C32", si, d)
                        stt(C32[d][:, :, 0:256], C32[d][:, :, 0:256], W0bc[d][:, cg:cg + 1],
                            psd[:, :].rearrange("p (a b) -> p a b", a=2), ALU.mult, ALU.add, [ck, ("W0bc", d), pkd], [ck])
                        stt(C32[d][:, :, 256], C32[d][:, :, 256], W0bc[d][:, cg:cg + 1], ps3[:, 384:386], ALU.mult, ALU.add,
                            [ck, ("W0bc", d), pk3], [ck])
                        cp("scalar", Cb[d][par][:, :, :], C32[d][:, :, :], [ck], [("Cb", si, d, par)])
                    for (d, a0, a1, cg, skip, r), (ps3, pk3, psd, pkd) in zip(info, res):
                        colb = d * 16
                        ts("vector", dd[si * 2 + d][:, 0:1], ps3[:, 256:257], cols[:, cg, colb + 4 + hb:colb + 4 + hb + 1], ALU.max,
                           [pk3, ("cols", d)], [("dd", si, d)])
                        stt(dd[si * 2 + d][:, 0:1], ps3[:, 256:257], -1.0, dd[si * 2 + d][:, 0:1], ALU.mult, ALU.max, [pk3, ("dd", si, d)], [("dd", si, d)])
                        P.add("vector", lambda e, o=dd[si * 2 + d][:, 1:2], i=dd[si * 2 + d][:, 0:1]: e.reciprocal(o, i), [("dd", si, d)], [("dd", si, d)])
                        if cg not in hs_written:
                            hs_written.add(cg)
                            act(Hsum[:, cg, :], ps3[:, 0:256], AF.Copy, [pk3, ("dd", si, d)], [("Hs", cg)], scale=dd[si * 2 + d][:, 1:2])
                        else:
                            stt(Hsum[:, cg, :], ps3[:, 0:256], dd[si * 2 + d][:, 1:2], Hsum32[:, cg, :], ALU.mult, ALU.add,
                                [pk3, ("dd", si, d), ("Hs", cg)], [("Hs", cg)])

                def seg_pipe(si):
                    nch_ = (segs[si][1] - segs[si][0]) // 128
                    front(si, 0)
                    yield
                    for oi in range(nch_):
                        if oi + 1 < nch_:
                            front(si, oi + 1)
                            yield
                        back(si, oi)
                        yield
                    if not is_sample:
                        for d in range(2):
                            ck = ("C32", si, d)
                            for dc in range(2):
                                dma("sync", D["C_o"][j, si, d, hb, dc * 128:(dc + 1) * 128, :], C32s[si][d][:, dc, 0:256], [ck], [],
                                    semkey=("Cout", si, d), is_out=True)
                                dma("sync", D["n_o"][j, si, d, hb, dc * 128:(dc + 1) * 128].rearrange("(p o) -> p o", o=1),
                                    C32s[si][d][:, dc, 256:257], [ck], [], semkey=("Cout", si, d), is_out=True)

                gens_ = [seg_pipe(si) for si in range(nseg)]
                alive_ = [True] * nseg
                while any(alive_):
                    for gi_ in range(nseg):
                        if alive_[gi_]:
                            try:
                                next(gens_[gi_])
                            except StopIteration:
                                alive_[gi_] = False
            P.phase = "B_epi%d" % int(is_sample)
            arena_reset(keep_base=True, keepR=True)
            og = carve([nblk, 2, 512])
            zsn = carve([nblk, 2, 512])
            hgT = carve([nblk, 2, 512])
            tq = [carve([512]) for _ in range(2)]
            lnb = carve([nblk, 512])
            rstd = carve([nblk, 512])
            sq = carveR([nblk, 2, 512])
            boT = carveR([2, TP])
            woz, wozk = w_pop("Boz")
            it = 0
            for blk in range(nblk):
                c0, c1 = blk * 512, (blk + 1) * 512
                for c in range(2):
                    ps, pk = ps_next(ALLPS)
                    for kc in range(8):
                        mm(ps[:, :], woz[:, kc, c * 128:(c + 1) * 128], hT[:, kc, c0:c1], kc == 0, kc == 7, [wozk, hk(kc, blk)], [pk])
                    act(og[:, blk, c, :], ps[:, :], AF.Tanh, [pk], [("og", blk, c)], scale=0.5)
                    ts("gpsimd", og[:, blk, c, :], og[:, blk, c, :], 0.5, ALU.mult, [("og", blk, c)], [("og", blk, c)], s2=0.5, op1=ALU.add)
                    ps2, pk2 = ps_next(ALLPS)
                    for kc in range(8):
                        mm(ps2[:, :], woz[:, kc, 256 + c * 128:256 + (c + 1) * 128], hT[:, kc, c0:c1], kc == 0, kc == 7,
                           [wozk, hk(kc, blk)], [pk2])
                    t_ = tq[it % 2]
                    tk = ("tq", it % 2)
                    it += 1
                    act(t_[:, :], ps2[:, :], AF.Tanh, [pk2], [tk], scale=0.5)
                    ts("gpsimd", t_[:, :], t_[:, :], 0.5, ALU.mult, [tk], [tk], s2=0.5, op1=ALU.add)
                    nbc = smallc[:, 8 + 2 * hb + c:8 + 2 * hb + c + 1]
                    stt(zsn[:, blk, c, :], t_[:, :], nbc, ps2[:, :], ALU.mult, ALU.mult, [tk, "normbt", pk2], [("zsn", blk, c)])
            wo, wok = w_pop("Bo")
            for blk in range(nblk):
                for c in range(2):
                    ps, pk = ps_next(ALLPS)
                    for i in range(4):
                        qt = blk * 4 + i
                        mm(ps[:, i * 128:(i + 1) * 128], Hsum[:, qt, c * 128:(c + 1) * 128], identR[:, :], True, True,
                           [("Hs", qt), "identR"], [pk])
                    tt("vector", hgT[:, blk, c, :], ps[:, :], og[:, blk, c, :], ALU.mult, [pk, ("og", blk, c)], [("hgT", blk, c)])
            pss = []
            for blk in range(nblk):
                for c in range(2):
                    act(sq[:, blk, c, :], hgT[:, blk, c, :], AF.Square, [("hgT", blk, c)], [("sq", blk, c)])
                ps2, pk2 = ps_next(ALLPS)
                for c in range(2):
                    mm(ps2[:, :], onesR[:, :], sq[:, blk, c, :], c == 0, c == 1, ["onesR", ("sq", blk, c)], [pk2])
                pss.append((ps2, pk2))
            for blk in range(nblk):
                ps2, pk2 = pss[blk]
                act(lnb[:, blk, :], ps2[:, :], AF.Ln, [pk2, "epsb"], [("lnb", blk)], scale=1.0 / 256.0, bias=epsb[:, 0:1])
            for blk in range(nblk):
                act(rstd[:, blk, :], lnb[:, blk, :], AF.Exp, [("lnb", blk)], [("rstd", blk)], scale=-0.5)
            for blk in range(nblk):
                c0, c1 = blk * 512, (blk + 1) * 512
                for c in range(2):
                    tt("vector", hgT[:, blk, c, :], hgT[:, blk, c, :], rstd[:, blk, :], ALU.mult, [("hgT", blk, c), ("rstd", blk)], [("hgT", blk, c)])
                    tt("vector", boT[:, c, c0:c1], hgT[:, blk, c, :], zsn[:, blk, c, :], ALU.mult, [("hgT", blk, c), ("zsn", blk, c)], [("boT", c, blk)])
            out_proj2(wo, wok, boT, lambda fc, blk: ("boT", fc, blk), t0, TP, cond)


        CS = {}

        def c_pre(j, t0, TP, segs, is_sample):
            P.phase = "C_pre%d" % int(is_sample)
            arena_reset()
            nblk = TP // 512
            NK = TP + (256 if is_sample else 0)
            qnT = carveR([3, TP])
            ckvT = carveR([2, NK])
            KR = carve([NK], BF16)
            CS["qnT"], CS["ckvT"], CS["KR"], CS["NK"] = qnT, ckvT, KR, NK
            ar["base"] = ar["off"]
            ar["baseR"] = ar["offR"]
            sq = [carveR([512]) for _ in range(2)]
            lnb = carve([512])
            rstd = carve([512])
            tmb = [carve([512]) for _ in range(2)]
            stg = [carve([512]) for _ in range(2)]
            ta = [carve([512]) for _ in range(2)]
            tb = [carve([512]) for _ in range(2)]
            hk = lambda kc, blk: ("h", kc, blk)
            it = 0
            for (wname, nch, nfeat, ncol0, dst, dkey, is_kv) in (("Cqa", 3, 384.0, 16, qnT, "qnT", False), ("Ckva", 2, 256.0, 24, ckvT, "ckvT", True)):
                w, wk = w_pop(wname)
                for blk in range(nblk):
                    c0, c1 = blk * 512, (blk + 1) * 512
                    pss = []
                    for c in range(nch):
                        ps, pk = ps_next(ALLPS)
                        for kc in range(8):
                            mm(ps[:, :], w[:, kc, c * 128:(c + 1) * 128], hT[:, kc, c0:c1], kc == 0, kc == 7, [wk, hk(kc, blk)], [pk])
                        pss.append((ps, pk))
                    ps_s, pk_s = ps_next(ALLPS)
                    for c in range(nch):
                        act(sq[c % 2][:, :], pss[c][0][:, :], AF.Square, [pss[c][1]], [("sqc", c % 2)])
                        mm(ps_s[:, :], onesR[:, :], sq[c % 2][:, :], c == 0, c == nch - 1, ["onesR", ("sqc", c % 2)], [pk_s])
                    rstd_from_sumsq(ps_s[:, :], pk_s, nfeat, rstd[:, :], "rstd", 512, lnb[:, :], "lnb")
                    for c in range(nch):
                        t_ = tmb[it % 2]
                        tk = ("tmb", it % 2)
                        it += 1
                        tt("vector", t_[:, :], pss[c][0][:, :], rstd[:, :], ALU.mult, [pss[c][1], "rstd"], [tk])
                        nrm = smallc[:, ncol0 + c:ncol0 + c + 1]
                        act(dst[:, c, c0:c1], t_[:, :], AF.Copy, [tk, "qnormt", "kvnormt"], [(dkey, c, blk)], scale=nrm)
                        if is_kv and not is_sample:
                            sg = stg[c % 2]
                            act(sg[:, :], t_[:, :], AF.Copy, [tk, "kvnormt"], [("stgc", c % 2)], scale=nrm)
                            dma("sync", D["ckvT_o"][j, c * 128:(c + 1) * 128, c0:c1], sg[:, :], [("stgc", c % 2)], [],
                                semkey=("stgc", c % 2), is_out=True)
                if is_kv and is_sample:
                    for c in range(2):
                        dma("gpsimd", ckvT[:, c, TP:TP + 256], D["cckvT%d" % j][c * 128:(c + 1) * 128, :], (), [("ckvT", c, "ctx")])
            wkr, wkrk = w_pop("Ckr")
            for blk in range(nblk):
                cols = (blk * 512, (blk + 1) * 512)
                ps, pk = ps_next(ALLPS)
                for kc in range(8):
                    mm(ps[0:96, :], wkr[:, kc, 0:96], hT[:, kc, cols[0]:cols[1]], kc == 0, kc == 7, [wkrk, hk(kc, blk)], [pk])
                if is_sample:
                    psp, pkp = ps_next(ALLPS)
                    for kc in range(8):
                        mm(psp[0:96, :], wkr[:, kc, 96:192], hT[:, kc, cols[0]:cols[1]], kc == 0, kc == 7, [wkrk, hk(kc, blk)], [pkp])
                    rope_evac(ps, pk, psp, pkp, 64, 96, cols, KR[64:96, cols[0]:cols[1]], ("KR", blk), ta[blk % 2], tb[blk % 2],
                              ("ta", blk % 2), ("tb", blk % 2))
                else:
                    cp("scalar", KR[64:96, cols[0]:cols[1]], ps[64:96, :], [pk], [("KR", blk)])
                    cp("vector", stg[0][64:96, :], ps[64:96, :], [pk], [("stgc", 0)])
                    dma("sync", D["krT_o"][j, :, cols[0]:cols[1]], stg[0][64:96, :], [("stgc", 0)], [], semkey=("stgc", 0), is_out=True)
            if is_sample:
                dma("sync", stg[1][64:96, 0:256], D["ckrT%d" % j], (), [("stgc", 1)])
                cp("vector", KR[64:96, TP:TP + 256], stg[1][64:96, 0:256], [("stgc", 1)], [("KR", "ctx")])

        def c_unit(j, g, t0, TP, segs, is_sample, cond):
            arena_reset(keep_base=True, barrier=(g == 0))
            P.phase = "C_proj%d" % int(is_sample)
            qnT, ckvT, KR, NK = CS["qnT"], CS["ckvT"], CS["KR"], CS["NK"]
            nqt = TP // 128
            nblk = TP // 512
            nkt = NK // 128
            QT = [carve([TP], BF16) for _ in range(4)]
            KTh = [carve([NK], BF16) for _ in range(4)]
            Vaug = carve([nkt, 4, 65], BF16)
            zs = carve([nqt, 256], BF16)
            aoT = carveR([2, TP])
            ta = [carve([512]) for _ in range(2)]
            tb = [carve([512]) for _ in range(2)]
            tz = [carve([256]) for _ in range(2)]
            alloc_attn_rot()
            hk = lambda kc, blk: ("h", kc, blk)
            memset("gpsimd", Vaug[:, :, :, 64:65], 1.0, [("V", i) for i in range(nkt)])
            for hl in range(4):
                memset("gpsimd", QT[hl][64:128, :], 0.0, [("QTr", hl, b) for b in range(nblk)] + [("QTz", hl)])
                memset("gpsimd", KTh[hl][64:128, :], 0.0, [("KThr", hl)])
            qn_keys = lambda blk: [("qnT", c, blk) for c in range(3)]
            wq, wqk = w_pop("Cqb")
            it = 0
            for hl in range(4):
                for blk in range(nblk):
                    cols = (blk * 512, (blk + 1) * 512)
                    ps, pk = ps_next(ALLPS)
                    for kc in range(3):
                        mm(ps[0:96, :], wq[:, kc, hl * 192:hl * 192 + 96], qnT[:, kc, cols[0]:cols[1]], kc == 0, kc == 2,
                           [wqk] + qn_keys(blk), [pk])
                    if is_sample:
                        psp, pkp = ps_next(ALLPS)
                        for kc in range(3):
                            mm(psp[0:96, :], wq[:, kc, hl * 192 + 96:hl * 192 + 192], qnT[:, kc, cols[0]:cols[1]], kc == 0, kc == 2,
                               [wqk] + qn_keys(blk), [pkp])
                        cp("scalar", QT[hl][0:64, cols[0]:cols[1]], ps[0:64, :], [pk], [("QT", hl, blk)])
                        rope_evac(ps, pk, psp, pkp, 64, 96, cols, QT[hl][64:96, cols[0]:cols[1]], ("QTr", hl, blk),
                                  ta[it % 2], tb[it % 2], ("ta", it % 2), ("tb", it % 2))
                        it += 1
                    else:
                        cp("scalar", QT[hl][0:64, cols[0]:cols[1]], ps[0:64, :], [pk], [("QT", hl, blk)])
                        cp("vector", QT[hl][64:96, cols[0]:cols[1]], ps[64:96, :], [pk], [("QTr", hl, blk)])
            wkv, wkvk = w_pop("Ckvb")
            ck_keys = lambda k0, k1: [("ckvT", c, b) for c in range(2) for b in
                                      sorted(set(["ctx" if kk >= TP else kk // 512 for kk in (k0, k1 - 1)]))]
            for hl in range(4):
                k0 = 0
                while k0 < NK:
                    k1 = min(NK, k0 + 512)
                    n_ = k1 - k0
                    ps, pk = ps_next(ALLPS)
                    for kc in range(2):
                        mm(ps[0:64, 0:n_], wkv[:, kc, hl * 128:hl * 128 + 64], ckvT[:, kc, k0:k1], kc == 0, kc == 1,
                           [wkvk] + ck_keys(k0, k1), [pk])
                    cp("vector" if (k0 // 512) % 2 else "scalar", KTh[hl][0:64, k0:k1], ps[0:64, 0:n_], [pk], [("KTh", hl)])
                    k0 = k1
                cp("vector", KTh[hl][64:96, :], KR[64:96, :], [("KR", b) for b in range(nblk)] + ([("KR", "ctx")] if is_sample else []),
                   [("KThr", hl)])
            wv3 = wkv.rearrange("p k (h c) -> p k h c", c=128)
            for kt in range(nkt):
                ps, pk = ps_next(ALLPS)
                for kc in range(2):
                    mm(ps[:, 0:256], ckvT[:, kc, kt * 128:(kt + 1) * 128], wv3[:, kc, :, 64:128], kc == 0, kc == 1,
                       [wkvk] + ck_keys(kt * 128, (kt + 1) * 128), [pk])
                cp("scalar", Vaug[:, kt, :, 0:64], ps[:, 0:256].rearrange("p (h c) -> p h c", c=64), [pk], [("V", kt)])
            wz, wzk = w_pop("Cz")
            for qt in range(nqt):
                ps, pk = ps_next(ALLPS)
                for kc in range(8):
                    mm(ps[:, 0:256], hT[:, kc, qt * 128:(qt + 1) * 128], wz[:, kc, 0:256], kc == 0, kc == 7, [hk(kc, qt // 4), wzk], [pk])
                silu_tok(ps[:, 0:256], pk, zs[:, qt, :], ("zs", qt), tz[qt % 2][:, :], ("tz", qt % 2), 256)
            P.phase = "C_attn%d" % int(is_sample)
            for hl in range(4):
                if is_sample:
                    jobs = [(kt * 128, 0, 8, None, kt) for kt in range(nkt)]
                else:
                    jobs = []
                    for s_ in range(2):
                        for jt in (2 * s_, 2 * s_ + 1):
                            jobs.append((jt * 128, 2 * s_, 2 * s_ + 2, None, jt))
                attn_head(QT[hl], lambda q0, q1, hl=hl: [k_ for b in range(q0 // 4, (q1 - 1) // 4 + 1)
                                                          for k_ in (("QT", hl, b), ("QTr", hl, b), ("QTz", hl))],
                          0, 128, KTh[hl], lambda kc0, hl=hl: [("KTh", hl), ("KThr", hl)],
                          jobs, lambda vid, hl=hl: (Vaug[:, vid, hl, :], ("V", vid)), 96.0 ** -0.5, None,
                          zs, lambda qt: ("zs", qt), hl, nqt, (hl % 2) * 2)
            P.phase = "C_out%d" % int(is_sample)
            transpose_to_aoT(zs, lambda qt: ("zs", qt), aoT, lambda c, b: ("aoT", c, b), nqt)
            out_proj("Co", aoT, lambda fc, blk: ("aoT", fc, blk), t0, TP, cond)

        run_layers(BUILD_WHICH[0])
        P.emit()
        LAST_PROG[0] = P
    return nc


LAST_PROG = [None]
BUILD_WHICH = ["ABC"]
FLAGS = {}
NCORES = [8]
DBG_SPECS = {}
DBG_OUT = {}
_CACHE = {}


def kernel(**inputs):
    inp = {k: np.asarray(v) for k, v in inputs.items()}
    sh, per = prep_inputs(inp)
    specs = input_specs()
    key = "full"
    if key not in _CACHE:
        _CACHE[key] = build()
    nc = _CACHE[key]
    in_maps = []
    for i in range(NCORES[0]):
        d = {}
        for name in specs:
            a = per[i][name] if name in per[i] else sh[name]
            a = np.ascontiguousarray(a, dtype=np.float32)
            assert list(a.shape) == specs[name][0], (name, a.shape, specs[name][0])
            d[name] = a
        in_maps.append(d)
    res = run_bass_kernel_spmd(nc, in_maps, core_ids=list(range(NCORES[0])))
    R = res.results
    for name in DBG_SPECS:
        DBG_OUT[name] = [np.asarray(R[i][name]) for i in range(NCORES[0])]
    f32 = np.float32
    y_prompt = np.zeros((16, 256, 1024), f32)
    y_sample = np.zeros((8, 1024, 1024), f32)
    a_k = np.zeros((16, 2, 256, 4, 64), f32)
    a_v = np.zeros((16, 2, 256, 4, 64), f32)
    b_mem = np.zeros((16, 2, 2, 4, 256, 256), f32)
    b_nrm = np.zeros((16, 2, 2, 4, 256), f32)
    b_max = np.zeros((16, 2, 2, 4), f32)
    c_kv = np.zeros((16, 2, 256, 256), f32)
    c_kr = np.zeros((16, 2, 256, 32), f32)
    for i in range(NCORES[0]):
        r = R[i]
        yT = np.asarray(r["yT_o"])
        y_sample[i] = yT[:, 0:1024].T
        for s in range(2):
            b = 2 * i + s
            y_prompt[b] = yT[:, 1024 + 256 * s:1024 + 256 * (s + 1)].T
            for j in range(2):
                a_k[b, j] = np.asarray(r["kaT_o"])[j][:, 256 * s:256 * (s + 1)].T.reshape(256, 4, 64)
                a_v[b, j] = np.asarray(r["va_o"])[j][256 * s:256 * (s + 1), :].reshape(256, 4, 64)
                b_mem[b, j] = np.asarray(r["C_o"])[j, s]
                b_nrm[b, j] = np.asarray(r["n_o"])[j, s]
                b_max[b, j] = np.asarray(r["m_o"])[j, s]
                c_kv[b, j] = np.asarray(r["ckvT_o"])[j][:, 256 * s:256 * (s + 1)].T
                c_kr[b, j] = np.asarray(r["krT_o"])[j][:, 256 * s:256 * (s + 1)].T
    return (y_prompt, y_sample, a_k, a_v, b_mem, b_nrm, b_max, c_kv, c_kr)
```

```python
import contextlib
import numpy as np
import concourse.bass as bass
import concourse.mybir as mybir
from concourse.bass_utils import run_bass_kernel_spmd

F32 = mybir.dt.float32
F32R = mybir.dt.float32r
BF16 = mybir.dt.bfloat16
ALU = mybir.AluOpType
AF = mybir.ActivationFunctionType
AX = mybir.AxisListType

ENGS = ["sync", "scalar", "vector", "gpsimd", "tensor"]


class Op:
    __slots__ = ("eng", "fn", "deps", "dma", "semkey", "mark", "dmacount", "seq", "phase")

    def __init__(self, eng, fn, dma, semkey):
        self.eng = eng
        self.fn = fn
        self.deps = []
        self.dma = dma
        self.semkey = semkey
        self.mark = None
        self.dmacount = None
        self.seq = None


class Prog:
    def __init__(self, nc):
        self.nc = nc
        self.streams = {e: [] for e in ENGS}
        self.last_writer = {}
        self.readers = {}
        self.all_ops = []
        self.dma_counts = {}
        self.pending_barrier = {e: [] for e in ENGS}
        self.out_dmas = []
        self.phase = "init"

    def add(self, eng, fn, reads=(), writes=(), dma=False, semkey=None, is_out=False):
        if dma and semkey is None:
            semkey = writes[0] if writes else reads[0]
        o = Op(eng, fn, dma, semkey)
        o.phase = self.phase
        deps = {}

        def add_dep(p, raw):
            if p is None or p is o:
                return
            if (not raw) and (not p.dma) and (not dma) and p.eng == eng and eng == "tensor":
                return
            deps[id(p)] = p

        for k in reads:
            add_dep(self.last_writer.get(k), True)
            if isinstance(k, tuple) and k[0] == "ps":
                for r in self.readers.get(k, ()):
                    if r.eng != eng:
                        add_dep(r, True)
        for k in writes:
            add_dep(self.last_writer.get(k), False)
            for r in self.readers.get(k, ()):
                add_dep(r, False)
        for p in self.pending_barrier[eng]:
            add_dep(p, True)
        self.pending_barrier[eng] = []
        for k in writes:
            self.last_writer[k] = o
            self.readers[k] = []
        for k in reads:
            self.readers.setdefault(k, []).append(o)
        o.deps = list(deps.values())
        if dma:
            c = self.dma_counts.get(semkey, 0) + 1
            self.dma_counts[semkey] = c
            o.dmacount = c
            if is_out:
                self.out_dmas.append(o)
        o.seq = len(self.all_ops)
        self.all_ops.append(o)
        self.streams[eng].append(o)
        return o

    def barrier(self):
        lasts = []
        for e in ENGS:
            s = self.streams[e]
            if s:
                lasts.append(s[-1])
        dm = {}
        for o in self.all_ops:
            if o.dma:
                dm[o.semkey] = o
        lasts += list(dm.values())
        for e in ENGS:
            self.pending_barrier[e] = list(lasts)

    def emit(self):
        nc = self.nc
        self.barrier()
        self.add("sync", lambda e: e.nop(), reads=(), writes=())
        for o in self.all_ops:
            for p in o.deps:
                if not p.dma:
                    p.mark = True
        counts = {e: 0 for e in ENGS}
        for e in ENGS:
            for o in self.streams[e]:
                if o.mark:
                    counts[e] += 1
                    o.mark = counts[e]
        with contextlib.ExitStack() as st:
            esem = {e: st.enter_context(nc.semaphore("sem_" + e)) for e in ENGS}
            dsem = {}
            for i, k in enumerate(self.dma_counts):
                dsem[k] = st.enter_context(nc.semaphore("dsem%d" % i))
            block = st.enter_context(nc.Block())
            streams = self.streams

            def run(eng_name, e):
                known = {}
                for o in streams[eng_name]:
                    need = {}
                    for p in o.deps:
                        if p.dma:
                            key = ("d", p.semkey)
                            val = 16 * p.dmacount
                            sem = dsem[p.semkey]
                        else:
                            key = ("e", p.eng)
                            val = p.mark
                            sem = esem[p.eng]
                        if known.get(key, 0) >= val:
                            continue
                        if key not in need or need[key][1] < val:
                            need[key] = (sem, val)
                    for key, (sem, val) in need.items():
                        e.wait_ge(sem, val)
                        known[key] = val
                    ins = o.fn(e)
                    if o.dma:
                        ins.then_inc(dsem[o.semkey], 16)
                    elif o.mark:
                        ins.then_inc(esem[eng_name], 1)

            @block.sync
            def _(e):
                run("sync", e)

            @block.scalar
            def _(e):
                run("scalar", e)

            @block.vector
            def _(e):
                run("vector", e)

            @block.gpsimd
            def _(e):
                run("gpsimd", e)

            @block.tensor
            def _(e):
                run("tensor", e)


DM = 1024
DEPTH = 4
NEG = -30000.0
EPS = 1e-6
T_S = 1024
T_P = 512
NTOK = T_S + T_P
AB_OFF = {"qa": 0, "ka": 1024, "va": 1280, "za": 1536, "qb": 2560, "kb": 3584, "vb": 4608, "ob": 5632, "zb": 6656,
          "gb": 7680}
PASSES = [(0, 1024, [(0, 1024)], True), (1024, 512, [(0, 256), (256, 512)], False)]
ARENA_WORDS = 12800
ARENAR_WORDS = 7680


def _rope_perm(n_rot, d_axis):
    half = d_axis // 2
    idx = np.arange(n_rot)
    within = idx % d_axis
    return np.where(within < half, idx + half, idx - half)


def _rope_tables(n_rot, d_axis):
    half = d_axis // 2
    rows = np.repeat(np.arange(1024 // 64, dtype=np.int32), 64).astype(np.float32)
    cols = np.tile(np.arange(64, dtype=np.int32), 1024 // 64).astype(np.float32)
    freqs = np.power(np.float32(10000.0), -np.arange(half, dtype=np.float32) / np.float32(half)).astype(np.float32)
    cos = np.zeros((n_rot, 1024), np.float32)
    sin = np.zeros((n_rot, 1024), np.float32)
    for d in range(n_rot):
        axis = d // d_axis
        within = d % d_axis
        pos = rows if axis == 0 else cols
        ang = (pos * freqs[within % half]).astype(np.float32)
        cos[d] = np.cos(ang)
        s = np.sin(ang)
        sin[d] = -s if within < half else s
    return cos, sin


def _fm(v, nchunk):
    return np.ascontiguousarray(np.asarray(v, np.float32).reshape(nchunk, 128).T)


def prep_inputs(inp):
    f32 = np.float32
    sh = {}
    cA, sA = _rope_tables(64, 32)
    sh["cosA"] = np.concatenate([cA, cA], 0)
    sh["sinA"] = np.concatenate([sA, sA], 0)
    cC, sC = _rope_tables(32, 16)
    cosC = np.ones((128, 1024), f32)
    sinC = np.zeros((128, 1024), f32)
    cosC[64:96] = cC
    sinC[64:96] = sC
    sh["cosC"] = cosC
    sh["sinC"] = sinC
    kk = np.arange(128)[:, None]
    qq = np.arange(128)[None, :]
    mA = np.zeros((128, 384), f32)
    mA[:, 0:128] = np.where(kk <= qq, 0.0, NEG)
    mA[:, 256:384] = np.where(qq <= kk, 0.0, NEG)
    sh["maskA"] = mA
    sh["maskF"] = np.where(kk <= qq, 0.0, NEG).astype(f32)
    sh["maskB"] = np.where(kk >= qq, 0.0, NEG).astype(f32)
    sh["ident"] = np.eye(128, dtype=f32)
    pA = np.zeros((128, 128), f32)
    pidx = np.arange(128)
    pperm = np.where((pidx % 32) < 16, pidx + 16, pidx - 16)
    pA[pperm, pidx] = 1.0
    sh["permA"] = pA
    sel = np.zeros((128, 4, 128), f32)
    for base in (0, 32, 64):
        for h in range(4):
            sel[base + h, h, :] = 1.0
    sh["selrows"] = sel.reshape(128, 512)
    sh["fnormT"] = _fm(inp["final_norm"], 8)
    permA = _rope_perm(64, 32)
    permC = _rope_perm(32, 16)
    for l in range(DEPTH):
        sh["wmod%d" % l] = np.ascontiguousarray(inp["w_mod"][l])
        sh["bmodT%d" % l] = _fm(inp["b_mod"][l], 24)
        sh["normgT%d" % l] = _fm(inp["norm_g"][l], 8)
    for j in range(2):
        W = np.asarray(inp["w_in_ab"][j])
        WA = np.zeros((1024, 4, 704), f32)
        for g in range(4):
            qcols = AB_OFF["qa"] + g * 256 + np.arange(256)
            kcols = AB_OFF["ka"] + g * 64 + np.arange(64)
            WA[:, g, 0:256] = W[:, qcols]
            WA[:, g, 256:320] = W[:, kcols]
            WA[:, g, 320:384] = W[:, kcols]
            WA[:, g, 384:448] = W[:, AB_OFF["va"] + g * 64 + np.arange(64)]
            WA[:, g, 448:704] = W[:, AB_OFF["za"] + g * 256 + np.arange(256)]
        sh["WA%d" % j] = WA
        WB = np.zeros((1024, 4, 1280), f32)
        for h in range(4):
            for i, nm in enumerate(["qb", "kb", "ob", "zb", "vb"]):
                WB[:, h, i * 256:(i + 1) * 256] = W[:, AB_OFF[nm] + h * 256 + np.arange(256)]
        sh["WB%d" % j] = WB
        sh["WG%d" % j] = np.ascontiguousarray(W[:, AB_OFF["gb"]:AB_OFF["gb"] + 16])
        sh["woutab%d" % j] = np.ascontiguousarray(inp["w_out_ab"][j])
        sh["sinkb%d" % j] = np.ascontiguousarray(np.broadcast_to(np.asarray(inp["sink_a"][j], f32)[None, :], (128, 16)))
        cb = np.asarray(inp["conv_b"][j], f32)
        sh["convT%d" % j] = np.ascontiguousarray(cb.reshape(3, 16, 128).transpose(2, 1, 0))
        gbv = np.asarray(inp["gate_bias_b"][j], f32).reshape(4, 4).T
        gb = np.zeros((128, 4), f32)
        for base in (0, 32, 64):
            gb[base:base + 4] = gbv
        sh["gbias%d" % j] = gb
        sh["normbT%d" % j] = _fm(inp["norm_b"][j], 8)
    for j in range(2):
        W = np.asarray(inp["w_in_c"][j])
        sh["winc%d" % j] = np.ascontiguousarray(W)
        wkr = np.zeros((1024, 192), f32)
        wkr[:, 64:96] = W[:, 640:672]
        wkr[:, 160:192] = W[:, 640 + permC]
        sh["wkr%d" % j] = wkr
        Wq = np.asarray(inp["w_qb_c"][j]).reshape(384, 16, 96)
        wqb = np.zeros((384, 16, 192), f32)
        wqb[:, :, 0:96] = Wq
        wqb[:, :, 96:160] = Wq[:, :, 0:64]
        wqb[:, :, 160:192] = Wq[:, :, 64 + permC]
        sh["wqb%d" % j] = wqb
        sh["wkvb%d" % j] = np.ascontiguousarray(inp["w_kvb_c"][j])
        sh["woutc%d" % j] = np.ascontiguousarray(inp["w_out_c"][j])
        sh["qnormT%d" % j] = _fm(inp["q_norm_c"][j], 3)
        sh["kvnormT%d" % j] = _fm(inp["kv_norm_c"][j], 2)
    per = []
    for i in range(8):
        d = {}
        xs = np.asarray(inp["x_sample"][i])
        xp = np.asarray(inp["x_prompt"][2 * i:2 * i + 2]).reshape(512, 1024)
        d["xT"] = np.ascontiguousarray(np.concatenate([xs, xp], 0).T)
        cond = np.stack([np.asarray(inp["c"][i]), np.asarray(inp["c_ctx"])], -1)
        d["condT"] = np.ascontiguousarray(cond.reshape(8, 128, 2).transpose(1, 0, 2))
        for j in range(2):
            ck = np.asarray(inp["cache_a_k"][i, j])
            ckT = ck.transpose(1, 2, 0)
            d["KctxT%d" % j] = np.ascontiguousarray(np.concatenate([ckT, ckT], 1))
            d["Vctx%d" % j] = np.ascontiguousarray(np.asarray(inp["cache_a_v"][i, j]).reshape(256, 256))
            d["C0%d" % j] = np.ascontiguousarray(inp["state_b_mem"][i, j])
            d["n0%d" % j] = np.ascontiguousarray(inp["state_b_norm"][i, j])
            m0v = np.asarray(inp["state_b_max"][i, j]).T
            m0 = np.zeros((128, 2), f32)
            for base in (0, 32, 64):
                m0[base:base + 4] = m0v
            d["m0%d" % j] = m0
            d["cckvT%d" % j] = np.ascontiguousarray(np.asarray(inp["cache_c_kv"][i, j]).T)
            d["ckrT%d" % j] = np.ascontiguousarray(np.asarray(inp["cache_c_krope"][i, j]).T)
        per.append(d)
    return sh, per


IN_SPECS = None


def input_specs():
    s = {}
    s["xT"] = ([1024, NTOK], False)
    s["condT"] = ([128, 8, 2], False)
    for n in ["cosA", "sinA", "cosC", "sinC"]:
        s[n] = ([128, 1024], False)
    s["maskA"] = ([128, 384], False)
    s["maskF"] = ([128, 128], False)
    s["maskB"] = ([128, 128], False)
    s["ident"] = ([128, 128], False)
    s["permA"] = ([128, 128], False)
    s["selrows"] = ([128, 512], False)
    s["fnormT"] = ([128, 8], False)
    for l in range(DEPTH):
        s["wmod%d" % l] = ([1024, 3072], True)
        s["bmodT%d" % l] = ([128, 24], False)
        s["normgT%d" % l] = ([128, 8], False)
    for j in range(2):
        s["WA%d" % j] = ([1024, 4, 704], True)
        s["WB%d" % j] = ([1024, 4, 1280], True)
        s["WG%d" % j] = ([1024, 16], True)
        s["woutab%d" % j] = ([2048, 1024], True)
        s["sinkb%d" % j] = ([128, 16], False)
        s["convT%d" % j] = ([128, 16, 3], False)
        s["gbias%d" % j] = ([128, 4], False)
        s["normbT%d" % j] = ([128, 8], False)
        s["KctxT%d" % j] = ([4, 128, 256], False)
        s["Vctx%d" % j] = ([256, 256], False)
        s["C0%d" % j] = ([2, 4, 256, 256], False)
        s["n0%d" % j] = ([2, 4, 256], False)
        s["m0%d" % j] = ([128, 2], False)
        s["winc%d" % j] = ([1024, 1696], True)
        s["wkr%d" % j] = ([1024, 192], True)
        s["wqb%d" % j] = ([384, 16, 192], True)
        s["wkvb%d" % j] = ([256, 2048], True)
        s["woutc%d" % j] = ([1024, 1024], True)
        s["qnormT%d" % j] = ([128, 3], False)
        s["kvnormT%d" % j] = ([128, 2], False)
        s["cckvT%d" % j] = ([256, 256], True)
        s["ckrT%d" % j] = ([32, 256], False)
    return s


def output_specs():
    return {
        "yT_o": [1024, NTOK],
        "kaT_o": [2, 256, 512],
        "va_o": [2, 512, 256],
        "C_o": [2, 2, 2, 4, 256, 256],
        "n_o": [2, 2, 2, 4, 256],
        "m_o": [2, 2, 2, 4],
        "ckvT_o": [2, 256, 512],
        "krT_o": [2, 32, 512],
    }


def build(nlayers=DEPTH, do_final_norm=True):
    nc = bass.Bass("TRN2", target_bir_lowering=False)
    D = {}
    for name, (shape, r) in input_specs().items():
        D[name] = nc.dram_tensor(name, list(shape), F32R if r else F32, kind="ExternalInput").ap()
    for name, shape in output_specs().items():
        D[name] = nc.dram_tensor(name, list(shape), F32, kind="ExternalOutput").ap()
    for name, shape in DBG_SPECS.items():
        D[name] = nc.dram_tensor(name, list(shape), F32, kind="ExternalOutput").ap()
    P = Prog(nc)
    st = contextlib.ExitStack()
    with st:
        def sb(name, shape, dt=F32):
            return st.enter_context(nc.sbuf_tensor(name, list(shape), dt))

        psb = [st.enter_context(nc.psum_tensor("psb%d" % i, [128, 512], F32)) for i in range(8)]
        yT = sb("yT", [128, 8, NTOK])
        hT = sb("hT", [128, 8, 1024], F32R)
        wr = [sb("wr%d" % i, [128, 4096], F32R) for i in range(2)]
        tabC = sb("tabC", [128, 1024])
        tabS = sb("tabS", [128, 1024])
        arena = sb("arena", [128, ARENA_WORDS])
        arenaR = sb("arenaR", [128, ARENAR_WORDS], F32R)
        identF = sb("identF", [128, 128])
        identR = sb("identR", [128, 128], F32R)
        identB = sb("identB", [128, 128], BF16)
        permB = sb("permB", [128, 128], BF16)
        onesR = sb("onesR", [128, 128], F32R)
        ones4 = sb("ones4", [128, 128])
        selF = sb("selF", [128, 512])
        maskAf = arena[:, 0:384]
        maskAb = sb("maskAb", [128, 384], BF16)
        maskFf = arena[:, 384:640]
        maskFB = sb("maskFB", [128, 256], BF16)
        condt = sb("condt", [128, 8, 2])
        cond_t = sb("cond_t", [128, 8, 2])
        scond = sb("scond", [128, 8, 2], F32R)
        modv_b = [sb("modv%d" % i, [128, 24, 2]) for i in range(2)]
        avec_b = [sb("avec%d" % i, [128, 8, 2]) for i in range(2)]
        bmodt_b = [sb("bmodt%d" % i, [128, 24]) for i in range(2)]
        normgt_b = [sb("normgt%d" % i, [128, 8]) for i in range(2)]
        cur = {"l": 0}
        smallc = sb("smallc", [128, 96])
        sinkt = sb("sinkt", [128, 16])
        convt = sb("convt", [128, 16, 3])
        m0t = sb("m0t", [128, 2])
        epsb = sb("epsb", [128, 1])

        state = {"ps": 0, "uid": 0}

        def uid(prefix):
            state["uid"] += 1
            return (prefix, state["uid"])

        def ps_next(pool=(4, 5, 6, 7)):
            i = pool[state["ps"] % len(pool)]
            state["ps"] += 1
            return psb[i], ("ps", i)

        ALLPS = (0, 1, 2, 3, 4, 5, 6, 7)

        def mm(out, lhsT, rhs, start, stop, reads, writes):
            op_ = P.add("tensor", lambda e, o=out, l=lhsT, r=rhs, s=start, t=stop: e.matmul(o, l, r, start=s, stop=t),
                        reads, writes)
            op_.seq = 2 if lhsT.dtype == F32 else 1

        def act(out, in_, func, reads, writes, scale=1.0, bias=0.0):
            P.add("scalar", lambda e, o=out, i=in_, f=func, s=scale, b=bias: e.activation(out=o, in_=i, func=f, bias=b, scale=s),
                  reads, writes)

        def tt(eng, out, in0, in1, op, reads, writes):
            if eng == "gpsimd" and FLAGS.get("nopool"):
                eng = "vector"
            P.add(eng, lambda e, o=out, a=in0, b=in1, p=op: e.tensor_tensor(out=o, in0=a, in1=b, op=p), reads, writes)

        def ts(eng, out, in0, s1, op0, reads, writes, s2=None, op1=None):
            if eng == "gpsimd" and FLAGS.get("nopool"):
                eng = "vector"
            if op1 is None:
                P.add(eng, lambda e, o=out, a=in0, x=s1, p=op0: e.tensor_scalar(out=o, in0=a, scalar1=x, scalar2=None, op0=p),
                      reads, writes)
            else:
                P.add(eng, lambda e, o=out, a=in0, x=s1, p=op0, y=s2, q=op1: e.tensor_scalar(out=o, in0=a, scalar1=x, scalar2=y, op0=p, op1=q),
                      reads, writes)

        def stt(out, in0, scalar, in1, op0, op1, reads, writes):
            P.add("vector", lambda e, o=out, a=in0, s=scalar, b=in1, p=op0, q=op1: e.scalar_tensor_tensor(out=o, in0=a, scalar=s, in1=b, op0=p, op1=q),
                  reads, writes)

        def cp(eng, out, in_, reads, writes):
            if eng == "gpsimd" and FLAGS.get("nopool"):
                eng = "vector"
            if eng == "scalar":
                act(out, in_, AF.Copy, reads, writes)
            else:
                P.add(eng, lambda e, o=out, i=in_: e.tensor_copy(o, i), reads, writes)

        def memset(eng, ap, val, writes):
            if eng == "gpsimd" and FLAGS.get("nopool"):
                eng = "vector"
            P.add(eng, lambda e, a=ap, v=val: e.memset(a, v), (), writes)

        def dma(eng, out, in_, reads, writes, semkey=None, is_out=False):
            P.add(eng, lambda e, o=out, i=in_: e.dma_start(out=o, in_=i), reads, writes, dma=True, semkey=semkey, is_out=is_out)

        ar = {"off": 0, "offR": 0}

        def carveR(shape):
            n = 1
            for s_ in shape:
                n *= s_
            off = ar["offR"]
            ar["offR"] = off + n
            assert ar["offR"] <= ARENAR_WORDS, ("arenaR overflow", ar["offR"])
            ap = arenaR[:, off:off + n]
            if len(shape) == 2:
                ap = ap.rearrange("p (a b) -> p a b", a=shape[0])
            elif len(shape) == 3:
                ap = ap.rearrange("p (a b c) -> p a b c", a=shape[0], b=shape[1])
            return ap

        def carve(shape, dt=F32):
            n = 1
            for s_ in shape:
                n *= s_
            words = n if dt in (F32, F32R) else (n + 1) // 2
            off = ar["off"]
            ar["off"] = off + words
            assert ar["off"] <= ARENA_WORDS, ("arena overflow", ar["off"])
            ap = arena[:, off:off + words]
            if dt != F32:
                ap = ap.bitcast(dt)
            if len(shape) == 2:
                ap = ap.rearrange("p (a b) -> p a b", a=shape[0])
            elif len(shape) == 3:
                ap = ap.rearrange("p (a b c) -> p a b c", a=shape[0], b=shape[1])
            return ap

        wq_list = []
        wst = {"issued": 0, "next": 0}

        def w_issue_upto(i):
            while wst["issued"] <= min(i, len(wq_list) - 1):
                t = wst["issued"]
                name, dap, k, n = wq_list[t]
                slot = t % 2
                view = wr[slot][:, 0:k * n].rearrange("p (k n) -> p k n", k=k)
                dma("gpsimd", view, dap, (), [("wr", slot)])
                wst["issued"] += 1

        def w_pop(name):
            i = wst["next"]
            assert wq_list[i][0] == name, (wq_list[i][0], name)
            w_issue_upto(i + 1)
            wst["next"] += 1
            _, _, k, n = wq_list[i]
            slot = i % 2
            return wr[slot][:, 0:k * n].rearrange("p (k n) -> p k n", k=k), ("wr", slot)

        def wtile(name, dram2d, k, n):
            wq_list.append((name, dram2d.rearrange("(k p) n -> p k n", p=128), k, n))

        def nxt_mod(l, ip, b):
            if ip == 1 and l + 1 < nlayers:
                wtile("mod%d_%d" % (l + 1, b), D["wmod%d" % (l + 1)][:, b * 512:(b + 1) * 512], 8, 512)

        def layer_tiles(l, ip):
            j = l // 2
            if ip == 0 and l == 0:
                for b in range(6):
                    wtile("mod%d_%d" % (l, b), D["wmod%d" % l][:, b * 512:(b + 1) * 512], 8, 512)
            if l % 2 == 0:
                for g in range(4):
                    wtile("Aq", D["WA%d" % j][:, g, 0:256], 8, 256)
                    wtile("Ak", D["WA%d" % j][:, g, 256:384], 8, 128)
                    wtile("Avz", D["WA%d" % j][:, g, 384:704], 8, 320)
                    wtile("Ao", D["woutab%d" % j][g * 256:(g + 1) * 256, :], 2, 1024)
                    nxt_mod(l, ip, g)
                wtile("G", D["WG%d" % j][:, :], 8, 16)
                for h in range(4):
                    wtile("Bqk", D["WB%d" % j][:, h, 0:512], 8, 512)
                    wtile("Bv", D["WB%d" % j][:, h, 1024:1280], 8, 256)
                    wtile("Boz", D["WB%d" % j][:, h, 512:1024], 8, 512)
                    wtile("Bo", D["woutab%d" % j][1024 + h * 256:1024 + (h + 1) * 256, :], 2, 1024)
                    if h < 2:
                        nxt_mod(l, ip, 4 + h)
            else:
                wtile("Cqa", D["winc%d" % j][:, 0:384], 8, 384)
                wtile("Ckva", D["winc%d" % j][:, 384:640], 8, 256)
                wtile("Ckr", D["wkr%d" % j][:, :], 8, 192)
                nxt_mod(l, ip, 0)
                nxt_mod(l, ip, 1)
                for g in range(4):
                    wtile("Cqb", D["wqb%d" % j][:, g * 4:(g + 1) * 4, :].rearrange("a h c -> a (h c)"), 3, 768)
                    wtile("Ckvb", D["wkvb%d" % j][:, g * 512:(g + 1) * 512], 2, 512)
                    wtile("Cz", D["winc%d" % j][:, 672 + g * 256:672 + (g + 1) * 256], 8, 256)
                    wtile("Co", D["woutc%d" % j][g * 256:(g + 1) * 256, :], 2, 1024)
                    nxt_mod(l, ip, 2 + g)

        for l in range(nlayers):
            for ip in range(2):
                layer_tiles(l, ip)

        dma("sync", identF[:], D["ident"], (), ["identF"])
        cp("vector", identB[:], identF[:], ["identF"], ["identB"])
        dma("sync", arena[:, 640:768], D["permA"], (), ["permAf"])
        cp("vector", permB[:], arena[:, 640:768], ["permAf"], ["permB"])
        cp("vector", identR[:], identF[:], ["identF"], ["identR"])
        memset("vector", ones4[:], 1.0, ["ones4"])
        cp("vector", onesR[:], ones4[:, :], ["ones4"], ["onesR"])
        memset("vector", epsb[:], EPS, ["epsb"])
        dma("sync", selF[:], D["selrows"], (), ["selF"])
        dma("sync", maskAf[:], D["maskA"], (), ["maskAf"])
        cp("vector", maskAb[:], maskAf[:], ["maskAf"], ["maskAb"])
        dma("sync", maskFf[:, 0:128], D["maskF"], (), ["maskFf"])
        dma("sync", maskFf[:, 128:256], D["maskB"], (), ["maskFf"])
        cp("vector", maskFB[:], maskFf[:], ["maskFf"], ["maskFB"])
        for kc in range(8):
            dma("sync", yT[:, kc, :], D["xT"][kc * 128:(kc + 1) * 128, :], (), [("y", kc, b) for b in range(3)],
                semkey=("yload", kc))
        dma("sync", condt[:], D["condT"], (), ["condt"])
        act(cond_t[:], condt[:], AF.Tanh, ["condt"], ["cond_t"], scale=0.5)
        ts("vector", cond_t[:], cond_t[:], 0.5, ALU.mult, ["cond_t"], ["cond_t"], s2=0.5, op1=ALU.add)
        tt("vector", scond[:], cond_t[:], condt[:], ALU.mult, ["cond_t", "condt"], ["scond"])

        def ykey(kc, t0, blk):
            return ("y", kc, (t0 + blk * 512) // 512)

        def rstd_from_sumsq(ps_ap, pskey, n, out_ap, outkey, ncols, tmp_ap, tmpkey):
            act(tmp_ap, ps_ap, AF.Ln, [pskey, "epsb"], [tmpkey], scale=1.0 / n, bias=epsb[:, 0:1])
            act(out_ap, tmp_ap, AF.Exp, [tmpkey], [outkey], scale=-0.5)

        def y_update(ps_ap, pskey, oc, t0, blk, cond):
            k = ykey(oc, t0, blk)
            ysl = yT[:, oc, t0 + blk * 512:t0 + (blk + 1) * 512]
            mv = modv_b[cur["l"] % 2]
            stt(ysl, ps_ap, mv[:, 16 + oc, cond:cond + 1], ysl, ALU.mult, ALU.add, [pskey, k, ("modv", cur["l"] % 2)], [k])

        def out_proj(wname, aoT, aokey_fn, t0, TP, cond):
            wo, wkey = w_pop(wname)
            out_proj2(wo, wkey, aoT, aokey_fn, t0, TP, cond)

        def out_proj2(wo, wkey, aoT, aokey_fn, t0, TP, cond):
            for blk in range(TP // 512):
                for oc in range(8):
                    ps, pk = ps_next(ALLPS)
                    for fc in range(2):
                        mm(ps[:, :], wo[:, fc, oc * 128:(oc + 1) * 128], aoT[:, fc, blk * 512:(blk + 1) * 512],
                           fc == 0, fc == 1, [wkey, aokey_fn(fc, blk)], [pk])
                    y_update(ps[:, :], pk, oc, t0, blk, cond)

        def transpose_to_aoT(zs, zskey_fn, aoT, aokey_fn, nqt):
            for c in range(2):
                for b4 in range(nqt // 4):
                    ps, pk = ps_next(ALLPS)
                    for i in range(4):
                        qt = b4 * 4 + i
                        mm(ps[:, i * 128:(i + 1) * 128], zs[:, qt, c * 128:(c + 1) * 128], identB[:, :], True, True,
                           [zskey_fn(qt), "identB"], [pk])
                    cp("scalar", aoT[:, c, b4 * 512:(b4 + 1) * 512], ps[:, :], [pk], [aokey_fn(c, b4)])

        def attn_head(QT, qkeys_fn, qbase, qrows, KT, kkeys_fn, jobs, vfn, scale, sink_ap, zs, zskey_fn, hl, nqt, obank0, spool=(4, 5, 6, 7)):
            first = {}
            last = {}
            for ji, (kc0, lo, hi, mc, vid) in enumerate(jobs):
                for qt in range(lo, hi):
                    first.setdefault(qt, ji)
                    last[qt] = ji
            nb = (nqt + 3) // 4
            lastpv = {}
            for ji, (kc0, lo, hi, mc, vid) in enumerate(jobs):
                for qt in range(lo, hi):
                    lastpv[qt // 4] = (ji, qt)
            obanks = [(psb[obank0 + b], ("ps", obank0 + b)) for b in range(nb)]
            firstpv = {}
            for ji, (kc0, lo, hi, mc, vid) in enumerate(jobs):
                for qt in range(lo, hi):
                    firstpv.setdefault(qt // 4, (ji, qt))
            pieces = []
            for ji, (kc0, lo, hi, mc, vid) in enumerate(jobs):
                q0 = lo
                while q0 < hi:
                    q1 = min(hi, q0 + 4)
                    pieces.append((ji, kc0, lo, mc, vid, q0, q1))
                    q0 = q1

            def emit_score(pc):
                ji, kc0, lo, mc, vid, q0, q1 = pc
                ncol = (q1 - q0) * 128
                ps, pk = ps_next(spool)
                mm(ps[:, 0:ncol], KT[qbase:qbase + qrows, kc0:kc0 + 128], QT[qbase:qbase + qrows, q0 * 128:q1 * 128],
                   True, mc is None, kkeys_fn(kc0) + qkeys_fn(q0, q1), [pk])
                if mc is not None:
                    m0_ = mc + (q0 - lo) * 128
                    mm(ps[:, 0:ncol], identB[:, :], maskAb[:, m0_:m0_ + ncol], False, True, ["identB", "maskAb"], [pk])
                eb, ek = e_next()
                act(eb[:, 0:ncol], ps[:, 0:ncol], AF.Exp, [pk], [ek], scale=scale)
                return eb, ek

            def emit_pv(pc, eb, ek):
                ji, kc0, lo, mc, vid, q0, q1 = pc
                vap, vkey = vfn(vid)
                for qt in range(q0, q1):
                    ob, ok = obanks[qt // 4]
                    c0 = (qt % 4) * 128
                    P.add("tensor", lambda e, o=ob[:, c0:c0 + 65], l=eb[:, (qt - q0) * 128:(qt - q0 + 1) * 128], r=vap,
                          s_=(firstpv[qt // 4] == (ji, qt)), t_=(lastpv[qt // 4] == (ji, qt)):
                          e.matmul(o, l, r, start=s_, stop=t_, skip_group_check=True), [ek, vkey], [ok])

            LA = 3
            q_ = [emit_score(pieces[i]) for i in range(min(LA, len(pieces)))]
            for i in range(len(pieces)):
                if i + LA < len(pieces):
                    q_.append(emit_score(pieces[i + LA]))
                eb_, ek_ = q_.pop(0)
                emit_pv(pieces[i], eb_, ek_)
            for b in range(nb):
                ob, ok = obanks[b]
                n4 = min(4, nqt - b * 4)
                ov = ob[:, 0:n4 * 128].rearrange("p (a c) -> p a c", a=n4)
                dsum, dk = small_next()
                if sink_ap is not None:
                    ts("vector", dsum[:, 0:n4], ov[:, :, 64], sink_ap, ALU.add, [ok, "sinkt"], [dk])
                else:
                    cp("vector", dsum[:, 0:n4], ov[:, :, 64], [ok], [dk])
                P.add("vector", lambda e, o=dsum[:, 4:4 + n4], i=dsum[:, 0:n4]: e.reciprocal(o, i), [dk], [dk])
                at, atk = atmp_next()
                tt("vector", at[:, 0:n4, :], ov[:, :, 0:64], dsum[:, 4:4 + n4].to_broadcast([128, n4, 64]), ALU.mult,
                   [ok, dk], [atk])
                zsl = zs[:, b * 4:b * 4 + n4, hl * 64:(hl + 1) * 64]
                zk = [zskey_fn(qt) for qt in range(b * 4, b * 4 + n4)]
                tt("gpsimd", zsl, zsl, at[:, 0:n4, :], ALU.mult, zk + [atk], zk)

        rot = {}

        def e_next():
            r = rot["E"]
            i = r["i"] % len(r["bufs"])
            r["i"] += 1
            return r["bufs"][i], ("E", i)

        def small_next():
            r = rot["S"]
            i = r["i"] % len(r["bufs"])
            r["i"] += 1
            return r["bufs"][i], ("dsum", i)

        def atmp_next():
            r = rot["A"]
            i = r["i"] % len(r["bufs"])
            r["i"] += 1
            return r["bufs"][i], ("atmp", i)

        def alloc_attn_rot():
            rot["E"] = {"i": 0, "bufs": [carve([512], BF16) for _ in range(5)]}
            rot["S"] = {"i": 0, "bufs": [carve([8]) for _ in range(2)]}
            rot["A"] = {"i": 0, "bufs": [carve([4, 64], BF16) for _ in range(2)]}

        def silu_tok(ps_ap, pskey, out_ap, outkey, tmp_ap, tmpkey, ncols):
            act(tmp_ap, ps_ap, AF.Tanh, [pskey], [tmpkey], scale=0.5)
            ts("gpsimd", tmp_ap, tmp_ap, 0.5, ALU.mult, [tmpkey], [tmpkey], s2=0.5, op1=ALU.add)
            tt("vector", out_ap, tmp_ap, ps_ap, ALU.mult, [tmpkey, pskey], [outkey])

        def rope_evac(psx, pkx, psp, pkp, r0, r1, cols, out_ap, outkey, tmpa, tmpb, tka, tkb):
            tt("vector", tmpa[r0:r1, :], psp[r0:r1, :], tabS[r0:r1, cols[0]:cols[1]], ALU.mult, [pkp, "tabS"], [tka])
            tt("vector", tmpb[r0:r1, :], psx[r0:r1, :], tabC[r0:r1, cols[0]:cols[1]], ALU.mult, [pkx, "tabC"], [tkb])
            tt("gpsimd", out_ap, tmpa[r0:r1, :], tmpb[r0:r1, :], ALU.add, [tka, tkb], [outkey])

        def dbg(name, ap, reads):
            if name in DBG_SPECS:
                P.add("sync", lambda e, o=D[name], i=ap: e.dma_start(out=o, in_=i), reads, [], dma=True, semkey=("dbg", name), is_out=True)

        def arena_reset(keep_base=False, keepR=False, barrier=True):
            if barrier:
                P.barrier()
            ar["off"] = ar.get("base", 0) if keep_base else 0
            if not keep_base:
                ar["base"] = 0
                ar["baseR"] = 0
            if not keepR:
                ar["offR"] = ar.get("baseR", 0) if keep_base else 0

        def mod_load_small(l):
            i = l % 2
            dma("sync", bmodt_b[i][:], D["bmodT%d" % l], (), [("bmodt", i)])
            dma("sync", normgt_b[i][:], D["normgT%d" % l], (), [("normgt", i)])

        def mod_tile(l, b):
            i = l % 2
            modrow = arena[:, ARENA_WORDS - 512:ARENA_WORDS]
            w, wk = w_pop("mod%d_%d" % (l, b))
            psr, pkr = ps_next(ALLPS)
            for kc in range(8):
                mm(psr[0:2, :], scond[:, kc, :], w[:, kc, :], kc == 0, kc == 7, [wk, "scond"], [pkr])
            cp("scalar", modrow[0:2, :], psr[0:2, :], [pkr], ["modrow"])
            ps, pk = ps_next(ALLPS)
            for oc4 in range(4):
                mm(ps[:, 2 * oc4:2 * oc4 + 2], modrow[0:2, oc4 * 128:(oc4 + 1) * 128], identF[0:2, 0:2], True, True,
                   ["modrow", "identF"], [pk])
            psv = ps[:, 0:8].rearrange("p (c j) -> p c j", j=2)
            for jj in range(2):
                tt("vector", modv_b[i][:, b * 4:(b + 1) * 4, jj], psv[:, :, jj], bmodt_b[i][:, b * 4:(b + 1) * 4], ALU.add,
                   [pk, ("bmodt", i)], [("modv", i)])

        def mod_finish(l):
            i = l % 2
            for jj in range(2):
                stt(avec_b[i][:, :, jj], modv_b[i][:, 8:16, jj], 1.0, normgt_b[i][:, :], ALU.add, ALU.mult,
                    [("modv", i), ("normgt", i)], [("avec", i)])

        def layer_mod(l):
            P.phase = "mod"
            mod_load_small(l)
            for b in range(6):
                mod_tile(l, b)
            mod_finish(l)

        def norm_blocks(t0, nblk, scale_fn, bias_fn, out_fn, extra_reads):
            sqb = [carveR([512]) for _ in range(2)]
            lnb = carve([512])
            rstd = carve([512])
            tmb = [carve([512]) for _ in range(2)]
            for blk in range(nblk):
                c0 = t0 + blk * 512
                ps, pk = ps_next(ALLPS)
                for kc in range(8):
                    act(sqb[kc % 2][:, :], yT[:, kc, c0:c0 + 512], AF.Square, [ykey(kc, t0, blk)], [("sqb", kc % 2)])
                    mm(ps[:, :], onesR[:, :], sqb[kc % 2][:, :], kc == 0, kc == 7, ["onesR", ("sqb", kc % 2)], [pk])
                rstd_from_sumsq(ps[:, :], pk, 1024.0, rstd[:, :], "rstd", 512, lnb[:, :], "lnb")
                for kc in range(8):
                    tt("vector", tmb[kc % 2][:, :], yT[:, kc, c0:c0 + 512], rstd[:, :], ALU.mult,
                       [ykey(kc, t0, blk), "rstd"], [("tmb", kc % 2)])
                    oap, okey = out_fn(kc, blk)
                    P.add("scalar", lambda e, o=oap, i=tmb[kc % 2][:, :], s=scale_fn(kc), b=bias_fn(kc):
                          e.activation(out=o, in_=i, func=AF.Identity, bias=b, scale=s),
                          [("tmb", kc % 2)] + extra_reads, [okey])

        def layer_h(t0, TP, cond):
            P.phase = "h"
            arena_reset()
            i_ = cur["l"] % 2
            norm_blocks(t0, TP // 512,
                        lambda kc: avec_b[i_][:, kc, cond:cond + 1], lambda kc: modv_b[i_][:, kc, cond:cond + 1],
                        lambda kc, blk: (hT[:, kc, blk * 512:(blk + 1) * 512], ("h", kc, blk)), [("avec", i_), ("modv", i_)])

        def final_out():
            P.phase = "final"
            arena_reset()
            fn = carve([8])
            dma("sync", fn[:, :], D["fnormT"], (), ["fnorm"])
            stg = [carve([512]) for _ in range(3)]
            cnt = {"i": 0}

            def out_fn(kc, blk):
                i = cnt["i"] % 3
                cnt["i"] += 1
                cnt["last"] = (i, kc, blk)
                return stg[i][:, :], ("stg", i)
            if do_final_norm:
                sqb = [carveR([512]) for _ in range(2)]
                lnb = carve([512])
                rstd = carve([512])
                tmb = [carve([512]) for _ in range(2)]
                for blk in range(3):
                    c0 = blk * 512
                    ps, pk = ps_next(ALLPS)
                    for kc in range(8):
                        act(sqb[kc % 2][:, :], yT[:, kc, c0:c0 + 512], AF.Square, [("y", kc, blk)], [("sqb", kc % 2)])
                        mm(ps[:, :], onesR[:, :], sqb[kc % 2][:, :], kc == 0, kc == 7, ["onesR", ("sqb", kc % 2)], [pk])
                    rstd_from_sumsq(ps[:, :], pk, 1024.0, rstd[:, :], "rstd", 512, lnb[:, :], "lnb")
                    for kc in range(8):
                        tt("vector", tmb[kc % 2][:, :], yT[:, kc, c0:c0 + 512], rstd[:, :], ALU.mult,
                           [("y", kc, blk), "rstd"], [("tmb", kc % 2)])
                        oap, okey = out_fn(kc, blk)
                        act(oap, tmb[kc % 2][:, :], AF.Copy, [("tmb", kc % 2), "fnorm"], [okey], scale=fn[:, kc:kc + 1])
                        dma("sync", D["yT_o"][kc * 128:(kc + 1) * 128, c0:c0 + 512], oap, [okey], [], semkey=okey, is_out=True)
            else:
                for kc in range(8):
                    dma("sync", D["yT_o"][kc * 128:(kc + 1) * 128, :], yT[:, kc, :], [("y", kc, b) for b in range(3)], [],
                        semkey=("yout", kc), is_out=True)

        def a_unit(j, g, t0, TP, segs, is_sample, cond):
            arena_reset(barrier=(g == 0))
            P.phase = "A_proj%d" % int(is_sample)
            nqt = TP // 128
            nblk = TP // 512
            NK = TP + (256 if is_sample else 0)
            nvt = nqt + (2 if is_sample else 0)
            qT = carve([2, TP], BF16)
            KT2 = [carve([NK], BF16) for _ in range(2)]
            Vaug = carve([nvt, 65], BF16)
            zs = carve([nqt, 256], BF16)
            aoT = carveR([2, TP])
            ta = [carve([512]) for _ in range(2)]
            tb = [carve([512]) for _ in range(2)]
            tz = [carve([256]) for _ in range(2)]
            stg = carve([512])
            stv = [carve([64]) for _ in range(2)]
            xbr = [carve([512], BF16) for _ in range(2)]
            alloc_attn_rot()
            vkeys = [("V", i) for i in range(nvt)]
            memset("gpsimd", Vaug[:, :, 64:65], 1.0, vkeys)
            allkt = [("KT", b) for b in range(nblk)] + ([("KT", "ctx")] if is_sample else [])
            memset("gpsimd", KT2[0][64:128, :], 0.0, allkt + ["KTz"])
            memset("gpsimd", KT2[1][0:64, :], 0.0, allkt + ["KTz"])
            hk = lambda kc, blk: ("h", kc, blk)
            wq, wk = w_pop("Aq")
            tiles = [("q", c, blk) for c in range(2) for blk in range(nblk)] + [("k", 0, blk) for blk in range(nblk)]
            st_ = {}
            wkh = {}

            def proj_x(i):
                kind, c, blk = tiles[i]
                cols = (blk * 512, (blk + 1) * 512)
                if kind == "k" and "w" not in wkh:
                    wkh["w"] = w_pop("Ak")
                psx, pkx = ps_next(ALLPS)
                for kc in range(8):
                    if kind == "q":
                        lhs, wkey_ = wq[:, kc, c * 128:(c + 1) * 128], wk
                    else:
                        lhs, wkey_ = wkh["w"][0][:, kc, 0:128], wkh["w"][1]
                    mm(psx[:, :], lhs, hT[:, kc, cols[0]:cols[1]], kc == 0, kc == 7, [wkey_, hk(kc, blk)], [pkx])
                if is_sample:
                    xb_, xk_ = xbr[i % 2], ("xb", i % 2)
                    cp("scalar", xb_[:, :], psx[:, :], [pkx], [xk_])
                st_[i] = (psx, pkx)

            def proj_y(i):
                kind, c, blk = tiles[i]
                cols = (blk * 512, (blk + 1) * 512)
                psx, pkx = st_.pop(i)
                if is_sample:
                    xb_, xk_ = xbr[i % 2], ("xb", i % 2)
                    psp, pkp = ps_next(ALLPS)
                    mm(psp[:, :], permB[:, :], xb_[:, :], True, True, ["permB", xk_], [pkp])
                    tA, tB, kA, kB = ta[i % 2], tb[i % 2], ("ta", i % 2), ("tb", i % 2)
                    tt("vector", tA[:, :], psp[:, :], tabS[:, cols[0]:cols[1]], ALU.mult, [pkp, "tabS"], [kA])
                    tt("vector", tB[:, :], psx[:, :], tabC[:, cols[0]:cols[1]], ALU.mult, [pkx, "tabC"], [kB])
                    if kind == "q":
                        tt("gpsimd", qT[:, c, cols[0]:cols[1]], tA[:, :], tB[:, :], ALU.add, [kA, kB], [("qT", c, blk)])
                    else:
                        tt("gpsimd", KT2[0][0:64, cols[0]:cols[1]], tA[0:64, :], tB[0:64, :], ALU.add, [kA, kB], [("KT", blk)])
                        tt("gpsimd", KT2[1][64:128, cols[0]:cols[1]], tA[64:128, :], tB[64:128, :], ALU.add, [kA, kB], [("KT", blk)])
                else:
                    if kind == "q":
                        cp("scalar", qT[:, c, cols[0]:cols[1]], psx[:, :], [pkx], [("qT", c, blk)])
                    else:
                        cp("scalar", KT2[0][0:64, cols[0]:cols[1]], psx[0:64, :], [pkx], [("KT", blk)])
                        cp("scalar", KT2[1][64:128, cols[0]:cols[1]], psx[64:128, :], [pkx], [("KT", blk)])
                        if not FLAGS.get("noka"):
                            cp("vector", stg[0:64, :], psx[0:64, :], [pkx], ["stg"])
                            dma("sync", D["kaT_o"][j, g * 64:(g + 1) * 64, :], stg[0:64, :], ["stg"], [], semkey=("kaout", g), is_out=True)

            proj_x(0)
            for i in range(len(tiles)):
                if i + 1 < len(tiles):
                    proj_x(i + 1)
                proj_y(i)
            if is_sample and not FLAGS.get("noctx"):
                dma("sync", stg[:, 0:256], D["KctxT%d" % j][g], (), ["stg"])
                cp("vector", KT2[0][0:64, TP:TP + 256], stg[0:64, 0:256], ["stg"], [("KT", "ctx")])
                cp("vector", KT2[1][64:128, TP:TP + 256], stg[64:128, 0:256], ["stg"], [("KT", "ctx")])
            if FLAGS.get("a_stop", 9) <= 2:
                for nm in ("Avz", "Ao"):
                    w_pop(nm)
                return
            wv, wvk = w_pop("Avz")
            for qt in range(nqt):
                ps, pk = ps_next(ALLPS)
                for kc in range(8):
                    mm(ps[:, 0:320], hT[:, kc, qt * 128:(qt + 1) * 128], wv[:, kc, 0:320], kc == 0, kc == 7,
                       [hk(kc, qt // 4), wvk], [pk])
                cp("scalar", Vaug[:, qt, 0:64], ps[:, 0:64], [pk], [("V", qt)])
                silu_tok(ps[:, 64:320], pk, zs[:, qt, :], ("zs", qt), tz[qt % 2][:, :], ("tz", qt % 2), 256)
                if not is_sample:
                    cp("vector", stv[qt % 2][:, :], ps[:, 0:64], [pk], [("stv", qt % 2)])
                    dma("sync", D["va_o"][j, qt * 128:(qt + 1) * 128, g * 64:(g + 1) * 64], stv[qt % 2][:, :],
                        [("stv", qt % 2)], [], semkey=("stv", qt % 2), is_out=True)
            if is_sample:
                for c in range(2):
                    dma("sync", stv[c][:, :], D["Vctx%d" % j][c * 128:(c + 1) * 128, g * 64:(g + 1) * 64], (), [("stv", c)])
                    cp("vector", Vaug[:, nqt + c, 0:64], stv[c][:, :], [("stv", c)], [("V", nqt + c)])
            if FLAGS.get("a_stop", 9) <= 3:
                w_pop("Ao")
                return
            P.phase = "A_attn%d" % int(is_sample)
            for hl in range(4):
                p = hl % 2
                c = hl // 2
                h = g * 4 + hl
                if is_sample:
                    jobs = [(TP, 0, 8, None, nqt), (TP + 128, 0, 8, None, nqt + 1)]
                    for jt in range(8):
                        lo = max(0, jt - 1)
                        hi = min(8, jt + 2)
                        jobs.append((jt * 128, lo, hi, (lo - (jt - 1)) * 128, jt))
                else:
                    jobs = []
                    for s_ in range(2):
                        for jt in (2 * s_, 2 * s_ + 1):
                            jobs.append((jt * 128, 2 * s_, 2 * s_ + 2, None, jt))
                attn_head(qT[:, c, :], lambda q0, q1, c=c: [("qT", c, b) for b in range(q0 // 4, (q1 - 1) // 4 + 1)],
                          0, 128, KT2[p], lambda kc0: ["KTz"] + ([("KT", "ctx")] if kc0 >= TP else [("KT", kc0 // 512)]),
                          jobs, lambda vid: (Vaug[:, vid, :], ("V", vid)), 0.125, sinkt[:, h:h + 1],
                          zs, lambda qt: ("zs", qt), hl, nqt, (hl % 2) * 2, spool=((1, 3, 4, 5, 6, 7) if nqt <= 4 else (4, 5, 6, 7)))
            if FLAGS.get("a_stop", 9) <= 4:
                w_pop("Ao")
                return
            P.phase = "A_out%d" % int(is_sample)
            transpose_to_aoT(zs, lambda qt: ("zs", qt), aoT, lambda c, b: ("aoT", c, b), nqt)
            if FLAGS.get("a_stop", 9) <= 5:
                w_pop("Ao")
                return
            out_proj("Ao", aoT, lambda fc, blk: ("aoT", fc, blk), t0, TP, cond)

        def run_layers(which):
            for l in range(nlayers):
                j = l // 2
                even = (l % 2 == 0)
                cur["l"] = l
                if l == 0:
                    layer_mod(l)
                else:
                    mod_finish(l)
                if even:
                    dma("sync", tabC[:], D["cosA"], (), ["tabC"])
                    dma("sync", tabS[:], D["sinA"], (), ["tabS"])
                    dma("sync", sinkt[:], D["sinkb%d" % j], (), ["sinkt_raw"])
                    act(sinkt[:], sinkt[:], AF.Exp, ["sinkt_raw"], ["sinkt"])
                    dma("sync", convt[:], D["convT%d" % j], (), ["convt"])
                    dma("sync", smallc[:, 0:4], D["gbias%d" % j], (), ["gbias"])
                    dma("sync", smallc[:, 8:16], D["normbT%d" % j], (), ["normbt"])
                    dma("sync", m0t[:], D["m0%d" % j], (), ["m0t"])
                else:
                    dma("sync", tabC[:], D["cosC"], (), ["tabC"])
                    dma("sync", tabS[:], D["sinC"], (), ["tabS"])
                    dma("sync", smallc[:, 16:19], D["qnormT%d" % j], (), ["qnormt"])
                    dma("sync", smallc[:, 24:26], D["kvnormT%d" % j], (), ["kvnormt"])
                for ip, (t0, TP, segs, is_sample) in enumerate(PASSES):
                    cond = 0 if is_sample else 1
                    pre = (ip == 1 and l + 1 < nlayers)
                    if pre:
                        mod_load_small(l + 1)

                    def premod(b):
                        if pre:
                            ph = P.phase
                            P.phase = "mod"
                            mod_tile(l + 1, b)
                            P.phase = ph
                    layer_h(t0, TP, cond)
                    if l == 0:
                        dbg("hT%d" % ip, hT[:, :, 0:TP].bitcast(F32), [("h", kc, b) for kc in range(8) for b in range(TP // 512)])
                    if even:
                        for g in range(4):
                            if "A" in which:
                                a_unit(j, g, t0, TP, segs, is_sample, cond)
                            else:
                                for nm in ("Aq", "Ak", "Avz", "Ao"):
                                    w_pop(nm)
                            premod(g)
                        if "B" in which:
                            gates_stage(j, t0, TP, segs, is_sample)
                            for hb in range(4):
                                b_unit(j, hb, t0, TP, segs, is_sample, cond)
                                if hb < 2:
                                    premod(4 + hb)
                        else:
                            w_pop("G")
                            for hb in range(4):
                                for nm in ("Bqk", "Bv", "Boz", "Bo"):
                                    w_pop(nm)
                                if hb < 2:
                                    premod(4 + hb)
                    else:
                        if "C" in which:
                            c_pre(j, t0, TP, segs, is_sample)
                            premod(0)
                            premod(1)
                            for g in range(4):
                                c_unit(j, g, t0, TP, segs, is_sample, cond)
                                premod(2 + g)
                        else:
                            for nm in ("Cqa", "Ckva", "Ckr"):
                                w_pop(nm)
                            premod(0)
                            premod(1)
                            for g in range(4):
                                for nm in ("Cqb", "Ckvb", "Cz", "Co"):
                                    w_pop(nm)
                                premod(2 + g)
            final_out()


        G = {}

        def gates_stage(j, t0, TP, segs, is_sample):
            P.phase = "gates%d" % int(is_sample)
            arena_reset()
            nchtot = TP // 128
            ranges = [carve([TP]) for _ in range(1)]
            wib = carve([TP], BF16)

            def garr(idx, d_):
                return ranges[idx][d_ * 32:d_ * 32 + 4, :], d_ * 32, ("ga", idx, d_)
            cols = carve([nchtot, 32])
            w0r = carve([nchtot])
            G["cols"] = cols
            G["w0r"] = w0r
            G["U"] = [garr(0, 0), garr(0, 1)]
            G["WIb"] = [(wib[d_ * 32:d_ * 32 + 4, :], d_ * 32, ("wib", d_)) for d_ in range(2)]
            ar["base"] = ar["off"]
            ranges += [carve([TP]) for _ in range(5)]
            G["G"] = [garr(2, 0), garr(2, 1)]
            ts("vector", smallc[:, 4:8], smallc[:, 0:4], -1.0, ALU.mult, ["gbias"], ["ngbias"])
            wg, wgk = w_pop("G")
            hk = lambda kc, blk: ("h", kc, blk)
            def gdir(d):
                T_li, b_li, k_li = garr(1, d)
                T_lf, b_lf, k_lf = garr(3, d)
                T_B, b_B, k_B = garr(4, d)
                T_m, b_m, k_m = garr(5, d)
                U, b_U, k_U = G["U"][d]
                Gg, b_G, k_G = G["G"][d]
                qi_i = 2 * d
                qi_f = 2 * d + 1
                for blk in range(TP // 512):
                    c0, c1 = blk * 512, (blk + 1) * 512
                    ps, pk = ps_next(ALLPS)
                    for kc in range(8):
                        mm(ps[0:4, :], wg[:, kc, qi_i * 4:(qi_i + 1) * 4], hT[:, kc, c0:c1], kc == 0, kc == 7, [wgk, hk(kc, blk)], [pk])
                    act(T_li[:, c0:c1], ps[0:4, :], AF.Identity, [pk, "gbias"], [k_li], bias=smallc[0:4, qi_i:qi_i + 1])
                    yield
                    ps2, pk2 = ps_next(ALLPS)
                    for kc in range(8):
                        mm(ps2[0:4, :], wg[:, kc, qi_f * 4:(qi_f + 1) * 4], hT[:, kc, c0:c1], kc == 0, kc == 7, [wgk, hk(kc, blk)], [pk2])
                    act(T_lf[:, c0:c1], ps2[0:4, :], AF.Exp, [pk2, "ngbias"], [k_lf], scale=-1.0, bias=smallc[0:4, 4 + qi_f:5 + qi_f])
                    yield
                act(T_lf[:, :], T_lf[:, :], AF.Ln, [k_lf], [k_lf], bias=1.0)
                yield
                ts("vector", T_lf[:, :], T_lf[:, :], -1.0, ALU.mult, [k_lf], [k_lf])
                yield
                for si, (s0, s1) in enumerate(segs):
                    def dirv(ap):
                        v = ap[:, s0:s1]
                        return v[:, ::-1] if d == 1 else v
                    n_ = s1 - s0
                    onesv = ones4[b_B:b_B + 4, 0:1].to_broadcast([4, n_])
                    P.add("vector", lambda e, o=dirv(T_B), a=onesv, b=dirv(T_lf): e.tensor_tensor_scan(
                        out=o, data0=a, data1=b, initial=0.0, op0=ALU.mult, op1=ALU.add), ["ones4", k_lf], [k_B])
                    yield
                    init = m0t[b_m:b_m + 4, d:d + 1] if is_sample else 0.0
                    P.add("vector", lambda e, o=dirv(T_m), a=dirv(T_lf), b=dirv(T_li), i_=init: e.tensor_tensor_scan(
                        out=o, data0=a, data1=b, initial=i_, op0=ALU.add, op1=ALU.max), [k_lf, k_li, "m0t"], [k_m])
                    yield
                tt("vector", U[:, :], T_B[:, :], T_m[:, :], ALU.subtract, [k_B, k_m], [k_U])
                yield
                tt("vector", Gg[:, :], T_li[:, :], T_B[:, :], ALU.subtract, [k_li, k_B], [k_G])
                yield
                for si, (s0, s1) in enumerate(segs):
                    nch = (s1 - s0) // 128
                    order = list(range(nch)) if d == 0 else list(range(nch - 1, -1, -1))
                    for oi, c in enumerate(order):
                        a0 = s0 + c * 128
                        a1 = a0 + 128
                        endi = (a1 - 1) if d == 0 else a0
                        if oi == 0:
                            if is_sample:
                                ts("vector", T_li[:, a0:a1], U[:, a0:a1], m0t[b_li:b_li + 4, d:d + 1], ALU.add, [k_U, "m0t"], [k_li])
                                yield
                            else:
                                cp("vector", T_li[:, a0:a1], U[:, a0:a1], [k_U], [k_li])
                                yield
                        else:
                            pc = order[oi - 1]
                            pend = (s0 + pc * 128 + 127) if d == 0 else (s0 + pc * 128)
                            ts("vector", T_li[:, a0:a1], U[:, a0:a1], U[:, pend:pend + 1], ALU.subtract, [k_U], [k_li])
                            yield
                        ts("vector", T_lf[:, a0:a1], Gg[:, a0:a1], U[:, endi:endi + 1], ALU.add, [k_G, k_U], [k_lf])
                        yield
                act(T_li[:, :], T_li[:, :], AF.Exp, [k_li], [k_li])
                yield
                cp("vector", G["WIb"][d][0][:, :], T_li[:, :], [k_li], [G["WIb"][d][2]])
                yield
                act(T_lf[:, :], T_lf[:, :], AF.Exp, [k_lf], [k_lf])
                yield
                act(T_B[:, :], T_m[:, :], AF.Exp, [k_m], [k_B], scale=-1.0)
                yield
                for si, (s0, s1) in enumerate(segs):
                    nch = (s1 - s0) // 128
                    e0 = (s0 + 127) if d == 0 else s0
                    cp("vector", w0r[d * 32:d * 32 + 4, s0 // 128:s0 // 128 + nch], T_li[:, e0:s1:128], [k_li], [("w0r", d)])
                    yield
                    if not is_sample:
                        mi = (s1 - 1) if d == 0 else s0
                        dma("sync", D["m_o"][j, si, d, :].rearrange("(p o) -> p o", o=1), T_m[:, mi:mi + 1], [k_m], [], semkey=("mout", d, si), is_out=True)
                        yield
                for cg in range(nchtot):
                    ps, pk = ps_next(ALLPS)
                    for qq, (X, bX, kX) in enumerate([(T_li, b_li, k_li), (T_B, b_B, k_B), (T_lf, b_lf, k_lf), (Gg, b_G, k_G)]):
                        qi = d * 4 + qq
                        mm(ps[:, qi * 4:(qi + 1) * 4], X[:, cg * 128:(cg + 1) * 128], identF[bX:bX + 4, bX:bX + 4], True, True,
                           [kX, "identF"], [pk])
                    cp("vector", cols[:, cg, d * 16:(d + 1) * 16], ps[:, d * 16:(d + 1) * 16], [pk], [("cols", d)])
                    yield

            gens = [gdir(0), gdir(1)]
            alive = [True, True]
            while any(alive):
                for gi_ in range(2):
                    if alive[gi_]:
                        try:
                            next(gens[gi_])
                        except StopIteration:
                            alive[gi_] = False

        def b_unit(j, hb, t0, TP, segs, is_sample, cond):
            P.phase = "B_proj%d" % int(is_sample)
            arena_reset(keep_base=True)
            nqt = TP // 128
            nblk = TP // 512
            nchtot = nqt
            cols = G["cols"]
            w0r = G["w0r"]
            Hsum = carveR([nqt, 256])
            Hsum32 = Hsum.bitcast(F32)
            xbufs = [carve([TP]) for _ in range(2)]
            accs = [carve([TP]) for _ in range(2)]
            qT = carve([2, TP], BF16)
            kT = carve([2, TP], BF16)
            Vaug = carve([nqt, 258], BF16)
            nseg = len(segs)
            C32s = [[carve([2, 258]) for _ in range(2)] for _ in range(nseg)]
            Cbs = [[[carve([2, 258], BF16) for _ in range(2)] for _ in range(2)] for _ in range(nseg)]
            W0bc = [carve([nchtot]) for _ in range(2)]
            nrot = 4 * nseg
            DT = [carve([128], BF16) for _ in range(nrot)]
            PT = [carve([128], BF16) for _ in range(nrot)]
            kw = [carve([256], BF16) for _ in range(nrot)]
            qw = [carve([2, 128], BF16) for _ in range(nrot)]
            ktok_all = xbufs[0].bitcast(BF16).rearrange("p (a b) -> p a b", a=nqt)
            dd = [carve([2]) for _ in range(2 * nseg)]
            hk = lambda kc, blk: ("h", kc, blk)
            memset("gpsimd", Vaug[:, :, 256:257], 1.0, [("V", i) for i in range(nqt)])
            for si in range(nseg):
                for d in range(2):
                    ck = ("C32", si, d)
                    memset("gpsimd", C32s[si][d][:, :, :], 0.0, [ck])
                    if is_sample:
                        for dc in range(2):
                            dma("sync", C32s[si][d][:, dc, 0:256], D["C0%d" % j][d, hb, dc * 128:(dc + 1) * 128, :], (), [ck], semkey=("C0ld", d))
                            dma("sync", C32s[si][d][:, dc, 256:257],
                                D["n0%d" % j][d, hb, dc * 128:(dc + 1) * 128].rearrange("(p o) -> p o", o=1), (), [ck], semkey=("C0ld", d))
            wqk, wk = w_pop("Bqk")
            def conv_a(c4):
                isk = c4 >= 2
                c = c4 % 2
                ci = (8 if isk else 0) + 2 * hb + c
                xbuf, acc = xbufs[c4 % 2], accs[c4 % 2]
                xk, ak = ("xbuf", c4 % 2), ("acc", c4 % 2)
                for blk in range(nblk):
                    c0, c1 = blk * 512, (blk + 1) * 512
                    ps, pk = ps_next(ALLPS)
                    for kc in range(8):
                        mm(ps[:, :], wqk[:, kc, c4 * 128:(c4 + 1) * 128], hT[:, kc, c0:c1], kc == 0, kc == 7, [wk, hk(kc, blk)], [pk])
                    cp("scalar", xbuf[:, c0:c1], ps[:, :], [pk], [xk])
                for (s0, s1) in segs:
                    ts("gpsimd", acc[:, s0:s1], xbuf[:, s0:s1], convt[:, ci, 1:2], ALU.mult, [xk, "convt"], [ak], s2=0.0, op1=ALU.add)
                    stt(acc[:, s0 + 1:s1], xbuf[:, s0:s1 - 1], convt[:, ci, 0:1], acc[:, s0 + 1:s1], ALU.mult, ALU.add,
                        [xk, "convt", ak], [ak])
                    stt(acc[:, s0:s1 - 1], xbuf[:, s0 + 1:s1], convt[:, ci, 2:3], acc[:, s0:s1 - 1], ALU.mult, ALU.add,
                        [xk, "convt", ak], [ak])

            def conv_b(c4):
                isk = c4 >= 2
                c = c4 % 2
                xbuf, acc = xbufs[c4 % 2], accs[c4 % 2]
                xk, ak = ("xbuf", c4 % 2), ("acc", c4 % 2)
                act(xbuf[:, :], acc[:, :], AF.Tanh, [ak], [xk], scale=0.5)
                sf = (1.0 / 32.0) if isk else 0.5
                ts("gpsimd", xbuf[:, :], xbuf[:, :], sf, ALU.mult, [xk], [xk], s2=sf, op1=ALU.add)
                dst = kT if isk else qT
                tt("vector", dst[:, c, :], xbuf[:, :], acc[:, :], ALU.mult, [xk, ak], [("kT" if isk else "qT", c)])

            conv_a(0)
            for c4 in range(4):
                if c4 + 1 < 4:
                    conv_a(c4 + 1)
                conv_b(c4)
            wv, wvk = w_pop("Bv")
            for qt in range(nqt):
                ps, pk = ps_next(ALLPS)
                for kc in range(8):
                    mm(ps[:, 0:256], hT[:, kc, qt * 128:(qt + 1) * 128], wv[:, kc, 0:256], kc == 0, kc == 7, [hk(kc, qt // 4), wvk], [pk])
                cp("scalar", Vaug[:, qt, 0:256], ps[:, 0:256], [pk], [("V", qt)])
            for qt in range(nqt):
                ps, pk = ps_next(ALLPS)
                for dc in range(2):
                    mm(ps[:, dc * 128:(dc + 1) * 128], kT[:, dc, qt * 128:(qt + 1) * 128], identB[:, :], True, True, [("kT", dc), "identB"], [pk])
                cp("scalar" if qt % 2 else "vector", ktok_all[:, qt, :], ps[:, 0:256], [pk], [("ktok", qt), ("xbuf", 0)])
            for d in range(2):
                ps, pk = ps_next(ALLPS)
                mm(ps[:, 0:nchtot], selF[d * 32:d * 32 + 4, hb * 128:(hb + 1) * 128], w0r[d * 32:d * 32 + 4, 0:nchtot], True, True,
                   ["selF", ("w0r", d)], [pk])
                cp("vector", W0bc[d][:, :], ps[:, 0:nchtot], [pk], [("W0bc", d)])
            P.phase = "B_chunks%d" % int(is_sample)
            hs_sets = [set() for _ in segs]
            for si in range(nseg):
                for d in range(2):
                    cp("scalar", Cbs[si][d][1][:, :, :], C32s[si][d][:, :, :], [("C32", si, d)], [("Cb", si, d, 1)])
            if True:
                def step_info(si, oi):
                    s0, s1 = segs[si]
                    nch = (s1 - s0) // 128
                    orders = [list(range(nch)), list(range(nch - 1, -1, -1))]
                    par = oi % 2
                    info = []
                    for d in range(2):
                        c = orders[d][oi]
                        a0 = s0 + c * 128
                        info.append((d, a0, a0 + 128, a0 // 128, (is_sample and oi == nch - 1), si * 4 + d * 2 + par))
                    return par, info

                def front(si, oi):
                    par, info = step_info(si, oi)
                    banks = []
                    for (d, a0, a1, cg, skip, r) in info:
                        U, b_U, k_U = G["U"][d]
                        WI, b_W, k_W = G["WIb"][d]
                        psx, pkx = ps_next(ALLPS)
                        banks.append((psx, pkx))
                        for dc in range(2):
                            mm(psx[:, 0:128], kT[:, dc, a0:a1], qT[:, dc, a0:a1], dc == 0, dc == 1, [("kT", dc), ("qT", dc)], [pkx])
                        mm(psx[:, 128:256], selF[b_U:b_U + 4, hb * 128:(hb + 1) * 128], U[:, a0:a1], True, False, [k_U, "selF"], [pkx])
                        mm(psx[:, 128:256], identB[:, :], maskFB[:, d * 128:(d + 1) * 128], False, True, ["identB", "maskFB"], [pkx])
                        mm(psx[:, 256:384], identB[b_W:b_W + 4, b_W + hb:b_W + hb + 1].to_broadcast([4, 128]), WI[:, a0:a1], True, True,
                           [k_W, "identB"], [pkx])
                    for (d, a0, a1, cg, skip, r), (psx, pkx) in zip(info, banks):
                        colb = d * 16
                        act(DT[r][:, :], psx[:, 128:256], AF.Exp, [pkx, ("cols", d)], [("DT", r)],
                            bias=cols[:, cg, colb + 12 + hb:colb + 12 + hb + 1])
                        if not skip:
                            ts("gpsimd", kw[r][:, :], ktok_all[:, cg, :], cols[:, cg, colb + 8 + hb:colb + 8 + hb + 1], ALU.mult,
                               [("ktok", cg), ("cols", d)], [("kw", r)], s2=0.0, op1=ALU.add)
                        tt("vector", qw[r][:, :, :], qT[:, :, a0:a1], psx[:, 256:384].unsqueeze(1).to_broadcast([128, 2, 128]), ALU.mult,
                           [("qT", 0), ("qT", 1), pkx], [("qw", r)])
                    for (d, a0, a1, cg, skip, r), (psx, pkx) in zip(info, banks):
                        tt("vector", PT[r][:, :], psx[:, 0:128], DT[r][:, :], ALU.mult, [pkx, ("DT", r)], [("PT", r)])

                def back(si, oi):
                    par, info = step_info(si, oi)
                    C32 = C32s[si]
                    Cb = Cbs[si]
                    hs_written = hs_sets[si]
                    res = []
                    for (d, a0, a1, cg, skip, r) in info:
                        ps3, pk3 = ps_next(ALLPS)
                        mm(ps3[:, 0:257], PT[r][:, :], Vaug[:, cg, 0:257], True, False, [("PT", r), ("V", cg)], [pk3])
                        for dc in range(2):
                            mm(ps3[:, 0:257], qw[r][:, dc, :], Cb[d][1 - par][:, dc, 0:257], False, dc == 1,
                               [("qw", r), ("Cb", si, d, 1 - par)], [pk3])
                        psd, pkd = (None, None)
                        if not skip:
                            for dc in range(2):
                                mm(ps3[:, 384 + dc:385 + dc], kw[r][:, dc * 128:(dc + 1) * 128], Vaug[:, cg, 256:257], True, True,
                                   [("kw", r), ("V", cg)], [pk3])
                            psd, pkd = ps_next(ALLPS)
                            for dc in range(2):
                                mm(psd[:, dc * 256:(dc + 1) * 256], kw[r][:, dc * 128:(dc + 1) * 128], Vaug[:, cg, 0:256], True, True,
                                   [("kw", r), ("V", cg)], [pkd])
                        res.append((ps3, pk3, psd, pkd))
                    for (d, a0, a1, cg, skip, r), (ps3, pk3, psd, pkd) in zip(info, res):
                        if skip:
                            continue
                        ck = ("C32", si, d)
                        stt(C32[d][:, :, 0:256], C32[d][:, :, 0:256], W0bc[d][:, cg:cg + 1],
                            psd[:, :].rearrange("p (a b) -> p a b", a=2), ALU.mult, ALU.add, [ck, ("W0bc", d), pkd], [ck])
                        stt(C32[d][:, :, 256], C32[d][:, :, 256], W0bc[d][:, cg:cg + 1], ps3[:, 384:386], ALU.mult, ALU.add,
                            [ck, ("W0bc", d), pk3], [ck])
                        cp("scalar", Cb[d][par][:, :, :], C32[d][:, :, :], [ck], [("Cb", si, d, par)])
                    for (d, a0, a1, cg, skip, r), (ps3, pk3, psd, pkd) in zip(info, res):
                        colb = d * 16
                        ts("vector", dd[si * 2 + d][:, 0:1], ps3[:, 256:257], cols[:, cg, colb + 4 + hb:colb + 4 + hb + 1], ALU.max,
                           [pk3, ("cols", d)], [("dd", si, d)])
                        stt(dd[si * 2 + d][:, 0:1], ps3[:, 256:257], -1.0, dd[si * 2 + d][:, 0:1], ALU.mult, ALU.max, [pk3, ("dd", si, d)], [("dd", si, d)])
                        P.add("vector", lambda e, o=dd[si * 2 + d][:, 1:2], i=dd[si * 2 + d][:, 0:1]: e.reciprocal(o, i), [("dd", si, d)], [("dd", si, d)])
                        if cg not in hs_written:
                            hs_written.add(cg)
                            act(Hsum[:, cg, :], ps3[:, 0:256], AF.Copy, [pk3, ("dd", si, d)], [("Hs", cg)], scale=dd[si * 2 + d][:, 1:2])
                        else:
                            stt(Hsum[:, cg, :], ps3[:, 0:256], dd[si * 2 + d][:, 1:2], Hsum32[:, cg, :], ALU.mult, ALU.add,
                                [pk3, ("dd", si, d), ("Hs", cg)], [("Hs", cg)])

                def seg_pipe(si):
                    nch_ = (segs[si][1] - segs[si][0]) // 128
                    front(si, 0)
                    yield
                    for oi in range(nch_):
                        if oi + 1 < nch_:
                            front(si, oi + 1)
                            yield
                        back(si, oi)
                        yield
                    if not is_sample:
                        for d in range(2):
                            ck = ("C32", si, d)
                            for dc in range(2):
                                dma("sync", D["C_o"][j, si, d, hb, dc * 128:(dc + 1) * 128, :], C32s[si][d][:, dc, 0:256], [ck], [],
                                    semkey=("Cout", si, d), is_out=True)
                                dma("sync", D["n_o"][j, si, d, hb, dc * 128:(dc + 1) * 128].rearrange("(p o) -> p o", o=1),
                                    C32s[si][d][:, dc, 256:257], [ck], [], semkey=("Cout", si, d), is_out=True)

                gens_ = [seg_pipe(si) for si in range(nseg)]
                alive_ = [True] * nseg
                while any(alive_):
                    for gi_ in range(nseg):
                        if alive_[gi_]:
                            try:
                                next(gens_[gi_])
                            except StopIteration:
                                alive_[gi_] = False
            P.phase = "B_epi%d" % int(is_sample)
            arena_reset(keep_base=True, keepR=True)
            og = carve([nblk, 2, 512])
            zsn = carve([nblk, 2, 512])
            hgT = carve([nblk, 2, 512])
            tq = [carve([512]) for _ in range(2)]
            lnb = carve([nblk, 512])
            rstd = carve([nblk, 512])
            sq = carveR([nblk, 2, 512])
            boT = carveR([2, TP])
            woz, wozk = w_pop("Boz")
            it = 0
            for blk in range(nblk):
                c0, c1 = blk * 512, (blk + 1) * 512
                for c in range(2):
                    ps, pk = ps_next(ALLPS)
                    for kc in range(8):
                        mm(ps[:, :], woz[:, kc, c * 128:(c + 1) * 128], hT[:, kc, c0:c1], kc == 0, kc == 7, [wozk, hk(kc, blk)], [pk])
                    act(og[:, blk, c, :], ps[:, :], AF.Tanh, [pk], [("og", blk, c)], scale=0.5)
                    ts("gpsimd", og[:, blk, c, :], og[:, blk, c, :], 0.5, ALU.mult, [("og", blk, c)], [("og", blk, c)], s2=0.5, op1=ALU.add)
                    ps2, pk2 = ps_next(ALLPS)
                    for kc in range(8):
                        mm(ps2[:, :], woz[:, kc, 256 + c * 128:256 + (c + 1) * 128], hT[:, kc, c0:c1], kc == 0, kc == 7,
                           [wozk, hk(kc, blk)], [pk2])
                    t_ = tq[it % 2]
                    tk = ("tq", it % 2)
                    it += 1
                    act(t_[:, :], ps2[:, :], AF.Tanh, [pk2], [tk], scale=0.5)
                    ts("gpsimd", t_[:, :], t_[:, :], 0.5, ALU.mult, [tk], [tk], s2=0.5, op1=ALU.add)
                    nbc = smallc[:, 8 + 2 * hb + c:8 + 2 * hb + c + 1]
                    stt(zsn[:, blk, c, :], t_[:, :], nbc, ps2[:, :], ALU.mult, ALU.mult, [tk, "normbt", pk2], [("zsn", blk, c)])
            wo, wok = w_pop("Bo")
            for blk in range(nblk):
                for c in range(2):
                    ps, pk = ps_next(ALLPS)
                    for i in range(4):
                        qt = blk * 4 + i
                        mm(ps[:, i * 128:(i + 1) * 128], Hsum[:, qt, c * 128:(c + 1) * 128], identR[:, :], True, True,
                           [("Hs", qt), "identR"], [pk])
                    tt("vector", hgT[:, blk, c, :], ps[:, :], og[:, blk, c, :], ALU.mult, [pk, ("og", blk, c)], [("hgT", blk, c)])
            pss = []
            for blk in range(nblk):
                for c in range(2):
                    act(sq[:, blk, c, :], hgT[:, blk, c, :], AF.Square, [("hgT", blk, c)], [("sq", blk, c)])
                ps2, pk2 = ps_next(ALLPS)
                for c in range(2):
                    mm(ps2[:, :], onesR[:, :], sq[:, blk, c, :], c == 0, c == 1, ["onesR", ("sq", blk, c)], [pk2])
                pss.append((ps2, pk2))
            for blk in range(nblk):
                ps2, pk2 = pss[blk]
                act(lnb[:, blk, :], ps2[:, :], AF.Ln, [pk2, "epsb"], [("lnb", blk)], scale=1.0 / 256.0, bias=epsb[:, 0:1])
            for blk in range(nblk):
                act(rstd[:, blk, :], lnb[:, blk, :], AF.Exp, [("lnb", blk)], [("rstd", blk)], scale=-0.5)
            for blk in range(nblk):
                c0, c1 = blk * 512, (blk + 1) * 512
                for c in range(2):
                    tt("vector", hgT[:, blk, c, :], hgT[:, blk, c, :], rstd[:, blk, :], ALU.mult, [("hgT", blk, c), ("rstd", blk)], [("hgT", blk, c)])
                    tt("vector", boT[:, c, c0:c1], hgT[:, blk, c, :], zsn[:, blk, c, :], ALU.mult, [("hgT", blk, c), ("zsn", blk, c)], [("boT", c, blk)])
            out_proj2(wo, wok, boT, lambda fc, blk: ("boT", fc, blk), t0, TP, cond)


        CS = {}

        def c_pre(j, t0, TP, segs, is_sample):
            P.phase = "C_pre%d" % int(is_sample)
            arena_reset()
            nblk = TP // 512
            NK = TP + (256 if is_sample else 0)
            qnT = carveR([3, TP])
            ckvT = carveR([2, NK])
            KR = carve([NK], BF16)
            CS["qnT"], CS["ckvT"], CS["KR"], CS["NK"] = qnT, ckvT, KR, NK
            ar["base"] = ar["off"]
            ar["baseR"] = ar["offR"]
            sq = [carveR([512]) for _ in range(2)]
            lnb = carve([512])
            rstd = carve([512])
            tmb = [carve([512]) for _ in range(2)]
            stg = [carve([512]) for _ in range(2)]
            ta = [carve([512]) for _ in range(2)]
            tb = [carve([512]) for _ in range(2)]
            hk = lambda kc, blk: ("h", kc, blk)
            it = 0
            for (wname, nch, nfeat, ncol0, dst, dkey, is_kv) in (("Cqa", 3, 384.0, 16, qnT, "qnT", False), ("Ckva", 2, 256.0, 24, ckvT, "ckvT", True)):
                w, wk = w_pop(wname)
                for blk in range(nblk):
                    c0, c1 = blk * 512, (blk + 1) * 512
                    pss = []
                    for c in range(nch):
                        ps, pk = ps_next(ALLPS)
                        for kc in range(8):
                            mm(ps[:, :], w[:, kc, c * 128:(c + 1) * 128], hT[:, kc, c0:c1], kc == 0, kc == 7, [wk, hk(kc, blk)], [pk])
                        pss.append((ps, pk))
                    ps_s, pk_s = ps_next(ALLPS)
                    for c in range(nch):
                        act(sq[c % 2][:, :], pss[c][0][:, :], AF.Square, [pss[c][1]], [("sqc", c % 2)])
                        mm(ps_s[:, :], onesR[:, :], sq[c % 2][:, :], c == 0, c == nch - 1, ["onesR", ("sqc", c % 2)], [pk_s])
                    rstd_from_sumsq(ps_s[:, :], pk_s, nfeat, rstd[:, :], "rstd", 512, lnb[:, :], "lnb")
                    for c in range(nch):
                        t_ = tmb[it % 2]
                        tk = ("tmb", it % 2)
                        it += 1
                        tt("vector", t_[:, :], pss[c][0][:, :], rstd[:, :], ALU.mult, [pss[c][1], "rstd"], [tk])
                        nrm = smallc[:, ncol0 + c:ncol0 + c + 1]
                        act(dst[:, c, c0:c1], t_[:, :], AF.Copy, [tk, "qnormt", "kvnormt"], [(dkey, c, blk)], scale=nrm)
                        if is_kv and not is_sample:
                            sg = stg[c % 2]
                            act(sg[:, :], t_[:, :], AF.Copy, [tk, "kvnormt"], [("stgc", c % 2)], scale=nrm)
                            dma("sync", D["ckvT_o"][j, c * 128:(c + 1) * 128, c0:c1], sg[:, :], [("stgc", c % 2)], [],
                                semkey=("stgc", c % 2), is_out=True)
                if is_kv and is_sample:
                    for c in range(2):
                        dma("gpsimd", ckvT[:, c, TP:TP + 256], D["cckvT%d" % j][c * 128:(c + 1) * 128, :], (), [("ckvT", c, "ctx")])
            wkr, wkrk = w_pop("Ckr")
            for blk in range(nblk):
                cols = (blk * 512, (blk + 1) * 512)
                ps, pk = ps_next(ALLPS)
                for kc in range(8):
                    mm(ps[0:96, :], wkr[:, kc, 0:96], hT[:, kc, cols[0]:cols[1]], kc == 0, kc == 7, [wkrk, hk(kc, blk)], [pk])
                if is_sample:
                    psp, pkp = ps_next(ALLPS)
                    for kc in range(8):
                        mm(psp[0:96, :], wkr[:, kc, 96:192], hT[:, kc, cols[0]:cols[1]], kc == 0, kc == 7, [wkrk, hk(kc, blk)], [pkp])
                    rope_evac(ps, pk, psp, pkp, 64, 96, cols, KR[64:96, cols[0]:cols[1]], ("KR", blk), ta[blk % 2], tb[blk % 2],
                              ("ta", blk % 2), ("tb", blk % 2))
                else:
                    cp("scalar", KR[64:96, cols[0]:cols[1]], ps[64:96, :], [pk], [("KR", blk)])
                    cp("vector", stg[0][64:96, :], ps[64:96, :], [pk], [("stgc", 0)])
                    dma("sync", D["krT_o"][j, :, cols[0]:cols[1]], stg[0][64:96, :], [("stgc", 0)], [], semkey=("stgc", 0), is_out=True)
            if is_sample:
                dma("sync", stg[1][64:96, 0:256], D["ckrT%d" % j], (), [("stgc", 1)])
                cp("vector", KR[64:96, TP:TP + 256], stg[1][64:96, 0:256], [("stgc", 1)], [("KR", "ctx")])

        def c_unit(j, g, t0, TP, segs, is_sample, cond):
            arena_reset(keep_base=True, barrier=(g == 0))
            P.phase = "C_proj%d" % int(is_sample)
            qnT, ckvT, KR, NK = CS["qnT"], CS["ckvT"], CS["KR"], CS["NK"]
            nqt = TP // 128
            nblk = TP // 512
            nkt = NK // 128
            QT = [carve([TP], BF16) for _ in range(4)]
            KTh = [carve([NK], BF16) for _ in range(4)]
            Vaug = carve([nkt, 4, 65], BF16)
            zs = carve([nqt, 256], BF16)
            aoT = carveR([2, TP])
            ta = [carve([512]) for _ in range(2)]
            tb = [carve([512]) for _ in range(2)]
            tz = [carve([256]) for _ in range(2)]
            alloc_attn_rot()
            hk = lambda kc, blk: ("h", kc, blk)
            memset("gpsimd", Vaug[:, :, :, 64:65], 1.0, [("V", i) for i in range(nkt)])
            for hl in range(4):
                memset("gpsimd", QT[hl][64:128, :], 0.0, [("QTr", hl, b) for b in range(nblk)] + [("QTz", hl)])
                memset("gpsimd", KTh[hl][64:128, :], 0.0, [("KThr", hl)])
            qn_keys = lambda blk: [("qnT", c, blk) for c in range(3)]
            wq, wqk = w_pop("Cqb")
            it = 0
            for hl in range(4):
                for blk in range(nblk):
                    cols = (blk * 512, (blk + 1) * 512)
                    ps, pk = ps_next(ALLPS)
                    for kc in range(3):
                        mm(ps[0:96, :], wq[:, kc, hl * 192:hl * 192 + 96], qnT[:, kc, cols[0]:cols[1]], kc == 0, kc == 2,
                           [wqk] + qn_keys(blk), [pk])
                    if is_sample:
                        psp, pkp = ps_next(ALLPS)
                        for kc in range(3):
                            mm(psp[0:96, :], wq[:, kc, hl * 192 + 96:hl * 192 + 192], qnT[:, kc, cols[0]:cols[1]], kc == 0, kc == 2,
                               [wqk] + qn_keys(blk), [pkp])
                        cp("scalar", QT[hl][0:64, cols[0]:cols[1]], ps[0:64, :], [pk], [("QT", hl, blk)])
                        rope_evac(ps, pk, psp, pkp, 64, 96, cols, QT[hl][64:96, cols[0]:cols[1]], ("QTr", hl, blk),
                                  ta[it % 2], tb[it % 2], ("ta", it % 2), ("tb", it % 2))
                        it += 1
                    else:
                        cp("scalar", QT[hl][0:64, cols[0]:cols[1]], ps[0:64, :], [pk], [("QT", hl, blk)])
                        cp("vector", QT[hl][64:96, cols[0]:cols[1]], ps[64:96, :], [pk], [("QTr", hl, blk)])
            wkv, wkvk = w_pop("Ckvb")
            ck_keys = lambda k0, k1: [("ckvT", c, b) for c in range(2) for b in
                                      sorted(set(["ctx" if kk >= TP else kk // 512 for kk in (k0, k1 - 1)]))]
            for hl in range(4):
                k0 = 0
                while k0 < NK:
                    k1 = min(NK, k0 + 512)
                    n_ = k1 - k0
                    ps, pk = ps_next(ALLPS)
                    for kc in range(2):
                        mm(ps[0:64, 0:n_], wkv[:, kc, hl * 128:hl * 128 + 64], ckvT[:, kc, k0:k1], kc == 0, kc == 1,
                           [wkvk] + ck_keys(k0, k1), [pk])
                    cp("vector" if (k0 // 512) % 2 else "scalar", KTh[hl][0:64, k0:k1], ps[0:64, 0:n_], [pk], [("KTh", hl)])
                    k0 = k1
                cp("vector", KTh[hl][64:96, :], KR[64:96, :], [("KR", b) for b in range(nblk)] + ([("KR", "ctx")] if is_sample else []),
                   [("KThr", hl)])
            wv3 = wkv.rearrange("p k (h c) -> p k h c", c=128)
            for kt in range(nkt):
                ps, pk = ps_next(ALLPS)
                for kc in range(2):
                    mm(ps[:, 0:256], ckvT[:, kc, kt * 128:(kt + 1) * 128], wv3[:, kc, :, 64:128], kc == 0, kc == 1,
                       [wkvk] + ck_keys(kt * 128, (kt + 1) * 128), [pk])
                cp("scalar", Vaug[:, kt, :, 0:64], ps[:, 0:256].rearrange("p (h c) -> p h c", c=64), [pk], [("V", kt)])
            wz, wzk = w_pop("Cz")
            for qt in range(nqt):
                ps, pk = ps_next(ALLPS)
                for kc in range(8):
                    mm(ps[:, 0:256], hT[:, kc, qt * 128:(qt + 1) * 128], wz[:, kc, 0:256], kc == 0, kc == 7, [hk(kc, qt // 4), wzk], [pk])
                silu_tok(ps[:, 0:256], pk, zs[:, qt, :], ("zs", qt), tz[qt % 2][:, :], ("tz", qt % 2), 256)
            P.phase = "C_attn%d" % int(is_sample)
            for hl in range(4):
                if is_sample:
                    jobs = [(kt * 128, 0, 8, None, kt) for kt in range(nkt)]
                else:
                    jobs = []
                    for s_ in range(2):
                        for jt in (2 * s_, 2 * s_ + 1):
                            jobs.append((jt * 128, 2 * s_, 2 * s_ + 2, None, jt))
                attn_head(QT[hl], lambda q0, q1, hl=hl: [k_ for b in range(q0 // 4, (q1 - 1) // 4 + 1)
                                                          for k_ in (("QT", hl, b), ("QTr", hl, b), ("QTz", hl))],
                          0, 128, KTh[hl], lambda kc0, hl=hl: [("KTh", hl), ("KThr", hl)],
                          jobs, lambda vid, hl=hl: (Vaug[:, vid, hl, :], ("V", vid)), 96.0 ** -0.5, None,
                          zs, lambda qt: ("zs", qt), hl, nqt, (hl % 2) * 2, spool=((1, 3, 4, 5, 6, 7) if nqt <= 4 else (4, 5, 6, 7)))
            P.phase = "C_out%d" % int(is_sample)
            transpose_to_aoT(zs, lambda qt: ("zs", qt), aoT, lambda c, b: ("aoT", c, b), nqt)
            out_proj("Co", aoT, lambda fc, blk: ("aoT", fc, blk), t0, TP, cond)

        run_layers(BUILD_WHICH[0])
        P.emit()
        LAST_PROG[0] = P
    return nc


LAST_PROG = [None]
BUILD_WHICH = ["ABC"]
FLAGS = {}
NCORES = [8]
DBG_SPECS = {}
DBG_OUT = {}
_CACHE = {}


def kernel(**inputs):
    inp = {k: np.asarray(v) for k, v in inputs.items()}
    sh, per = prep_inputs(inp)
    specs = input_specs()
    key = "full"
    if key not in _CACHE:
        _CACHE[key] = build()
    nc = _CACHE[key]
    in_maps = []
    for i in range(NCORES[0]):
        d = {}
        for name in specs:
            a = per[i][name] if name in per[i] else sh[name]
            a = np.ascontiguousarray(a, dtype=np.float32)
            assert list(a.shape) == specs[name][0], (name, a.shape, specs[name][0])
            d[name] = a
        in_maps.append(d)
    res = run_bass_kernel_spmd(nc, in_maps, core_ids=list(range(NCORES[0])))
    R = res.results
    for name in DBG_SPECS:
        DBG_OUT[name] = [np.asarray(R[i][name]) for i in range(NCORES[0])]
    f32 = np.float32
    y_prompt = np.zeros((16, 256, 1024), f32)
    y_sample = np.zeros((8, 1024, 1024), f32)
    a_k = np.zeros((16, 2, 256, 4, 64), f32)
    a_v = np.zeros((16, 2, 256, 4, 64), f32)
    b_mem = np.zeros((16, 2, 2, 4, 256, 256), f32)
    b_nrm = np.zeros((16, 2, 2, 4, 256), f32)
    b_max = np.zeros((16, 2, 2, 4), f32)
    c_kv = np.zeros((16, 2, 256, 256), f32)
    c_kr = np.zeros((16, 2, 256, 32), f32)
    for i in range(NCORES[0]):
        r = R[i]
        yT = np.asarray(r["yT_o"])
        y_sample[i] = yT[:, 0:1024].T
        for s in range(2):
            b = 2 * i + s
            y_prompt[b] = yT[:, 1024 + 256 * s:1024 + 256 * (s + 1)].T
            for j in range(2):
                a_k[b, j] = np.asarray(r["kaT_o"])[j][:, 256 * s:256 * (s + 1)].T.reshape(256, 4, 64)
                a_v[b, j] = np.asarray(r["va_o"])[j][256 * s:256 * (s + 1), :].reshape(256, 4, 64)
                b_mem[b, j] = np.asarray(r["C_o"])[j, s]
                b_nrm[b, j] = np.asarray(r["n_o"])[j, s]
                b_max[b, j] = np.asarray(r["m_o"])[j, s]
                c_kv[b, j] = np.asarray(r["ckvT_o"])[j][:, 256 * s:256 * (s + 1)].T
                c_kr[b, j] = np.asarray(r["krT_o"])[j][:, 256 * s:256 * (s + 1)].T
    return (y_prompt, y_sample, a_k, a_v, b_mem, b_nrm, b_max, c_kv, c_kr)
```
